# Optimizing a Trainium2 kernel written in Bass

```python
import functools
import jax, jax.numpy as jnp
from jax import lax
import numpy as np

D_MODEL = 1024
BATCH = 16
SEQ = 2048
DEPTH = 1
DEC_BATCH = 128
DEC_SEQ = 1
PAST_LEN = 16384
PAGE_SIZE = 128

HEAD_DIM = 64
A_Q_HEADS = 8
A_KV_HEADS = 2
A_GROUP = A_Q_HEADS // A_KV_HEADS
A_WINDOW = 128
B_PATTERNS = ((128, 1), (512, 4), (2048, 16))
B_N_GROUPS = 3
B_HEADS_PER_GROUP = 4
B_HEADS = B_N_GROUPS * B_HEADS_PER_GROUP
WIN_UNITS = 128
BLOCK = WIN_UNITS
D_FF = 2816
EPS = 1e-6
ATTN_SCALE = HEAD_DIM ** -0.5
N_ALIBI_HEADS = A_Q_HEADS + B_HEADS
A_W = A_Q_HEADS * HEAD_DIM
A_KVW = A_KV_HEADS * HEAD_DIM
B_W = B_HEADS * HEAD_DIM
B_OUT_W = B_HEADS_PER_GROUP * HEAD_DIM
IN_W = A_W + 2 * A_KVW + 3 * B_W + 2 * D_MODEL

kernel_name = "gated_parallel_swa_sink_dilated_macaron_step"


def rmsnorm(x, g):
    xf = x.astype(jnp.float32)
    y = xf * lax.rsqrt(jnp.mean(xf * xf, axis=-1, keepdims=True) + EPS)
    return (y * g.astype(jnp.float32)).astype(x.dtype)


def swiglu(x, w_gate, w_up, w_down):
    return (jax.nn.silu(x @ w_gate) * (x @ w_up)) @ w_down


def alibi_slopes():
    i = jnp.arange(1, N_ALIBI_HEADS + 1, dtype=jnp.float32)
    return jnp.exp2(-8.0 * i / N_ALIBI_HEADS)


def a_slopes(slopes):
    return slopes[:A_Q_HEADS].reshape(A_KV_HEADS, A_GROUP)


def b_slopes(slopes, g):
    lo = A_Q_HEADS + g * B_HEADS_PER_GROUP
    return slopes[lo:lo + B_HEADS_PER_GROUP].reshape(B_HEADS_PER_GROUP, 1)


def pad_seq(t, length):
    return jnp.pad(t, [(0, 0), (0, length - t.shape[1])] + [(0, 0)] * (t.ndim - 2))


def fold(t, d):
    n, sp = t.shape[:2]
    t = jnp.moveaxis(t.reshape(n, sp // d, d, *t.shape[2:]), 2, 1)
    return t.reshape(n * d, sp // d, *t.shape[3:])


def unfold(t, d, n):
    u = t.shape[1]
    t = jnp.moveaxis(t.reshape(n, d, u, *t.shape[2:]), 1, 2)
    return t.reshape(n, u * d, *t.shape[3:])


def attend(s, mask, sink):
    s = jnp.where(mask, s, -jnp.inf)
    m = jnp.max(s, axis=-1)
    if sink is not None:
        sink = sink.astype(jnp.float32)
        m = jnp.maximum(m, sink)
    p = jnp.exp(s - m[..., None])
    denom = jnp.sum(p, axis=-1)
    if sink is not None:
        denom = denom + jnp.exp(sink - m)
    return p / denom[..., None], m + jnp.log(denom)


def banded_window_attention(q, k, v, slopes, dil, sink):
    n, length, kvh, grp, hd = q.shape
    nb = length // BLOCK
    qb = q.reshape(n, nb, BLOCK, kvh, grp, hd)
    kb = k.reshape(n, nb, BLOCK, kvh, hd)
    vb = v.reshape(n, nb, BLOCK, kvh, hd)
    shift = lambda t: jnp.concatenate([jnp.zeros_like(t[:, :1]), t[:, :-1]], axis=1)
    kk = jnp.concatenate([shift(kb), kb], axis=2)
    vv = jnp.concatenate([shift(vb), vb], axis=2)
    s = jnp.einsum('nbqkgd,nbskd->nbkgqs', qb, kk, preferred_element_type=jnp.float32) * ATTN_SCALE
    qi = jnp.arange(BLOCK)[:, None] + BLOCK
    ki = jnp.arange(2 * BLOCK)[None, :]
    dist = qi - ki
    band = (dist >= 0) & (dist <= WIN_UNITS)
    key_ok = (jnp.arange(nb)[:, None] * BLOCK - BLOCK + ki) >= 0
    mask = (band[None] & key_ok[:, None, :])[None, :, None, None]
    s = s - slopes.astype(jnp.float32)[:, :, None, None] * (dil * dist).astype(jnp.float32)
    p, lse = attend(s, mask, None if sink is None else sink[:, :, None])
    o = jnp.einsum('nbkgqs,nbskd->nbqkgd', p.astype(v.dtype), vv)
    lse = jnp.transpose(lse, (0, 1, 4, 2, 3)).reshape(n, length, kvh, grp)
    return o.reshape(n, length, kvh, grp, hd), lse


def gathered_window_attention(q, k_all, v_all, q_off, dil, slopes, sink):
    t = q.shape[1]
    mstep = jnp.arange(WIN_UNITS + 1)
    idx = (q_off + jnp.arange(t))[:, None] - dil * mstep[None, :]
    valid = idx >= 0
    idx = jnp.maximum(idx, 0)
    kg = jnp.take(k_all, idx, axis=1)
    vg = jnp.take(v_all, idx, axis=1)
    s = jnp.einsum('ntkgd,ntmkd->ntkgm', q, kg, preferred_element_type=jnp.float32) * ATTN_SCALE
    s = s - slopes.astype(jnp.float32)[:, :, None] * (dil * mstep).astype(jnp.float32)
    p, lse = attend(s, valid[None, :, None, None, :], sink)
    o = jnp.einsum('ntkgm,ntmkd->ntkgd', p.astype(v_all.dtype), vg)
    return o, lse


def project_heads(h, w_in, q_norm_a, k_norm_a, q_norm_b, k_norm_b):
    n, t, _ = h.shape
    z = h @ w_in
    cuts = [A_W, A_W + A_KVW, A_W + 2 * A_KVW, A_W + 2 * A_KVW + B_W,
            A_W + 2 * A_KVW + 2 * B_W, A_W + 2 * A_KVW + 3 * B_W,
            A_W + 2 * A_KVW + 3 * B_W + D_MODEL]
    qa, ka, va, qb, kb, vb, ga, gb = jnp.split(z, cuts, axis=-1)
    qa = rmsnorm(qa.reshape(n, t, A_KV_HEADS, A_GROUP, HEAD_DIM), q_norm_a)
    ka = rmsnorm(ka.reshape(n, t, A_KV_HEADS, HEAD_DIM), k_norm_a)
    va = va.reshape(n, t, A_KV_HEADS, HEAD_DIM)
    qb = rmsnorm(qb.reshape(n, t, B_N_GROUPS, B_HEADS_PER_GROUP, HEAD_DIM), q_norm_b)
    kb = rmsnorm(kb.reshape(n, t, B_N_GROUPS, B_HEADS_PER_GROUP, HEAD_DIM), k_norm_b)
    vb = vb.reshape(n, t, B_N_GROUPS, B_HEADS_PER_GROUP, HEAD_DIM)
    return qa, ka, va, qb, kb, vb, ga, gb


def merge(o_a, o_b, lse_b, ga, gb, w_up_a, w_up_b, w_o):
    n, t = o_a.shape[:2]
    wts = jax.nn.softmax(lse_b, axis=2)
    ob = jnp.sum(wts[..., None] * o_b.astype(jnp.float32), axis=2).astype(o_b.dtype)
    ua = o_a.reshape(n, t, A_W) @ w_up_a
    ub = ob.reshape(n, t, B_OUT_W) @ w_up_b
    return (jax.nn.sigmoid(ga) * ua + jax.nn.sigmoid(gb) * ub) @ w_o


def mixer_prompt(h, proj, sinks, outp, slopes):
    qa, ka, va, qb, kb, vb, ga, gb = project_heads(h, *proj)
    n, s_len = h.shape[:2]
    sa = -(-s_len // BLOCK) * BLOCK
    o_a, _ = banded_window_attention(pad_seq(qa, sa), pad_seq(ka, sa), pad_seq(va, sa),
                                     a_slopes(slopes), 1, sinks)
    o_a = o_a[:, :s_len]
    states = [jnp.stack([ka, va], axis=2)[:, s_len - min(A_WINDOW, s_len):]]
    o_b, lse_b = [], []
    for g, (win, dil) in enumerate(B_PATTERNS):
        sp = -(-s_len // (dil * BLOCK)) * dil * BLOCK
        q = fold(pad_seq(qb[:, :, g, :, None], sp), dil)
        k = fold(pad_seq(kb[:, :, g], sp), dil)
        v = fold(pad_seq(vb[:, :, g], sp), dil)
        o, lse = banded_window_attention(q, k, v, b_slopes(slopes, g), dil, None)
        o_b.append(unfold(o, dil, n)[:, :s_len, :, 0])
        lse_b.append(unfold(lse, dil, n)[:, :s_len, :, 0])
        states.append(jnp.stack([kb[:, :, g], vb[:, :, g]], axis=2)[:, s_len - min(win, s_len):])
    y = merge(o_a, jnp.stack(o_b, axis=2), jnp.stack(lse_b, axis=2), ga, gb, *outp)
    return y, states


def mixer_step(h, bufs, proj, sinks, outp, slopes):
    qa, ka, va, qb, kb, vb, ga, gb = project_heads(h, *proj)
    t = h.shape[1]

    def extend(buf, k, v, win):
        rows = jnp.concatenate([buf, jnp.stack([k, v], axis=2)], axis=1)
        return rows[:, :, 0], rows[:, :, 1], rows[:, -min(win, buf.shape[1] + t):]

    k_all, v_all, new_a = extend(bufs[0], ka, va, A_WINDOW)
    o_a, _ = gathered_window_attention(qa, k_all, v_all, bufs[0].shape[1], 1, a_slopes(slopes), sinks)
    states = [new_a]
    o_b, lse_b = [], []
    for g, (win, dil) in enumerate(B_PATTERNS):
        buf = bufs[1 + g]
        k_all, v_all, new_b = extend(buf, kb[:, :, g], vb[:, :, g], win)
        o, lse = gathered_window_attention(qb[:, :, g, :, None], k_all, v_all, buf.shape[1], dil,
                                           b_slopes(slopes, g), None)
        o_b.append(o[:, :, :, 0])
        lse_b.append(lse[..., 0])
        states.append(new_b)
    y = merge(o_a, jnp.stack(o_b, axis=2), jnp.stack(lse_b, axis=2), ga, gb, *outp)
    return y, states


def macaron_layer(x, mixer, norm_ffn1, w1_gate, w1_up, w1_down, norm_mix,
                  norm_ffn2, w2_gate, w2_up, w2_down):
    x = x + 0.5 * swiglu(rmsnorm(x, norm_ffn1), w1_gate, w1_up, w1_down)
    mixed, states = mixer(rmsnorm(x, norm_mix))
    x = x + mixed
    x = x + 0.5 * swiglu(rmsnorm(x, norm_ffn2), w2_gate, w2_up, w2_down)
    return x, states


def setup_inputs(seed: int = 0) -> dict:
    key = jax.random.key(seed)
    ks = iter(jax.random.split(key, 32))
    f32 = jnp.float32

    def nrm(shape, scale=1.0):
        return scale * jax.random.normal(next(ks), shape, f32)

    def gain(shape):
        return 1.0 + 0.02 * jax.random.normal(next(ks), shape, f32)

    def kv_buf(window, heads):
        return nrm((DEPTH, DEC_BATCH, min(window, PAST_LEN), 2, heads, HEAD_DIM))

    return {
        "x_prompt": nrm((BATCH, SEQ, D_MODEL)),
        "x_sample": nrm((DEC_BATCH, DEC_SEQ, D_MODEL)),
        "cache_a_kv": kv_buf(A_WINDOW, A_KV_HEADS),
        "cache_b1_kv": kv_buf(B_PATTERNS[0][0], B_HEADS_PER_GROUP),
        "cache_b2_kv": kv_buf(B_PATTERNS[1][0], B_HEADS_PER_GROUP),
        "cache_b3_kv": kv_buf(B_PATTERNS[2][0], B_HEADS_PER_GROUP),
        "norm_ffn1": gain((DEPTH, D_MODEL)),
        "w1_gate": nrm((DEPTH, D_MODEL, D_FF), D_MODEL ** -0.5),
        "w1_up": nrm((DEPTH, D_MODEL, D_FF), D_MODEL ** -0.5),
        "w1_down": nrm((DEPTH, D_FF, D_MODEL), D_FF ** -0.5),
        "norm_mix": gain((DEPTH, D_MODEL)),
        "w_in": nrm((DEPTH, D_MODEL, IN_W), D_MODEL ** -0.5),
        "q_norm_a": gain((DEPTH, HEAD_DIM)),
        "k_norm_a": gain((DEPTH, HEAD_DIM)),
        "q_norm_b": gain((DEPTH, HEAD_DIM)),
        "k_norm_b": gain((DEPTH, HEAD_DIM)),
        "sinks_a": nrm((DEPTH, A_KV_HEADS, A_GROUP), 0.5),
        "w_up_a": nrm((DEPTH, A_W, D_MODEL), A_W ** -0.5),
        "w_up_b": nrm((DEPTH, B_OUT_W, D_MODEL), B_OUT_W ** -0.5),
        "w_o": nrm((DEPTH, D_MODEL, D_MODEL), D_MODEL ** -0.5),
        "norm_ffn2": gain((DEPTH, D_MODEL)),
        "w2_gate": nrm((DEPTH, D_MODEL, D_FF), D_MODEL ** -0.5),
        "w2_up": nrm((DEPTH, D_MODEL, D_FF), D_MODEL ** -0.5),
        "w2_down": nrm((DEPTH, D_FF, D_MODEL), D_FF ** -0.5),
    }


def reference(x_prompt, x_sample, cache_a_kv, cache_b1_kv, cache_b2_kv, cache_b3_kv,
              norm_ffn1, w1_gate, w1_up, w1_down, norm_mix, w_in,
              q_norm_a, k_norm_a, q_norm_b, k_norm_b, sinks_a,
              w_up_a, w_up_b, w_o, norm_ffn2, w2_gate, w2_up, w2_down):
    slopes = alibi_slopes()
    yp, ys = x_prompt, x_sample
    st_p = ([], [], [], [])
    st_s = ([], [], [], [])
    for l in range(DEPTH):
        proj = (w_in[l], q_norm_a[l], k_norm_a[l], q_norm_b[l], k_norm_b[l])
        outp = (w_up_a[l], w_up_b[l], w_o[l])
        ffn = (norm_ffn1[l], w1_gate[l], w1_up[l], w1_down[l], norm_mix[l],
               norm_ffn2[l], w2_gate[l], w2_up[l], w2_down[l])
        bufs = (cache_a_kv[l], cache_b1_kv[l], cache_b2_kv[l], cache_b3_kv[l])
        yp, sp = macaron_layer(yp, functools.partial(mixer_prompt, proj=proj, sinks=sinks_a[l],
                                                     outp=outp, slopes=slopes), *ffn)
        ys, ss = macaron_layer(ys, functools.partial(mixer_step, bufs=bufs, proj=proj, sinks=sinks_a[l],
                                                     outp=outp, slopes=slopes), *ffn)
        for i in range(4):
            st_p[i].append(sp[i])
            st_s[i].append(ss[i])
    a_p, b1_p, b2_p, b3_p = [jnp.stack(s) for s in st_p]
    a_s, b1_s, b2_s, b3_s = [jnp.stack(s) for s in st_s]
    return (yp, ys, a_p, b1_p, b2_p, b3_p, a_s, b1_s, b2_s, b3_s)
```

```python
import numpy as np
from contextlib import ExitStack
import concourse.bass as bass
import concourse.mybir as mybir
from concourse.bass_utils import run_bass_kernel_spmd

F32 = mybir.dt.float32
BF16 = mybir.dt.bfloat16
AF = mybir.ActivationFunctionType
ALU = mybir.AluOpType
AX = mybir.AxisListType

NCORES = 8
D = 1024
DFF = 2816
NF = 22
SEQ = 2048
NSEQ = 2
HD = 64
EPS = 1e-6
NS = 16
NSTG = 5
N_AL = 20
SLOPES = [2.0 ** (-8.0 * i / N_AL) for i in range(1, N_AL + 1)]
DILS = [1, 4, 16]
WINB = [1, 4, 16]


class Sched:
    def __init__(self, nc, es):
        self.nc = nc
        self.es = es
        self.eng = {"pe": nc.tensor, "act": nc.scalar, "dve": nc.vector, "pool": nc.gpsimd, "sp": nc.sync}
        self.sem = {e: es.enter_context(nc.semaphore("s_" + e)) for e in ["pe", "act", "dve", "pool"]}
        self.cnt = {e: 0 for e in self.sem}
        self.pending = {e: [] for e in self.sem}
        self.lastw = {}
        self.readers = {}
        self.seen = {e: {} for e in self.eng}
        self.dsem = {}
        self.nwaits = 0
        self.nops = {e: 0 for e in self.eng}

    def _need(self, reads, writes):
        toks = []
        for r in reads:
            t = self.lastw.get(r)
            if t is not None:
                toks.append(t)
        for w in writes:
            t = self.lastw.get(w)
            if t is not None:
                toks.append(t)
            toks.extend(self.readers.get(w, []))
        return toks

    def _emit_waits(self, e, toks):
        best = {}
        for (own, sem, val) in toks:
            if own == e and (val is None or e == "pe"):
                continue
            if val is None:
                raise RuntimeError("dependency on unsignalled op")
            k = sem.name
            if self.seen[e].get(k, 0) >= val:
                continue
            if k not in best or best[k][1] < val:
                best[k] = (sem, val)
        for k, (sem, val) in best.items():
            self.eng[e].wait_ge(sem, val)
            self.seen[e][k] = val
            self.nwaits += 1

    def _record(self, tok, reads, writes):
        for r in reads:
            lst = self.readers.setdefault(r, [])
            lst[:] = [t for t in lst if t[1].name != tok[1].name]
            lst.append(tok)
        for w in writes:
            self.lastw[w] = tok
            self.readers[w] = []

    def op(self, e, fn, reads=(), writes=(), signal=True):
        toks = self._need(reads, writes)
        self._emit_waits(e, toks)
        inst = fn()
        self.nops[e] += 1
        if signal:
            self.cnt[e] += 1
            inst.then_inc(self.sem[e], 1)
            tok = (e, self.sem[e], self.cnt[e])
            for (rs, ws) in self.pending[e]:
                self._record(tok, rs, ws)
            self.pending[e] = []
            self._record(tok, reads, writes)
        else:
            self.pending[e].append((tuple(reads), tuple(writes)))
            bad = (e, self.sem[e], None)
            for w in writes:
                self.lastw[w] = bad
                self.readers[w] = []
            for r in reads:
                self.readers.setdefault(r, []).append(bad)
        return inst

    def dma(self, q, slot, out, in_, reads=(), writes=(), disjoint=False, **kw):
        if slot not in self.dsem:
            self.dsem[slot] = [self.es.enter_context(self.nc.semaphore("d_" + slot)), 0]
        ds = self.dsem[slot]
        toks = self._need(reads, writes)
        if disjoint:
            toks = [t for t in toks if t[1].name != ds[0].name]
        self._emit_waits(q, toks)
        ds[1] += 16
        self.eng[q].dma_start(out=out, in_=in_, **kw).then_inc(ds[0], 16)
        self.nops[q] += 1
        tok = ("dma", ds[0], ds[1])
        self._record(tok, reads, writes)
        return tok

    def barrier(self):
        for e in ["pe", "act", "dve", "pool", "sp"]:
            for f in ["pe", "act", "dve", "pool"]:
                if f != e and self.cnt[f] > self.seen[e].get(self.sem[f].name, 0):
                    self.eng[e].wait_ge(self.sem[f], self.cnt[f])
                    self.seen[e][self.sem[f].name] = self.cnt[f]
            for slot, (sem, val) in self.dsem.items():
                if val > self.seen[e].get(sem.name, 0):
                    self.eng[e].wait_ge(sem, val)
                    self.seen[e][sem.name] = val

    def finish(self):
        for e in ["sp"]:
            for slot, (sem, val) in self.dsem.items():
                if val > 0:
                    self.eng[e].wait_ge(sem, val)
            for f in ["pe", "act", "dve", "pool"]:
                if self.cnt[f] > 0:
                    self.eng[e].wait_ge(self.sem[f], self.cnt[f])


def host_tables():
    ident = np.eye(128, dtype=np.float32)
    k = np.arange(128)[:, None].astype(np.float64)
    q = np.arange(128)[None, :].astype(np.float64)
    tabs = []
    idx = {}
    for h in range(8):
        s = SLOPES[h]
        idx[("A", h, 1)] = len(tabs); tabs.append(np.where(q <= k, np.exp(-s * (128 + q - k)), 0.0))
        idx[("A", h, 0)] = len(tabs); tabs.append(np.where(q >= k, np.exp(-s * (q - k)), 0.0))
    for g in range(3):
        dil = DILS[g]
        mod = ((q - k) % dil) == 0
        for h in range(4):
            s = SLOPES[8 + g * 4 + h]
            if g == 0:
                idx[(g, h, 1)] = len(tabs); tabs.append(np.where(q <= k, np.exp(-s * (128 + q - k)), 0.0))
                idx[(g, h, 0)] = len(tabs); tabs.append(np.where(q >= k, np.exp(-s * (q - k)), 0.0))
            elif g == 1:
                idx[(g, h, 4)] = len(tabs); tabs.append(np.where((q <= k) & mod, np.exp(-s * (q - k)), 0.0))
                for dl in (3, 2, 1):
                    idx[(g, h, dl)] = len(tabs); tabs.append(np.where(mod, np.exp(-s * (q - k)), 0.0))
                idx[(g, h, 0)] = len(tabs); tabs.append(np.where((q >= k) & mod, np.exp(-s * (q - k)), 0.0))
            else:
                idx[(g, h, "m")] = len(tabs); tabs.append(np.where(mod, np.exp(-s * (q - k)), 0.0))
                idx[(g, h, "0")] = len(tabs); tabs.append(np.where((q >= k) & mod, np.exp(-s * (q - k)), 0.0))
    dt = np.stack(tabs, axis=1).astype(np.float32)
    ab = np.ones((128, 16, 3, 4, 2), np.float64)
    for g in (1, 2):
        for h in range(4):
            s = SLOPES[8 + g * 4 + h]
            for b in range(16):
                ab[:, b, g, h, 0] = np.exp(-s * 128.0 * b)
                ab[:, b, g, h, 1] = np.exp(s * 128.0 * b)
    return ident, dt, idx, ab.astype(np.float32)


def build_program(with_sample=True):
    ident_np, dt_np, TIDX, ab_np = host_tables()
    NTAB = dt_np.shape[1]
    nc = bass.Bass("TRN2", target_bir_lowering=False)

    def din(name, shape, dt=F32):
        return nc.dram_tensor(name, list(shape), dt, kind="ExternalInput").ap()

    def dout(name, shape):
        return nc.dram_tensor(name, list(shape), F32, kind="ExternalOutput").ap()

    def dscr(name, shape, dt=BF16):
        return nc.dram_tensor(name, list(shape), dt, kind="Internal").ap()

    x_p = din("x_prompt", [NSEQ, SEQ, D])
    x_s = din("x_sample", [NS, D])
    if with_sample:
        c_a = din("cache_a", [NS, 128, 2, 2, HD])
        c_b = [din("cache_b1", [NS, 128, 2, 4, HD]), din("cache_b2", [NS, 512, 2, 4, HD]),
               din("cache_b3", [NS, 2048, 2, 4, HD])]
    g_f1 = din("norm_ffn1", [D]); g_mx = din("norm_mix", [D]); g_f2 = din("norm_ffn2", [D])
    w_g = [din("w1_gate", [D, DFF]), din("w2_gate", [D, DFF])]
    w_u = [din("w1_up", [D, DFF]), din("w2_up", [D, DFF])]
    w_d = [din("w1_down", [DFF, D]), din("w2_down", [DFF, D])]
    w_in = din("w_in", [D, 5120])
    qn_a = din("q_norm_a", [HD]); kn_a = din("k_norm_a", [HD]); qn_b = din("q_norm_b", [HD]); kn_b = din("k_norm_b", [HD])
    sinks = din("sinks_a", [8])
    w_upa = din("w_up_a", [512, D]); w_upb = din("w_up_b", [256, D]); w_o = din("w_o", [D, D])
    c_ident = din("c_ident", [128, 128]); c_dt = din("c_dt", [128, NTAB, 128]); c_ab = din("c_ab", [128, 16 * 3 * 4 * 2])
    c_sal = din("c_sal", [128, 4, 129])
    y_p = dout("y_prompt", [NSEQ, SEQ, D]); y_s = dout("y_sample", [NS, D])
    o_ap = dout("a_p", [NSEQ, 128, 2, 2, HD])
    o_bp = [dout("b1_p", [NSEQ, 128, 2, 4, HD]), dout("b2_p", [NSEQ, 512, 2, 4, HD]), dout("b3_p", [NSEQ, 2048, 2, 4, HD])]
    if with_sample:
        o_as = dout("a_s", [NS, 128, 2, 2, HD])
        o_bs = [dout("b1_s", [NS, 128, 2, 4, HD]), dout("b2_s", [NS, 512, 2, 4, HD]), dout("b3_s", [NS, 2048, 2, 4, HD])]
    s_gu = [dscr("s_gu1", [11, 128, 2, 2, 8, 128]), dscr("s_gu2", [11, 128, 2, 2, 8, 128])]
    s_d = [dscr("s_d1", [NF, 128, D]), dscr("s_d2", [NF, 128, D])]
    s_in = dscr("s_in", [6, 128, 8, 512])
    s_m = dscr("s_m", [8, 128, 22 * 128])
    s_o = dscr("s_o", [2, 128, 8, 512])

    es = ExitStack()
    with es:
        S = Sched(nc, es)

        def sb(name, shape, dt, stack=es):
            return stack.enter_context(nc.sbuf_tensor(name, list(shape), dt))

        psT = [es.enter_context(nc.psum_tensor("psT%d" % i, [128, 1024], BF16)) for i in range(2)]
        psB = [es.enter_context(nc.psum_tensor("psB%d" % i, [128, 512], F32)) for i in range(6)]
        PB = ["P%d" % i for i in range(6)]
        TB = ["T0", "T1"]

        idb = sb("idb", [128, 128], BF16)
        dtab = sb("dtab", [128, NTAB, 128], BF16)
        abt = sb("abt", [128, 16, 3, 4, 2], F32)
        gq_a = sb("gq_a", [128, HD], F32); gk_a = sb("gk_a", [128, HD], F32)
        gq_b = sb("gq_b", [128, HD], F32); gk_b = sb("gk_b", [128, HD], F32)
        esink = sb("esink", [128, 8], F32)
        mhalf = sb("mhalf", [128, 8], F32)
        small = sb("small", [128, 96], F32)

        S.op("pool", lambda: nc.gpsimd.memset(mhalf[:], -0.5), writes=["mhalf"])
        gcol = sb("gcol", [128, 4], F32)
        for ci, v in enumerate([qn_a, kn_a, qn_b, kn_b]):
            for hf in range(2):
                S.dma("sp", "gcol", gcol[hf * 64:(hf + 1) * 64, ci:ci + 1], v.rearrange("(d o) -> d o", o=1), writes=["gcol"],
                      allow_slow_non_contiguous=True)
        S.op("act", lambda: nc.scalar.mul(out=gcol[:, 0:1], in_=gcol[:, 0:1], mul=HD ** -0.5), reads=["gcol"], writes=["gcol"])
        S.op("act", lambda: nc.scalar.mul(out=gcol[:, 2:3], in_=gcol[:, 2:3], mul=HD ** -0.5), reads=["gcol"], writes=["gcol"])

        def bcast_load(dst, src, n, name):
            S.dma("sp", name, dst, src.rearrange("(o n) -> o n", o=1).broadcast_to([128, n]), writes=[name])

        bcast_load(gq_a[:], qn_a, HD, "gq_a"); bcast_load(gk_a[:], kn_a, HD, "gk_a")
        bcast_load(gq_b[:], qn_b, HD, "gq_b"); bcast_load(gk_b[:], kn_b, HD, "gk_b")
        bcast_load(esink[:], sinks, 8, "esink")
        S.dma("sp", "abt", abt[:].rearrange("p a b c d -> p (a b c d)"), c_ab, writes=["abt"])
        S.op("act", lambda: nc.scalar.mul(out=gq_a[:], in_=gq_a[:], mul=HD ** -0.5), reads=["gq_a"], writes=["gq_a"])
        S.op("act", lambda: nc.scalar.mul(out=gq_b[:], in_=gq_b[:], mul=HD ** -0.5), reads=["gq_b"], writes=["gq_b"])
        S.op("act", lambda: nc.scalar.activation(out=esink[:], in_=esink[:], func=AF.Exp), reads=["esink"], writes=["esink"])

        with ExitStack() as ps:
            stg_in = [sb("stg_in%d" % i, [128, 8, 1024], F32, ps) for i in range(2)]
            stg_out = [sb("stg_out%d" % i, [128, 8, 1024], BF16, ps) for i in range(2)]
            gains = sb("gains", [128, 3, 8], F32, ps)
            dt32 = sb("dt32", [128, NTAB, 128], F32, ps)
            id32 = sb("id32", [128, 128], F32, ps)
            for i, g in enumerate([g_f1, g_mx, g_f2]):
                S.dma("sp", "gains", gains[:, i, :], g.rearrange("(c p) -> p c", p=128), writes=["gains"],
                      allow_slow_non_contiguous=True)
            S.dma("sp", "id32", id32[:], c_ident, writes=["id32"])
            S.dma("sp", "dt32", dt32[:], c_dt, writes=["dt32"])
            S.op("dve", lambda: nc.vector.tensor_copy(out=idb[:], in_=id32[:]), reads=["id32"], writes=["idb"])
            S.op("dve", lambda: nc.vector.tensor_copy(out=dtab[:], in_=dt32[:]), reads=["dt32"], writes=["dtab"])

            pcount = [0]
            def block(src_ap, nk, n, gain_idx, perm_f, stores):
                i = pcount[0] % 2
                pcount[0] += 1
                tin = stg_in[i][:, 0:nk, 0:n]
                S.dma("sp", "stg_in%d" % i, tin, src_ap, writes=["stg_in%d" % i])
                flat = stg_out[i][:].rearrange("p k c -> p (k c)")[:, 0:nk * n]
                if perm_f:
                    f = n // 128
                    ov_all = flat.rearrange("p (f k c) -> p k f c", f=f, k=nk)
                    iv_all = tin.rearrange("p k (f c) -> p k f c", c=128)
                else:
                    ov_all = flat.rearrange("p (k c) -> p k c", k=nk)
                    iv_all = tin
                if gain_idx is not None:
                    for kc in range(nk):
                        S.op("act", lambda kc=kc: nc.scalar.activation(out=ov_all[:, kc], in_=iv_all[:, kc], func=AF.Copy,
                                                                       scale=gains[:, gain_idx, kc:kc + 1]),
                             reads=["stg_in%d" % i, "gains"], writes=["stg_out%d" % i], signal=(kc == nk - 1))
                else:
                    e = "dve" if pcount[0] % 2 else "pool"
                    if e == "dve":
                        S.op("dve", lambda: nc.vector.tensor_copy(out=ov_all, in_=iv_all), reads=["stg_in%d" % i], writes=["stg_out%d" % i])
                    else:
                        S.op("pool", lambda: nc.gpsimd.tensor_copy(out=ov_all, in_=iv_all), reads=["stg_in%d" % i], writes=["stg_out%d" % i])
                for (dst, src, res) in stores(flat):
                    S.dma("pool", "stg_out%d" % i, dst, src, reads=["stg_out%d" % i], writes=[res], disjoint=True)

            def wsrc(w, c0, ncol, nk):
                return w.rearrange("(c p) n -> p c n", p=128)[:, 0:nk, c0:c0 + ncol]

            for l in range(2):
                gi = 0 if l == 0 else 2
                for c0, n in ((0, 1024), (1024, 1024), (2048, 768)):
                    for gu, w in enumerate([w_g[l], w_u[l]]):
                        def st(flat, c0=c0, n=n, gu=gu, l=l):
                            out = []
                            for q_ in range(n // 256):
                                fp = c0 // 256 + q_
                                out.append((s_gu[l][fp, :, :, gu, :, :].rearrange("p f k c -> p f (k c)"),
                                            flat[:, q_ * 2048:(q_ + 1) * 2048].rearrange("p (f x) -> p f x", f=2),
                                            "s_gu%d_%d_%d" % (l, fp, gu)))
                            return out
                        block(wsrc(w, c0, n, 8), 8, n, gi, True, st)
                for f0, nf in ((0, 8), (8, 8), (16, 6)):
                    def st(flat, f0=f0, nf=nf, l=l):
                        return [(s_d[l][f0:f0 + nf].rearrange("f p c -> p f c"), flat.rearrange("p (f c) -> p f c", f=nf),
                                 "s_d%d_%d" % (l, f0))]
                    block(w_d[l].rearrange("(f p) c -> p f c", p=128)[:, f0:f0 + nf, :], nf, 1024, None, False, st)
            for bk in range(3):
                def st(flat, bk=bk):
                    v = flat.rearrange("p (k c) -> p k c", k=8)
                    return [(s_in[2 * bk + hf], v[:, :, hf * 512:(hf + 1) * 512], "s_in%d" % (2 * bk + hf)) for hf in range(2)]
                block(wsrc(w_in, bk * 1024, 1024, 8), 8, 1024, 1, False, st)
            for which, c0, off in [("ga", 3072, 6 * 128), ("gb", 4096, 14 * 128)]:
                def st(flat, off=off, which=which):
                    return [(s_m[:, :, off:off + 1024].rearrange("m p c -> p m c"), flat.rearrange("p (m x) -> p m x", m=8), "s_m_" + which)]
                block(wsrc(w_in, c0, 1024, 8), 8, 1024, 1, True, st)
            def st(flat):
                return [(s_m[:, :, 0:512].rearrange("m p c -> p m c"), flat.rearrange("p (m x) -> p m x", m=8), "s_m_upa")]
            block(wsrc(w_upa, 0, 1024, 4), 4, 1024, None, True, st)
            def st(flat):
                return [(s_m[:, :, 512:768].rearrange("m p c -> p m c"), flat.rearrange("p (m x) -> p m x", m=8), "s_m_upb")]
            block(wsrc(w_upb, 0, 1024, 2), 2, 1024, None, True, st)
            def st(flat):
                v = flat.rearrange("p (k c) -> p k c", k=8)
                return [(s_o[ch], v[:, :, ch * 512:(ch + 1) * 512], "s_o%d" % ch) for ch in range(2)]
            block(wsrc(w_o, 0, 1024, 8), 8, 1024, None, False, st)
            S.barrier()

        ms = ExitStack()
        es.enter_context(ms)
        xt = sb("xt", [128, 5, D], F32, ms)
        xmap = [0, 1, 2, 3]
        hb = sb("hb", [128, D], BF16, ms)
        sq = sb("sq", [128, 256], F32, ms)
        hT = sb("hT", [128, 8, 512], BF16, ms)
        big = sb("big", [128, 11, 512], BF16, ms)
        wd = sb("wd", [128, 11, D], BF16, ms)
        ring = [sb("ring%d" % i, [128, 4096], BF16, ms) for i in range(4)]
        sg = [sb("sg%d" % i, [128, 512], F32, ms) for i in range(2)]
        hs = ExitStack()
        stg = [sb("stg%d" % i, [128, 256], F32, ms) for i in range(NSTG)]
        qn = sb("qn", [128, 256], F32, ms)
        qb16 = sb("qb16", [128, 256], BF16, ms)
        pT = [sb("pT%d" % i, [128, 512], BF16, ms) for i in range(2)]
        oacc = sb("oacc", [128, 12, 65], F32, ms)
        otmp = sb("otmp", [128, 4, 65], F32, ms)
        onrm = sb("onrm", [128, 768], BF16, ms)
        oT = sb("oT", [128, 6, 512], BF16, ms)
        rden = sb("rden", [128, 12], F32, ms)
        hb2 = [hb, sb("hb1", [128, D], BF16, ms)]
        sqb = [sb("sqb%d" % i, [128, 512], F32, ms) for i in range(3)]
        qbb = [sb("qbb%d" % i, [128, 512], BF16, ms) for i in range(3)]
        sg.append(sb("sg2", [128, 512], F32, ms))
        pT.append(sb("pT2", [128, 512], BF16, ms))
        KTA = sb("KTA", [128, 1, SEQ], BF16, hs)
        VA = sb("VA", [128, 16, 2, 65], BF16, hs)
        KTB = sb("KTB", [128, 6, SEQ], BF16, hs)
        VB = sb("VB", [128, 16, 3, 4, 65], BF16, hs)
        zs_box = [None]

        S.op("dve", lambda: nc.vector.memset(VA[:].rearrange("p a b c -> p (a b c)"), 1.0), writes=["VA"])
        S.op("dve", lambda: nc.vector.memset(VB[:].rearrange("p a b c d -> p (a b c d)"), 1.0), writes=["VB"])
        for g_ in (1, 2):
            S.op("dve", lambda g_=g_: nc.vector.tensor_copy(out=VB[:, :, g_, :, HD:HD + 1], in_=abt[:, :, g_, :, 1:2]),
                 reads=["abt"], writes=["VB"])

        ring_i = [0]
        st_i = [0]
        bank_i = [0]
        tb_i = [0]
        sg_i = [0]
        pt_i = [0]

        def next_ring():
            i = ring_i[0] % 4
            ring_i[0] += 1
            return i

        def next_bank(lo=0, hi=4):
            i = lo + bank_i[0] % (hi - lo)
            bank_i[0] += 1
            return i

        def next_tb():
            i = tb_i[0] % 2
            tb_i[0] += 1
            return i

        ew_i = [0]

        def ew():
            ew_i[0] += 1
            return "dve" if ew_i[0] % 2 else "act"

        def copy_op(e, out, in_, reads, writes):
            if e == "act":
                S.op("act", lambda: nc.scalar.copy(out=out, in_=in_), reads=reads, writes=writes)
            elif e == "dve":
                S.op("dve", lambda: nc.vector.tensor_copy(out=out, in_=in_), reads=reads, writes=writes)
            else:
                S.op("pool", lambda: nc.gpsimd.tensor_copy(out=out, in_=in_), reads=reads, writes=writes)

        def norm_T(P, nsub, NT):
            for s in range(nsub):
                c = 32 + 3 * s
                S.op("act", lambda: nc.scalar.activation(out=hb2[s % 2][0:P, :], in_=xt[0:P, xmap[s], :], func=AF.Square,
                                                         accum_out=small[0:P, c:c + 1]),
                     reads=["xt%d" % xmap[s]], writes=["hb%d" % (s % 2), "nst%d" % s])
                S.op("pool", lambda: nc.gpsimd.tensor_scalar(out=small[0:P, c + 1:c + 2], in0=small[0:P, c:c + 1], scalar1=1.0 / D,
                                                             scalar2=EPS, op0=ALU.mult, op1=ALU.add),
                     reads=["nst%d" % s], writes=["nst%d" % s])
                S.op("pool", lambda: nc.gpsimd.tensor_tensor(out=small[0:P, c + 2:c + 3], in0=small[0:P, c + 1:c + 2],
                                                             in1=mhalf[0:P, 0:1], op=ALU.pow),
                     reads=["nst%d" % s, "mhalf"], writes=["nst%d" % s])
                if s >= 1:
                    norm_tail(P, s - 1)
            norm_tail(P, nsub - 1)

        def norm_tail(P, s):
            c = 32 + 3 * s
            hbuf = hb2[s % 2]
            S.op("act", lambda: nc.scalar.activation(out=hbuf[0:P, :], in_=xt[0:P, xmap[s], :], func=AF.Copy,
                                                     scale=small[0:P, c + 2:c + 3]),
                 reads=["xt%d" % xmap[s], "nst%d" % s], writes=["hb%d" % (s % 2)])
            t = next_tb()
            for kc in range(8):
                S.op("pe", lambda kc=kc: nc.tensor.transpose(out=psT[t][:, kc * 128:kc * 128 + P],
                                                             in_=hbuf[0:P, kc * 128:(kc + 1) * 128],
                                                             identity=idb[0:P, 0:P]),
                     reads=["hb%d" % (s % 2), "idb"], writes=[TB[t]], signal=(kc == 7))
            copy_op(ew(), hT[:, :, s * 128:s * 128 + P],
                    psT[t][:].rearrange("p (k c) -> p k c", k=8)[:, :, 0:P], [TB[t]], [TB[t], "hT"])

        def ffn(l, P, nsub, NT, on_sub_done=None):
            plan_l = []
            loads_idx = {}
            for half_ in range(2):
                for j_ in range(11):
                    f_ = half_ * 11 + j_
                    if f_ % 2 == 0 or j_ == 0:
                        loads_idx[(half_, f_ // 2)] = len(plan_l)
                        plan_l.append(f_ // 2)
            load_slot = {}
            emitted = [0]

            def ensure(k):
                while emitted[0] <= min(k, len(plan_l) - 1):
                    i_ = emitted[0]
                    r_ = next_ring()
                    S.dma("sp", "ring%d" % r_, ring[r_][:], s_gu[l][plan_l[i_]].rearrange("p f g k c -> p (f g k c)"),
                          reads=["s_gu%d_%d_%d" % (l, plan_l[i_], a_) for a_ in range(2)], writes=["ring%d" % r_])
                    load_slot[i_] = r_
                    emitted[0] += 1

            if LOOK > 0:
                ensure(1)
            norm_T(P, nsub, NT)
            for half in range(2):
                for j in range(11):
                    f = half * 11 + j
                    S.dma("sp", "wd%d" % j, wd[:, j, :], s_d[l][f], reads=["s_d%d_%d" % (l, (f // 8) * 8)], writes=["wd%d" % j])
                for j in range(11):
                    f = half * 11 + j
                    fp, fi = f // 2, f % 2
                    if fi == 0 or j == 0:
                        li = loads_idx[(half, fp)]
                        ensure(li + LOOK)
                        r = load_slot[li]
                        rv = ring[r][:].rearrange("p (f g k c) -> p f g k c", f=2, g=2, k=8)
                    bg, bu = next_bank(0, 2), 2 + next_bank(0, 2)
                    for gu, b in [(0, bg), (1, bu)]:
                        for kc in range(8):
                            S.op("pe", lambda gu=gu, b=b, kc=kc: nc.tensor.matmul(
                                psB[b][:, 0:NT], lhsT=rv[:, fi, gu, kc, :], rhs=hT[:, kc, 0:NT],
                                start=(kc == 0), stop=(kc == 7)),
                                reads=["ring%d" % r, "hT"], writes=[PB[b]], signal=(kc == 7))
                    si = sg_i[0] % 2
                    sg_i[0] += 1
                    S.op("act", lambda: nc.scalar.activation(out=sg[si][:, 0:NT], in_=psB[bg][:, 0:NT], func=AF.Silu),
                         reads=[PB[bg]], writes=[PB[bg], "sg%d" % si])
                    S.op("dve", lambda: nc.vector.tensor_tensor(out=big[:, j, 0:NT], in0=sg[si][:, 0:NT],
                                                                in1=psB[bu][:, 0:NT], op=ALU.mult),
                         reads=["sg%d" % si, PB[bu]], writes=[PB[bu], "big%d" % j])
                for s in range(nsub):
                    for ch in range(2):
                        b = 4 + ch
                        for j in range(11):
                            S.op("pe", lambda j=j, b=b, ch=ch: nc.tensor.matmul(
                                psB[b][0:P, :], lhsT=big[:, j, s * 128:s * 128 + P], rhs=wd[:, j, ch * 512:(ch + 1) * 512],
                                start=(j == 0), stop=(j == 10)),
                                reads=["big%d" % j, "wd%d" % j], writes=[PB[b]], signal=(j == 10))
                        S.op("dve", lambda b=b, ch=ch: nc.vector.scalar_tensor_tensor(
                            out=xt[0:P, xmap[s], ch * 512:(ch + 1) * 512], in0=psB[b][0:P, :], scalar=0.5,
                            in1=xt[0:P, xmap[s], ch * 512:(ch + 1) * 512], op0=ALU.mult, op1=ALU.add),
                            reads=[PB[b], "xt%d" % xmap[s]], writes=[PB[b], "xt%d" % xmap[s]])
                        if half == 1 and ch == 1 and on_sub_done is not None:
                            on_sub_done(s)

        def qk_norm(P, bank, c0, nh, gain_tab):
            W = nh * HD
            S.op("act", lambda: nc.scalar.activation(out=sq[0:P, 0:W], in_=psB[bank][0:P, c0:c0 + W], func=AF.Square),
                 reads=[PB[bank]], writes=[PB[bank], "sq"])
            S.op("dve", lambda: nc.vector.tensor_reduce(out=small[0:P, 8:8 + nh],
                                                        in_=sq[0:P, 0:W].rearrange("p (h d) -> p h d", h=nh),
                                                        axis=AX.X, op=ALU.add), reads=["sq"], writes=["small"])
            S.op("pool", lambda: nc.gpsimd.tensor_scalar(out=small[0:P, 16:16 + nh], in0=small[0:P, 8:8 + nh],
                                                         scalar1=1.0 / HD, scalar2=EPS, op0=ALU.mult, op1=ALU.add),
                 reads=["small"], writes=["small"])
            S.op("pool", lambda: nc.gpsimd.tensor_tensor(out=small[0:P, 24:24 + nh], in0=small[0:P, 16:16 + nh],
                                                         in1=mhalf[0:P, 0:nh], op=ALU.pow),
                 reads=["small", "mhalf"], writes=["small"])
            S.op("dve", lambda: nc.vector.tensor_tensor(
                out=qn[0:P, 0:W].rearrange("p (h d) -> p h d", h=nh),
                in0=psB[bank][0:P, c0:c0 + W].rearrange("p (h d) -> p h d", h=nh),
                in1=small[0:P, 24:24 + nh].unsqueeze(2).broadcast_to([P, nh, HD]), op=ALU.mult),
                reads=[PB[bank], "small"], writes=[PB[bank], "qn"])
            S.op("pool", lambda: nc.gpsimd.tensor_tensor(
                out=qn[0:P, 0:W].rearrange("p (h d) -> p h d", h=nh),
                in0=qn[0:P, 0:W].rearrange("p (h d) -> p h d", h=nh),
                in1=gain_tab[0:P, :].unsqueeze(1).broadcast_to([P, nh, HD]), op=ALU.mult),
                reads=["qn"], writes=["qn"])

        def transpose_to(P, src_tok, ncols, dsts):
            t = next_tb()
            n = ncols // 128
            for c in range(n):
                S.op("pe", lambda c=c: nc.tensor.transpose(out=psT[t][:, c * 128:c * 128 + P],
                                                           in_=src_tok[0:P, c * 128:(c + 1) * 128], identity=idb[0:P, 0:P]),
                     reads=["qb16"], writes=[TB[t]], signal=(c == n - 1))
            for c, (dst, res) in enumerate(dsts):
                copy_op(ew(), dst, psT[t][:, c * 128:c * 128 + P], [TB[t]], [TB[t], res])

        def kv_out(seq, blk, grp, kv, st_idx, nh):
            if seq is None:
                return
            if grp == "A":
                if blk == 15:
                    S.dma("sp", "stq%d" % st_idx, o_ap[seq, :, kv, :, :].rearrange("t h d -> t (h d)"),
                          stg[st_idx][:, 0:nh * HD], reads=["stg%d" % st_idx])
                return
            g = grp
            nb = WINB[g]
            if blk >= 16 - nb:
                t0 = (blk - (16 - nb)) * 128
                S.dma("sp", "stq%d" % st_idx, o_bp[g][seq, t0:t0 + 128, kv, :, :].rearrange("t h d -> t (h d)"),
                      stg[st_idx][:, 0:nh * HD], reads=["stg%d" % st_idx])

        def next_stg():
            i = st_i[0] % NSTG
            st_i[0] += 1
            return i

        def project(P, nsub, NT, seq, blk0):
            for nt in range(6):
                r = next_ring()
                S.dma("sp", "ring%d" % r, ring[r][:], s_in[nt].rearrange("p k c -> p (k c)"),
                      reads=["s_in%d" % nt], writes=["ring%d" % r])
                rv = ring[r][:].rearrange("p (k c) -> p k c", k=8)
                for s in range(nsub):
                    blk = blk0 + s
                    tok = slice(s * 128, s * 128 + P)
                    b = next_bank(0, 4)
                    for kc in range(8):
                        S.op("pe", lambda kc=kc: nc.tensor.matmul(psB[b][0:P, :], lhsT=hT[:, kc, tok], rhs=rv[:, kc, :],
                                                                  start=(kc == 0), stop=(kc == 7)),
                             reads=["hT", "ring%d" % r], writes=[PB[b]], signal=(kc == 7))
                    for uu in range(2):
                        u = nt * 2 + uu
                        c0 = uu * 256
                        if seq is None:
                            zs = zs_box[0]
                            if u in (0, 1, 3, 4, 5, 6, 7, 8):
                                qk_norm(P, b, c0, 4, gq_a if u < 2 else (gq_b if u < 6 else gk_b))
                                S.op("act", lambda: nc.scalar.copy(out=zs[0:P, u * 256:(u + 1) * 256], in_=qn[0:P, :]),
                                     reads=["qn"], writes=["zs"])
                            elif u == 2:
                                qk_norm(P, b, c0, 2, gk_a)
                                S.op("act", lambda: nc.scalar.copy(out=zs[0:P, 512:640], in_=qn[0:P, 0:128]),
                                     reads=["qn"], writes=["zs"])
                                S.op("dve", lambda: nc.vector.tensor_copy(out=zs[0:P, 640:768], in_=psB[b][0:P, c0 + 128:c0 + 256]),
                                     reads=[PB[b]], writes=[PB[b], "zs"])
                            else:
                                S.op("dve", lambda: nc.vector.tensor_copy(out=zs[0:P, u * 256:(u + 1) * 256], in_=psB[b][0:P, c0:c0 + 256]),
                                     reads=[PB[b]], writes=[PB[b], "zs"])
                            continue
                        if u in (0, 1):
                            qk_norm(P, b, c0, 4, gq_a)
                            S.op("act", lambda: nc.scalar.copy(out=qb16[0:P, :], in_=qn[0:P, :]), reads=["qn"], writes=["qb16"])
                            transpose_to(P, qb16, 256, [(big[:, 2 * u + c, tok], "big%d" % (2 * u + c)) for c in range(2)])
                        elif u == 2:
                            qk_norm(P, b, c0, 2, gk_a)
                            si = next_stg()
                            S.op("act", lambda: nc.scalar.copy(out=stg[si][0:P, 0:128], in_=qn[0:P, 0:128]),
                                 reads=["qn"], writes=["stg%d" % si])
                            kv_out(seq, blk, "A", 0, si, 2)
                            S.op("dve", lambda: nc.vector.tensor_copy(
                                out=qb16[0:P, :].rearrange("p (h r d) -> p h r d", h=2, r=2),
                                in_=qn[0:P, 0:128].rearrange("p (h d) -> p h d", h=2).unsqueeze(2).broadcast_to([P, 2, 2, HD])),
                                reads=["qn"], writes=["qb16"])
                            if seq is not None:
                                transpose_to(P, qb16, 256, [(KTA[:, c, blk * 128:blk * 128 + P], "KTA") for c in range(2)])
                            else:
                                transpose_to(P, qb16, 256, [(KTA[:, c, 0:P], "KTA") for c in range(2)])
                            si = next_stg()
                            S.op("act", lambda: nc.scalar.copy(out=stg[si][0:P, 0:128], in_=psB[b][0:P, c0 + 128:c0 + 256]),
                                 reads=[PB[b]], writes=[PB[b], "stg%d" % si])
                            kv_out(seq, blk, "A", 1, si, 2)
                            S.op("dve", lambda: nc.vector.tensor_copy(
                                out=VA[0:P, blk if seq is not None else 0, :, 0:HD],
                                in_=stg[si][0:P, 0:128].rearrange("p (h d) -> p h d", h=2)),
                                reads=["stg%d" % si], writes=["VA"])
                        elif u in (3, 4, 5):
                            g = u - 3
                            qk_norm(P, b, c0, 4, gq_b)
                            S.op("act", lambda: nc.scalar.copy(out=qb16[0:P, :], in_=qn[0:P, :]), reads=["qn"], writes=["qb16"])
                            transpose_to(P, qb16, 256, [(big[:, 4 + 2 * g + c, tok], "big%d" % (4 + 2 * g + c)) for c in range(2)])
                        elif u in (6, 7, 8):
                            g = u - 6
                            qk_norm(P, b, c0, 4, gk_b)
                            si = next_stg()
                            S.op("act", lambda: nc.scalar.copy(out=stg[si][0:P, :], in_=qn[0:P, :]), reads=["qn"], writes=["stg%d" % si])
                            kv_out(seq, blk, g, 0, si, 4)
                            S.op("dve", lambda: nc.vector.tensor_copy(out=qb16[0:P, :], in_=qn[0:P, :]), reads=["qn"], writes=["qb16"])
                            kcol = slice(blk * 128, blk * 128 + P) if seq is not None else slice(0, P)
                            transpose_to(P, qb16, 256, [(KTB[:, 2 * g + c, kcol], "KTB") for c in range(2)])
                        else:
                            g = u - 9
                            si = next_stg()
                            S.op("act", lambda: nc.scalar.copy(out=stg[si][0:P, :], in_=psB[b][0:P, c0:c0 + 256]),
                                 reads=[PB[b]], writes=[PB[b], "stg%d" % si])
                            kv_out(seq, blk, g, 1, si, 4)
                            vb = blk if seq is not None else 0
                            if g == 0 or seq is None:
                                S.op("dve", lambda: nc.vector.tensor_copy(
                                    out=VB[0:P, vb, g, :, 0:HD], in_=stg[si][0:P, :].rearrange("p (h d) -> p h d", h=4)),
                                    reads=["stg%d" % si], writes=["VB"])
                            else:
                                S.op("dve", lambda: nc.vector.tensor_tensor(
                                    out=VB[0:P, vb, g, :, 0:HD], in0=stg[si][0:P, :].rearrange("p (h d) -> p h d", h=4),
                                    in1=abt[0:P, vb, g, :, 1:2].broadcast_to([P, 4, HD]), op=ALU.mult),
                                    reads=["stg%d" % si, "abt"], writes=["VB"])
                                S.op("pool", lambda: nc.gpsimd.tensor_copy(out=VB[0:P, vb, g, :, HD:HD + 1],
                                                                           in_=abt[0:P, vb, g, :, 1:2]),
                                     reads=["abt"], writes=["VB"])

        def project_prompt(seq, blk0):
            P = 128
            tails = []
            gidx = [0]

            def emit_group(nt, s, r, rv):
                blk = blk0 + s
                tok = slice(s * 128, (s + 1) * 128)
                kcol = slice(blk * 128, (blk + 1) * 128)
                b = next_bank(0, 4)
                par = gidx[0] % 3
                gidx[0] += 1
                for kc in range(8):
                    S.op("pe", lambda kc=kc: nc.tensor.matmul(psB[b][:, :], lhsT=hT[:, kc, tok], rhs=rv[:, kc, :],
                                                              start=(kc == 0), stop=(kc == 7)),
                         reads=["hT", "ring%d" % r], writes=[PB[b]], signal=(kc == 7))
                ps3 = psB[b][:, :].rearrange("p (h d) -> p h d", h=8)
                st = "pst%d" % par
                c = 64 + par * 8
                if nt < 5:
                    S.op("act", lambda: nc.scalar.activation(out=sqb[par][:, :], in_=psB[b][:, :], func=AF.Square),
                         reads=[PB[b]], writes=[PB[b], "sqb%d" % par])
                    S.op("dve", lambda: nc.vector.tensor_reduce(out=small[:, c:c + 8],
                                                                in_=sqb[par][:, :].rearrange("p (h d) -> p h d", h=8),
                                                                axis=AX.X, op=ALU.add), reads=["sqb%d" % par], writes=[st])
                    S.op("pool", lambda: nc.gpsimd.tensor_scalar(out=small[:, c:c + 8], in0=small[:, c:c + 8],
                                                                 scalar1=1.0 / HD, scalar2=EPS, op0=ALU.mult, op1=ALU.add),
                         reads=[st], writes=[st])
                    S.op("pool", lambda: nc.gpsimd.tensor_tensor(out=small[:, c:c + 8], in0=small[:, c:c + 8],
                                                                 in1=mhalf[:, 0:8], op=ALU.pow), reads=[st, "mhalf"], writes=[st])
                plan = []

                def stageB():
                  if nt < 5:
                    rs = small[:, c:c + 8].unsqueeze(2).broadcast_to([128, 8, HD])
                    if nt == 0:
                        ov = qbb[par][:, :].rearrange("p (c hf d) -> p hf c d", c=4, hf=2)
                        iv = psB[b][:, :].rearrange("p (hf c d) -> p hf c d", hf=2, c=4)
                        rs4 = small[:, c:c + 8].rearrange("p (hf c) -> p hf c", hf=2).unsqueeze(3).broadcast_to([128, 2, 4, HD])
                        S.op("dve", lambda: nc.vector.tensor_tensor(out=ov, in0=iv, in1=rs4, op=ALU.mult),
                             reads=[PB[b], st], writes=[PB[b], "qbb%d" % par])
                    else:
                        S.op("dve", lambda: nc.vector.tensor_tensor(out=qbb[par][:, :].rearrange("p (h d) -> p h d", h=8),
                                                                    in0=ps3, in1=rs, op=ALU.mult),
                             reads=[PB[b], st], writes=[PB[b], "qbb%d" % par])

                  def k_out(c0, nh, gtab, grp):
                      need = (blk == 15) if grp == "A" else (blk >= 16 - WINB[grp])
                      if not need:
                          return
                      si = next_stg()
                      h0 = c0 // HD
                      S.op("dve", lambda: nc.vector.tensor_tensor(
                          out=stg[si][:, 0:nh * HD].rearrange("p (h d) -> p h d", h=nh), in0=ps3[:, h0:h0 + nh, :],
                          in1=small[:, c + h0:c + h0 + nh].unsqueeze(2).broadcast_to([128, nh, HD]), op=ALU.mult),
                          reads=[PB[b], st], writes=[PB[b], "stg%d" % si])
                      S.op("pool", lambda: nc.gpsimd.tensor_tensor(
                          out=stg[si][:, 0:nh * HD].rearrange("p (h d) -> p h d", h=nh),
                          in0=stg[si][:, 0:nh * HD].rearrange("p (h d) -> p h d", h=nh),
                          in1=gtab[:, :].unsqueeze(1).broadcast_to([128, nh, HD]), op=ALU.mult),
                          reads=["stg%d" % si], writes=["stg%d" % si])
                      kv_out(seq, blk, grp, 0, si, nh)

                  def v_part(c0, nh, grp):
                      need = (blk == 15) if grp == "A" else (blk >= 16 - WINB[grp])
                      src = psB[b][:, c0:c0 + nh * HD]
                      if need:
                          si = next_stg()
                          S.op("act", lambda: nc.scalar.copy(out=stg[si][:, 0:nh * HD], in_=src),
                               reads=[PB[b]], writes=[PB[b], "stg%d" % si])
                          kv_out(seq, blk, grp, 1, si, nh)
                      s3 = src.rearrange("p (h d) -> p h d", h=nh)
                      if grp == "A":
                          copy_op(ew(), VA[:, blk, :, 0:HD], s3, [PB[b]], [PB[b], "VA"])
                      elif grp == 0:
                          copy_op(ew(), VB[:, blk, 0, :, 0:HD], s3, [PB[b]], [PB[b], "VB"])
                      else:
                          S.op("dve", lambda: nc.vector.tensor_tensor(
                              out=VB[:, blk, grp, :, 0:HD], in0=s3, in1=abt[:, blk, grp, :, 1:2].broadcast_to([128, 4, HD]),
                              op=ALU.mult), reads=[PB[b], "abt"], writes=[PB[b], "VB"])

                  if nt == 0:
                      plan.append(([0, 1, 2, 3], big[:, 0:4, tok], ["big0", "big1", "big2", "big3"], 0))
                  elif nt == 1:
                      k_out(0, 2, gk_a, "A")
                      v_part(128, 2, "A")
                      plan.append(([0], KTA[:, 0:1, kcol], ["KTA"], 1))
                      plan.append(([2, 3], big[:, 4:6, tok], ["big4", "big5"], 2))
                  elif nt == 2:
                      plan.append(([0, 1, 2, 3], big[:, 6:10, tok], ["big6", "big7", "big8", "big9"], 2))
                  elif nt == 3:
                      k_out(0, 4, gk_b, 0)
                      k_out(256, 4, gk_b, 1)
                      plan.append(([0, 1, 2, 3], KTB[:, 0:4, kcol], ["KTB"], 3))
                  elif nt == 4:
                      k_out(0, 4, gk_b, 2)
                      v_part(256, 4, 0)
                      plan.append(([0, 1], KTB[:, 4:6, kcol], ["KTB"], 3))
                  else:
                      v_part(0, 4, 1)
                      v_part(256, 4, 2)

                def tail():
                    if not plan:
                        return
                    t = next_tb()
                    allc = [ci for (cs, _, _, _) in plan for ci in cs]
                    for i, ci in enumerate(allc):
                        S.op("pe", lambda ci=ci: nc.tensor.transpose(out=psT[t][:, ci * 128:(ci + 1) * 128],
                                                                     in_=qbb[par][:, ci * 128:(ci + 1) * 128], identity=idb[:, :]),
                             reads=["qbb%d" % par, "idb"], writes=[TB[t]], signal=(i == len(allc) - 1))
                    for (cs, dst, dres, gi) in plan:
                        n = len(cs)
                        src = psT[t][:, cs[0] * 128:(cs[0] + n) * 128].rearrange("p (n c) -> p n c", n=n)
                        if ew() == "act":
                            S.op("act", lambda: nc.scalar.activation(out=dst, in_=src, func=AF.Copy, scale=gcol[:, gi:gi + 1]),
                                 reads=[TB[t], "gcol"], writes=[TB[t]] + dres)
                        else:
                            S.op("dve", lambda: nc.vector.tensor_scalar(out=dst, in0=src, scalar1=gcol[:, gi:gi + 1], scalar2=None,
                                                                        op0=ALU.mult), reads=[TB[t], "gcol"], writes=[TB[t]] + dres)
                return stageB, tail

            ptails = []
            pendB = [None]
            for nt in range(6):
                r = next_ring()
                S.dma("sp", "ring%d" % r, ring[r][:], s_in[nt].rearrange("p k c -> p (k c)"),
                      reads=["s_in%d" % nt], writes=["ring%d" % r])
                rv = ring[r][:].rearrange("p (k c) -> p k c", k=8)
                for s in range(4):
                    sB, tl_ = emit_group(nt, s, r, rv)
                    if pendB[0] is not None:
                        pendB[0]()
                    pendB[0] = sB
                    ptails.append(tl_)
                    if len(ptails) > PDEPTH:
                        ptails.pop(0)()
            pendB[0]()
            while ptails:
                ptails.pop(0)()

        def attention_block(s, blk):
            qcol = slice(s * 128, s * 128 + 128)
            jobs = []
            for h in range(8):
                kvh, par, ch = h // 4, h // 4, h % 4
                pairs = [(kb, TIDX[("A", h, blk - kb)]) for kb in (blk - 1, blk) if kb >= 0]
                jobs.append((h, "A", par, lambda kb: KTA[:, 0, kb * 128:(kb + 1) * 128], big[:, ch, qcol], pairs,
                             lambda kb, kvh=kvh: VA[:, kb, kvh, :], "big%d" % ch))
            for g in range(3):
                for hh in (0, 2, 1, 3):
                    par = hh % 2
                    ch = 2 * g + hh // 2
                    pairs = []
                    for kb in range(max(0, blk - WINB[g]), blk + 1):
                        dl = blk - kb
                        if g <= 1:
                            ti = TIDX[(g, hh, dl)]
                        else:
                            ti = TIDX[(g, hh, "0" if dl == 0 else "m")]
                        pairs.append((kb, ti))
                    jobs.append((8 + hh, g, par, lambda kb, ch=ch: KTB[:, ch, kb * 128:(kb + 1) * 128], big[:, 4 + ch, qcol], pairs,
                                 lambda kb, g=g, hh=hh: VB[:, kb, g, hh, :], "big%d" % (4 + ch)))
            items = []
            for job in jobs:
                npair = len(job[5])
                for pi_ in range(npair):
                    items.append((job, pi_))
            import os as _os
            if _os.environ.get("KATT", "1") == "1":
                batches = []
                cur = []
                for it in items:
                    if cur and (len(cur) == 4 or cur[-1][0][2] != it[0][2]):
                        batches.append(cur)
                        cur = []
                    cur.append(it)
                if cur:
                    batches.append(cur)
            else:
                batches = []
                for job in jobs:
                    its = [(job, pi_) for pi_ in range(len(job[5]))]
                    batches.extend([its[i:i + 4] for i in range(0, len(its), 4)])

            def front(batch):
                n = len(batch)
                b = next_bank(0, 4)
                reads = set()
                for i, (job, pi_) in enumerate(batch):
                    (slot, grp, par, ktf, qap, pairs, vf, qres) = job
                    kb, ti = pairs[pi_]
                    pr = slice(par * 64, par * 64 + 64)
                    S.op("pe", lambda i=i, kb=kb, ktf=ktf, qap=qap, pr=pr: nc.tensor.matmul(
                        psB[b][:, i * 128:(i + 1) * 128], lhsT=ktf(kb)[pr, :], rhs=qap[pr, :], start=True, stop=True),
                        reads=["KTA" if grp == "A" else "KTB", qres], writes=[PB[b]], signal=(i == n - 1))
                si = sg_i[0] % 3
                sg_i[0] += 1
                S.op("act", lambda: nc.scalar.activation(out=sg[si][:, 0:n * 128], in_=psB[b][:, 0:n * 128], func=AF.Exp),
                     reads=[PB[b]], writes=[PB[b], "sg%d" % si])
                pi = pt_i[0] % 3
                pt_i[0] += 1
                e = "dve" if pt_i[0] % 2 else "pool"
                tis = [job[5][pi_][1] for (job, pi_) in batch]
                runs = []
                i = 0
                while i < n:
                    j = i
                    if j + 1 < n and tis[j + 1] == tis[i]:
                        while j + 1 < n and tis[j + 1] == tis[i]:
                            j += 1
                        in1 = dtab[:, tis[i]:tis[i] + 1, :].broadcast_to([128, j + 1 - i, 128])
                    else:
                        while j + 1 < n and tis[j + 1] == tis[j] + 1:
                            j += 1
                        in1 = dtab[:, tis[i]:tis[j] + 1, :]
                    runs.append((i, j + 1, in1))
                    i = j + 1
                for (a, z, in1) in runs:
                    ov = pT[pi][:, a * 128:z * 128].rearrange("p (n c) -> p n c", n=z - a)
                    iv = sg[si][:, a * 128:z * 128].rearrange("p (n c) -> p n c", n=z - a)
                    if e == "dve":
                        S.op("dve", lambda ov=ov, iv=iv, in1=in1: nc.vector.tensor_tensor(out=ov, in0=iv, in1=in1, op=ALU.mult),
                             reads=["sg%d" % si, "dtab"], writes=["pT%d" % pi])
                    else:
                        S.op("pool", lambda ov=ov, iv=iv, in1=in1: nc.gpsimd.tensor_tensor(out=ov, in0=iv, in1=in1, op=ALU.mult),
                             reads=["sg%d" % si, "dtab"], writes=["pT%d" % pi])
                return pi

            def back(batch, pi):
                n = len(batch)
                for i, (job, pi_) in enumerate(batch):
                    (slot, grp, par, ktf, qap, pairs, vf, qres) = job
                    kb, ti = pairs[pi_]
                    npair = len(pairs)
                    vres = "VA" if grp == "A" else "VB"
                    if grp == "A":
                        accb, acol = 4 + slot // 4, (slot % 4) * 65
                    else:
                        accb, acol = 4 + (grp % 2), (slot - 8) * 65
                    last = (pi_ == npair - 1)
                    S.op("pe", lambda i=i, kb=kb, vf=vf, accb=accb, acol=acol, pi_=pi_, last=last: nc.tensor.matmul(
                        psB[accb][:, acol:acol + 65], lhsT=pT[pi][:, i * 128:(i + 1) * 128], rhs=vf(kb),
                        start=(pi_ == 0), stop=last),
                        reads=["pT%d" % pi, vres], writes=[PB[accb]], signal=(i == n - 1 or last))
                    if not last:
                        continue
                    if grp == "A" and slot % 4 == 3:
                        hs_ = slot - 3
                        S.op("dve", lambda hs_=hs_, accb=accb: nc.vector.tensor_copy(
                            out=oacc[:, hs_:hs_ + 4, :], in_=psB[accb][:, 0:260].rearrange("p (h c) -> p h c", h=4)),
                            reads=[PB[accb]], writes=[PB[accb], "oacc"])
                    elif grp != "A" and slot == 11:
                        g = grp
                        pv = psB[accb][:, 0:260].rearrange("p (h c) -> p h c", h=4)
                        if g == 0:
                            S.op("dve", lambda pv=pv: nc.vector.tensor_copy(out=oacc[:, 8:12, :], in_=pv),
                                 reads=[PB[accb]], writes=[PB[accb], "oacc"])
                        else:
                            S.op("dve", lambda pv=pv, g=g: nc.vector.tensor_tensor(
                                out=otmp[:], in0=pv, in1=abt[:, blk, g, :, 0:1].broadcast_to([128, 4, 65]), op=ALU.mult),
                                reads=[PB[accb], "abt"], writes=[PB[accb], "otmp"])
                            S.op("pool", lambda: nc.gpsimd.tensor_tensor(out=oacc[:, 8:12, :], in0=oacc[:, 8:12, :], in1=otmp[:],
                                                                         op=ALU.add), reads=["otmp", "oacc"], writes=["oacc"])

            pend = []
            for batch in batches:
                pi = front(batch)
                pend.append((batch, pi))
                if len(pend) > ADEPTH:
                    back(*pend.pop(0))
            while pend:
                back(*pend.pop(0))

        def finish_o(P, s):
            S.op("dve", lambda: nc.vector.tensor_tensor(out=oacc[0:P, 0:8, 64:65], in0=oacc[0:P, 0:8, 64:65],
                                                        in1=esink[0:P, :].unsqueeze(2), op=ALU.add),
                 reads=["oacc", "esink"], writes=["oacc"])
            S.op("dve", lambda: nc.vector.reciprocal(out=rden[0:P, :].unsqueeze(2), in_=oacc[0:P, :, 64:65]),
                 reads=["oacc"], writes=["rden"])
            S.op("dve", lambda: nc.vector.tensor_tensor(
                out=onrm[0:P, :].rearrange("p (h d) -> p h d", h=12), in0=oacc[0:P, :, 0:HD],
                in1=rden[0:P, :].unsqueeze(2).broadcast_to([P, 12, HD]), op=ALU.mult),
                reads=["oacc", "rden"], writes=["onrm"])
            t = next_tb()
            for c in range(6):
                S.op("pe", lambda c=c: nc.tensor.transpose(out=psT[t][:, c * 128:c * 128 + P],
                                                           in_=onrm[0:P, c * 128:(c + 1) * 128], identity=idb[0:P, 0:P]),
                     reads=["onrm", "idb"], writes=[TB[t]], signal=(c == 5))
            copy_op(ew(), oT[:, :, s * 128:s * 128 + P],
                    psT[t][:, 0:768].rearrange("p (k c) -> p k c", k=6)[:, :, 0:P], [TB[t]], [TB[t], "oT"])

        def merge_out(P, nsub, NT):
            for m in range(8):
                r = next_ring()
                S.dma("sp", "ring%d" % r, ring[r][:, 0:2816], s_m[m], reads=["s_m_ga", "s_m_gb", "s_m_upa", "s_m_upb"], writes=["ring%d" % r])
                rv = ring[r][:, 0:2816].rearrange("p (k c) -> p k c", k=22)
                ga_b, gb_b = (2, 3) if m % 2 == 0 else (4, 5)
                specs = [(ga_b, 6, 8, hT, 0), (gb_b, 14, 8, hT, 0), (0, 0, 4, oT, 0), (1, 4, 2, oT, 4)]
                for (b, k0, nk, src, s0) in specs:
                    for kc in range(nk):
                        S.op("pe", lambda b=b, k0=k0, kc=kc, src=src, s0=s0, nk=nk: nc.tensor.matmul(
                            psB[b][:, 0:NT], lhsT=rv[:, k0 + kc, :], rhs=src[:, s0 + kc, 0:NT],
                            start=(kc == 0), stop=(kc == nk - 1)),
                            reads=["ring%d" % r, "oT" if src is oT else "hT"], writes=[PB[b]], signal=(kc == nk - 1))
                ia, ib = (2 * m) % 3, (2 * m + 1) % 3
                sA, sB = sg[ia], sg[ib]
                S.op("act", lambda: nc.scalar.activation(out=sA[:, 0:NT], in_=psB[ga_b][:, 0:NT], func=AF.Sigmoid),
                     reads=[PB[ga_b]], writes=[PB[ga_b], "sg%d" % ia])
                S.op("act", lambda: nc.scalar.activation(out=sB[:, 0:NT], in_=psB[gb_b][:, 0:NT], func=AF.Sigmoid),
                     reads=[PB[gb_b]], writes=[PB[gb_b], "sg%d" % ib])
                S.op("dve", lambda: nc.vector.tensor_tensor(out=sA[:, 0:NT], in0=sA[:, 0:NT], in1=psB[0][:, 0:NT], op=ALU.mult),
                     reads=["sg%d" % ia, PB[0]], writes=["sg%d" % ia, PB[0]])
                S.op("dve", lambda: nc.vector.tensor_tensor(out=sB[:, 0:NT], in0=sB[:, 0:NT], in1=psB[1][:, 0:NT], op=ALU.mult),
                     reads=["sg%d" % ib, PB[1]], writes=["sg%d" % ib, PB[1]])
                S.op("pool", lambda m=m: nc.gpsimd.tensor_tensor(out=big[:, m, 0:NT], in0=sA[:, 0:NT], in1=sB[:, 0:NT], op=ALU.add),
                     reads=["sg%d" % ia, "sg%d" % ib], writes=["big%d" % m])
            rs = []
            for ch in range(2):
                r = next_ring()
                S.dma("sp", "ring%d" % r, ring[r][:], s_o[ch].rearrange("p k c -> p (k c)"), reads=["s_o%d" % ch], writes=["ring%d" % r])
                rs.append(r)
            for s in range(nsub):
                for ch in range(2):
                    r = rs[ch]
                    rv = ring[r][:].rearrange("p (k c) -> p k c", k=8)
                    b = 4 + ch
                    for kc in range(8):
                        S.op("pe", lambda kc=kc, b=b, rv=rv: nc.tensor.matmul(
                            psB[b][0:P, :], lhsT=big[:, kc, s * 128:s * 128 + P], rhs=rv[:, kc, :], start=(kc == 0), stop=(kc == 7)),
                            reads=["big%d" % kc, "ring%d" % r], writes=[PB[b]], signal=(kc == 7))
                    S.op("dve", lambda b=b, ch=ch: nc.vector.tensor_tensor(
                        out=xt[0:P, xmap[s], ch * 512:(ch + 1) * 512], in0=psB[b][0:P, :], in1=xt[0:P, xmap[s], ch * 512:(ch + 1) * 512], op=ALU.add),
                        reads=[PB[b], "xt%d" % xmap[s]], writes=[PB[b], "xt%d" % xmap[s]])

        import os as _os2
        LOOK = int(_os2.environ.get("KLOOK", "3"))
        ADEPTH = int(_os2.environ.get("KADEPTH", "2"))
        PDEPTH = int(_os2.environ.get("KPDEPTH", "2"))
        copies = []
        if with_sample:
            copies.append((o_as[:, 0:127].rearrange("b t k h d -> b (t k h d)"), c_a[:, 1:128].rearrange("b t k h d -> b (t k h d)")))
            copies.append((o_bs[0][:, 0:127].rearrange("b t k h d -> b (t k h d)"), c_b[0][:, 1:128].rearrange("b t k h d -> b (t k h d)")))
            for bb in range(0, NS, 4):
                copies.append((o_bs[1][bb:bb + 4, 0:511].rearrange("b t k h d -> b (t k h d)"),
                               c_b[1][bb:bb + 4, 1:512].rearrange("b t k h d -> b (t k h d)")))
            for bb in range(NS):
                copies.append((o_bs[2][bb:bb + 1, 0:2047].rearrange("b t k h d -> b (t k h d)"),
                               c_b[2][bb:bb + 1, 1:2048].rearrange("b t k h d -> b (t k h d)")))

        def issue_copies(n):
            for _ in range(n):
                if copies:
                    o, i = copies.pop(0)
                    S.dma("sp", "ccopy", o, i)

        import os
        STAGE = int(os.environ.get("KSTAGE", "9"))
        NTILES = int(os.environ.get("KTILES", "8"))
        tcount = 0
        for seq in range(NSEQ):
            for tl in range(4):
                if STAGE < 1 or tcount >= NTILES:
                    continue
                tcount += 1
                blk0 = tl * 4
                xmap[:] = [(4 * (tcount - 1) + s_) % 5 for s_ in range(4)]
                xfree = (4 * (tcount - 1) + 4) % 5
                if tcount == 1:
                    for s_ in range(4):
                        S.dma("sp", "xi%d" % s_, xt[:, s_, :], x_p[seq, tl * 512 + s_ * 128:tl * 512 + (s_ + 1) * 128, :], writes=["xt%d" % s_])
                ffn(0, 128, 4, 512)
                if STAGE >= 2:
                    norm_T(128, 4, 512)
                    project_prompt(seq, blk0)
                if STAGE >= 3:
                    for s in range(4):
                        issue_copies(1 if s < 3 else 0)
                        attention_block(s, blk0 + s)
                        finish_o(128, s)
                if STAGE >= 4:
                    merge_out(128, 4, 512)
                nxt = None
                if tcount < NTILES and not (seq == NSEQ - 1 and tl == 3):
                    nxt = (seq, tl + 1) if tl < 3 else (seq + 1, 0)

                def sub_done(s_, seq=seq, tl=tl, nxt=nxt):
                    sl_ = xmap[s_]
                    S.dma("pool", "yout%d" % sl_, y_p[seq, tl * 512 + s_ * 128:tl * 512 + (s_ + 1) * 128, :], xt[:, sl_, :],
                          reads=["xt%d" % sl_])
                    if nxt is not None and s_ < 3:
                        S.dma("pool", "xt%d" % sl_, xt[:, sl_, :],
                              x_p[nxt[0], nxt[1] * 512 + (s_ + 1) * 128:nxt[1] * 512 + (s_ + 2) * 128, :], writes=["xt%d" % sl_])

                if nxt is not None:
                    S.dma("pool", "xt%d" % xfree, xt[:, xfree, :], x_p[nxt[0], nxt[1] * 512:nxt[1] * 512 + 128, :], writes=["xt%d" % xfree])
                if STAGE >= 5:
                    ffn(1, 128, 4, 512, sub_done)
                else:
                    for s_ in range(4):
                        sub_done(s_)

        issue_copies(len(copies))
        if not with_sample:
            hs.close()
        if with_sample:
            S.barrier()
            hs.close()
            ss_ = ExitStack()
            es.enter_context(ss_)
            zs = sb("zs", [128, 3072], F32, ss_)
            zs_box[0] = zs
            KVt = sb("KVt", [128, 129, HD], F32, ss_)
            qs = sb("qs", [128, HD], F32, ss_)
            ssc = sb("ssc", [128, 129], F32, ss_)
            pp = sb("pp", [128, 129], F32, ss_)
            sal = sb("sal", [128, 4, 129], F32, ss_)
            osl = sb("osl", [128, 65], F32, ss_)
            oslB = sb("oslB", [128, 65], F32, ss_)
            S.dma("sp", "sal", sal[:], c_sal, writes=["sal"])
            xmap[:] = [0, 1, 2, 3]
            S.dma("sp", "xi0", xt[0:NS, 0, :], x_s, writes=["xt0"])
            ffn(0, NS, 1, NS)
            norm_T(NS, 1, NS)
            project(NS, 1, NS, None, 0)
            S.dma("sp", "nrow", o_as[:, 127, 0, :, :].rearrange("b h d -> b (h d)"), zs[0:NS, 512:640], reads=["zs"])
            S.dma("sp", "nrow", o_as[:, 127, 1, :, :].rearrange("b h d -> b (h d)"), zs[0:NS, 640:768], reads=["zs"])
            for g in range(3):
                W = [128, 512, 2048][g]
                S.dma("sp", "nrow", o_bs[g][:, W - 1, 0, :, :].rearrange("b h d -> b (h d)"),
                      zs[0:NS, 1536 + g * 256:1536 + (g + 1) * 256], reads=["zs"])
                S.dma("sp", "nrow", o_bs[g][:, W - 1, 1, :, :].rearrange("b h d -> b (h d)"),
                      zs[0:NS, 2304 + g * 256:2304 + (g + 1) * 256], reads=["zs"])

            def sample_pass(pi, nslot, cache, dil, qc0, kc0, vc0, kvh_of):
                NP = nslot * NS
                cv = cache.rearrange("b (j r) k h d -> b j r k h d", r=dil)
                for kv, col0 in ((0, kc0), (1, vc0)):
                    for sl in range(nslot):
                        S.dma("sp", "KVt", KVt[sl * NS:(sl + 1) * NS, 0:128, :], cv[:, 0:128, 0, kv, kvh_of(sl), :], writes=["KVt"], disjoint=True)
                        S.dma("sp", "KVt", KVt[sl * NS:(sl + 1) * NS, 128, :],
                              zs[0:NS, col0 + kvh_of(sl) * HD:col0 + (kvh_of(sl) + 1) * HD], reads=["zs"], writes=["KVt"], disjoint=True)
                        if kv == 0:
                            S.dma("sp", "qs", qs[sl * NS:(sl + 1) * NS, :], zs[0:NS, qc0 + sl * HD:qc0 + (sl + 1) * HD],
                                  reads=["zs"], writes=["qs"], disjoint=True)
                    if kv == 0:
                        S.op("dve", lambda: nc.vector.tensor_tensor(out=KVt[0:NP], in0=KVt[0:NP],
                                                                    in1=qs[0:NP, :].unsqueeze(1).broadcast_to([NP, 129, HD]), op=ALU.mult),
                             reads=["KVt", "qs"], writes=["KVt"])
                        S.op("dve", lambda: nc.vector.tensor_reduce(out=ssc[0:NP, :], in_=KVt[0:NP], axis=AX.X, op=ALU.add),
                             reads=["KVt"], writes=["ssc"])
                        S.op("dve", lambda: nc.vector.tensor_tensor(out=ssc[0:NP, :], in0=ssc[0:NP, :], in1=sal[0:NP, pi, :], op=ALU.add),
                             reads=["ssc", "sal"], writes=["ssc"])
                        tgt = osl if pi == 0 else (oslB if pi == 1 else pp)
                        S.op("act", lambda: nc.scalar.activation(out=pp[0:NP, :], in_=ssc[0:NP, :], func=AF.Exp,
                                                                 accum_out=small[0:NP, 40:41]),
                             reads=["ssc"], writes=["pp", "small"])
                    else:
                        S.op("dve", lambda: nc.vector.tensor_tensor(out=KVt[0:NP], in0=KVt[0:NP],
                                                                    in1=pp[0:NP, :].unsqueeze(2).broadcast_to([NP, 129, HD]), op=ALU.mult),
                             reads=["KVt", "pp"], writes=["KVt"])
                        dst = osl if pi == 0 else oslB
                        if pi <= 1:
                            S.op("dve", lambda: nc.vector.tensor_reduce(out=dst[0:NP, 0:HD], in_=KVt[0:NP].rearrange("p j d -> p d j"),
                                                                        axis=AX.X, op=ALU.add), reads=["KVt"], writes=[("osl" if pi == 0 else "oslB")])
                            S.op("act", lambda: nc.scalar.copy(out=dst[0:NP, HD:HD + 1], in_=small[0:NP, 40:41]),
                                 reads=["small"], writes=[("osl" if pi == 0 else "oslB")])
                        else:
                            S.op("dve", lambda: nc.vector.tensor_reduce(out=osl[0:NP, 0:HD], in_=KVt[0:NP].rearrange("p j d -> p d j"),
                                                                        axis=AX.X, op=ALU.add), reads=["KVt"], writes=["osl"])
                            S.op("act", lambda: nc.scalar.copy(out=osl[0:NP, HD:HD + 1], in_=small[0:NP, 40:41]),
                                 reads=["small"], writes=["osl"])
                            S.op("dve", lambda: nc.vector.tensor_tensor(out=oslB[0:NP, :], in0=oslB[0:NP, :], in1=osl[0:NP, :], op=ALU.add),
                                 reads=["osl", "oslB"], writes=["oslB"])

            sample_pass(0, 8, c_a, 1, 0, 512, 640, lambda sl: sl // 4)
            for sl in range(8):
                S.dma("sp", "oacc", oacc[0:NS, sl, :], osl[sl * NS:(sl + 1) * NS, :], reads=["osl"], writes=["oacc"], disjoint=True)
            for g in range(3):
                sample_pass(1 + g, 4, c_b[g], DILS[g], 768 + g * 256, 1536 + g * 256, 2304 + g * 256, lambda sl: sl)
            for sl in range(4):
                S.dma("sp", "oacc", oacc[0:NS, 8 + sl, :], oslB[sl * NS:(sl + 1) * NS, :], reads=["oslB"], writes=["oacc"], disjoint=True)
            finish_o(NS, 0)
            merge_out(NS, 1, NS)
            ffn(1, NS, 1, NS)
            S.dma("pool", "yout0", y_s, xt[0:NS, 0, :], reads=["xt0"])

        S.finish()
        print("ops", S.nops, "waits", S.nwaits, "cnt", S.cnt, "dma sems", len(S.dsem))
    sal_np = np.zeros((128, 4, 129), np.float64)
    j = np.arange(129)
    dist = np.where(j < 128, 128 - j, 0).astype(np.float64)
    for p in range(128):
        sal_np[p, 0] = -SLOPES[p // NS] * dist
        for g in range(3):
            sal_np[p, 1 + g] = -SLOPES[8 + 4 * g + (p // NS) % 4] * DILS[g] * dist
    consts = {"c_ident": ident_np, "c_dt": dt_np, "c_ab": ab_np.reshape(128, -1), "c_sal": sal_np.astype(np.float32)}
    return nc, consts


_CACHE = {}


def kernel(**inputs):
    import os
    WS = os.environ.get("KNOSAMPLE", "0") != "1"
    if "prog" not in _CACHE:
        _CACHE["prog"] = build_program(WS)
    nc, consts = _CACHE["prog"]
    f = lambda k: np.ascontiguousarray(np.asarray(inputs[k], dtype=np.float32))
    xp = f("x_prompt"); xs = f("x_sample").reshape(128, D)
    ca = f("cache_a_kv")[0]; cb1 = f("cache_b1_kv")[0]; cb2 = f("cache_b2_kv")[0]; cb3 = f("cache_b3_kv")[0]
    shared = {
        "norm_ffn1": f("norm_ffn1")[0], "norm_mix": f("norm_mix")[0], "norm_ffn2": f("norm_ffn2")[0],
        "w1_gate": f("w1_gate")[0], "w1_up": f("w1_up")[0], "w1_down": f("w1_down")[0],
        "w2_gate": f("w2_gate")[0], "w2_up": f("w2_up")[0], "w2_down": f("w2_down")[0],
        "w_in": f("w_in")[0], "q_norm_a": f("q_norm_a")[0], "k_norm_a": f("k_norm_a")[0],
        "q_norm_b": f("q_norm_b")[0], "k_norm_b": f("k_norm_b")[0], "sinks_a": f("sinks_a")[0].reshape(8),
        "w_up_a": f("w_up_a")[0], "w_up_b": f("w_up_b")[0], "w_o": f("w_o")[0],
    }
    shared.update(consts)
    in_maps = []
    for c in range(NCORES):
        m = dict(shared)
        m["x_prompt"] = xp[c * NSEQ:(c + 1) * NSEQ]
        m["x_sample"] = xs[c * NS:(c + 1) * NS]
        if WS:
            m["cache_a"] = ca[c * NS:(c + 1) * NS]
            m["cache_b1"] = cb1[c * NS:(c + 1) * NS]
            m["cache_b2"] = cb2[c * NS:(c + 1) * NS]
            m["cache_b3"] = cb3[c * NS:(c + 1) * NS]
        in_maps.append(m)
    res = run_bass_kernel_spmd(nc, in_maps, core_ids=list(range(NCORES)))
    R = res.results
    cat = lambda k: np.concatenate([np.asarray(r[k]) for r in R], axis=0)
    y_prompt = cat("y_prompt")
    y_sample = cat("y_sample").reshape(128, 1, D)
    outs = [y_prompt, y_sample]
    for k in ["a_p", "b1_p", "b2_p", "b3_p"] + (["a_s", "b1_s", "b2_s", "b3_s"] if WS else []):
        outs.append(cat(k)[None])
    return tuple(o.astype(np.float32) for o in outs)
```

```python
import numpy as np
from contextlib import ExitStack
import concourse.bass as bass
import concourse.mybir as mybir
from concourse.bass_utils import run_bass_kernel_spmd

F32 = mybir.dt.float32
BF16 = mybir.dt.bfloat16
AF = mybir.ActivationFunctionType
ALU = mybir.AluOpType
AX = mybir.AxisListType

NCORES = 8
D = 1024
DFF = 2816
NF = 22
SEQ = 2048
NSEQ = 2
HD = 64
EPS = 1e-6
NS = 16
NSTG = 5
N_AL = 20
SLOPES = [2.0 ** (-8.0 * i / N_AL) for i in range(1, N_AL + 1)]
DILS = [1, 4, 16]
WINB = [1, 4, 16]


class Sched:
    def __init__(self, nc, es):
        self.nc = nc
        self.es = es
        self.eng = {"pe": nc.tensor, "act": nc.scalar, "dve": nc.vector, "pool": nc.gpsimd, "sp": nc.sync}
        self.sem = {e: es.enter_context(nc.semaphore("s_" + e)) for e in ["pe", "act", "dve", "pool"]}
        self.cnt = {e: 0 for e in self.sem}
        self.pending = {e: [] for e in self.sem}
        self.lastw = {}
        self.readers = {}
        self.seen = {e: {} for e in self.eng}
        self.dsem = {}
        self.nwaits = 0
        self.nops = {e: 0 for e in self.eng}

    def _need(self, reads, writes):
        toks = []
        for r in reads:
            t = self.lastw.get(r)
            if t is not None:
                toks.append(t)
        for w in writes:
            t = self.lastw.get(w)
            if t is not None:
                toks.append(t)
            toks.extend(self.readers.get(w, []))
        return toks

    def _emit_waits(self, e, toks):
        best = {}
        for (own, sem, val) in toks:
            if own == e and (val is None or e == "pe"):
                continue
            if val is None:
                raise RuntimeError("dependency on unsignalled op")
            k = sem.name
            if self.seen[e].get(k, 0) >= val:
                continue
            if k not in best or best[k][1] < val:
                best[k] = (sem, val)
        for k, (sem, val) in best.items():
            self.eng[e].wait_ge(sem, val)
            self.seen[e][k] = val
            self.nwaits += 1

    def _record(self, tok, reads, writes):
        for r in reads:
            lst = self.readers.setdefault(r, [])
            lst[:] = [t for t in lst if t[1].name != tok[1].name]
            lst.append(tok)
        for w in writes:
            self.lastw[w] = tok
            self.readers[w] = []

    def op(self, e, fn, reads=(), writes=(), signal=True):
        toks = self._need(reads, writes)
        self._emit_waits(e, toks)
        inst = fn()
        self.nops[e] += 1
        if signal:
            self.cnt[e] += 1
            inst.then_inc(self.sem[e], 1)
            tok = (e, self.sem[e], self.cnt[e])
            for (rs, ws) in self.pending[e]:
                self._record(tok, rs, ws)
            self.pending[e] = []
            self._record(tok, reads, writes)
        else:
            self.pending[e].append((tuple(reads), tuple(writes)))
            bad = (e, self.sem[e], None)
            for w in writes:
                self.lastw[w] = bad
                self.readers[w] = []
            for r in reads:
                self.readers.setdefault(r, []).append(bad)
        return inst

    def dma(self, q, slot, out, in_, reads=(), writes=(), disjoint=False, **kw):
        if slot not in self.dsem:
            self.dsem[slot] = [self.es.enter_context(self.nc.semaphore("d_" + slot)), 0]
        ds = self.dsem[slot]
        toks = self._need(reads, writes)
        if disjoint:
            toks = [t for t in toks if t[1].name != ds[0].name]
        self._emit_waits(q, toks)
        ds[1] += 16
        self.eng[q].dma_start(out=out, in_=in_, **kw).then_inc(ds[0], 16)
        self.nops[q] += 1
        tok = ("dma", ds[0], ds[1])
        self._record(tok, reads, writes)
        return tok

    def barrier(self):
        for e in ["pe", "act", "dve", "pool", "sp"]:
            for f in ["pe", "act", "dve", "pool"]:
                if f != e and self.cnt[f] > self.seen[e].get(self.sem[f].name, 0):
                    self.eng[e].wait_ge(self.sem[f], self.cnt[f])
                    self.seen[e][self.sem[f].name] = self.cnt[f]
            for slot, (sem, val) in self.dsem.items():
                if val > self.seen[e].get(sem.name, 0):
                    self.eng[e].wait_ge(sem, val)
                    self.seen[e][sem.name] = val

    def finish(self):
        for e in ["sp"]:
            for slot, (sem, val) in self.dsem.items():
                if val > 0:
                    self.eng[e].wait_ge(sem, val)
            for f in ["pe", "act", "dve", "pool"]:
                if self.cnt[f] > 0:
                    self.eng[e].wait_ge(self.sem[f], self.cnt[f])


def host_tables():
    ident = np.eye(128, dtype=np.float32)
    k = np.arange(128)[:, None].astype(np.float64)
    q = np.arange(128)[None, :].astype(np.float64)
    tabs = []
    idx = {}
    for h in range(8):
        s = SLOPES[h]
        idx[("A", h, 1)] = len(tabs); tabs.append(np.where(q <= k, np.exp(-s * (128 + q - k)), 0.0))
        idx[("A", h, 0)] = len(tabs); tabs.append(np.where(q >= k, np.exp(-s * (q - k)), 0.0))
    for g in range(3):
        dil = DILS[g]
        mod = ((q - k) % dil) == 0
        for h in range(4):
            s = SLOPES[8 + g * 4 + h]
            if g == 0:
                idx[(g, h, 1)] = len(tabs); tabs.append(np.where(q <= k, np.exp(-s * (128 + q - k)), 0.0))
                idx[(g, h, 0)] = len(tabs); tabs.append(np.where(q >= k, np.exp(-s * (q - k)), 0.0))
            elif g == 1:
                idx[(g, h, 4)] = len(tabs); tabs.append(np.where((q <= k) & mod, np.exp(-s * (q - k)), 0.0))
                for dl in (3, 2, 1):
                    idx[(g, h, dl)] = len(tabs); tabs.append(np.where(mod, np.exp(-s * (q - k)), 0.0))
                idx[(g, h, 0)] = len(tabs); tabs.append(np.where((q >= k) & mod, np.exp(-s * (q - k)), 0.0))
            else:
                idx[(g, h, "m")] = len(tabs); tabs.append(np.where(mod, np.exp(-s * (q - k)), 0.0))
                idx[(g, h, "0")] = len(tabs); tabs.append(np.where((q >= k) & mod, np.exp(-s * (q - k)), 0.0))
    dt = np.stack(tabs, axis=1).astype(np.float32)
    ab = np.ones((128, 16, 3, 4, 2), np.float64)
    for g in (1, 2):
        for h in range(4):
            s = SLOPES[8 + g * 4 + h]
            for b in range(16):
                ab[:, b, g, h, 0] = np.exp(-s * 128.0 * b)
                ab[:, b, g, h, 1] = np.exp(s * 128.0 * b)
    return ident, dt, idx, ab.astype(np.float32)


def build_program(with_sample=True):
    ident_np, dt_np, TIDX, ab_np = host_tables()
    NTAB = dt_np.shape[1]
    nc = bass.Bass("TRN2", target_bir_lowering=False)

    def din(name, shape, dt=F32):
        return nc.dram_tensor(name, list(shape), dt, kind="ExternalInput").ap()

    def dout(name, shape):
        return nc.dram_tensor(name, list(shape), F32, kind="ExternalOutput").ap()

    def dscr(name, shape, dt=BF16):
        return nc.dram_tensor(name, list(shape), dt, kind="Internal").ap()

    x_p = din("x_prompt", [NSEQ, SEQ, D])
    x_s = din("x_sample", [NS, D])
    if with_sample:
        c_a = din("cache_a", [NS, 128, 2, 2, HD])
        c_b = [din("cache_b1", [NS, 128, 2, 4, HD]), din("cache_b2", [NS, 512, 2, 4, HD]),
               din("cache_b3", [NS, 2048, 2, 4, HD])]
    g_f1 = din("norm_ffn1", [D]); g_mx = din("norm_mix", [D]); g_f2 = din("norm_ffn2", [D])
    w_g = [din("w1_gate", [D, DFF]), din("w2_gate", [D, DFF])]
    w_u = [din("w1_up", [D, DFF]), din("w2_up", [D, DFF])]
    w_d = [din("w1_down", [DFF, D]), din("w2_down", [DFF, D])]
    w_in = din("w_in", [D, 5120])
    qn_a = din("q_norm_a", [HD]); kn_a = din("k_norm_a", [HD]); qn_b = din("q_norm_b", [HD]); kn_b = din("k_norm_b", [HD])
    sinks = din("sinks_a", [8])
    w_upa = din("w_up_a", [512, D]); w_upb = din("w_up_b", [256, D]); w_o = din("w_o", [D, D])
    c_ident = din("c_ident", [128, 128]); c_dt = din("c_dt", [128, NTAB, 128]); c_ab = din("c_ab", [128, 16 * 3 * 4 * 2])
    c_sal = din("c_sal", [128, 4, 17 * 8])
    c_rep = din("c_rep", [16, 128]); c_sel = din("c_sel", [128, 16])
    y_p = dout("y_prompt", [NSEQ, SEQ, D]); y_s = dout("y_sample", [NS, D])
    o_ap = dout("a_p", [NSEQ, 128, 2, 2, HD])
    o_bp = [dout("b1_p", [NSEQ, 128, 2, 4, HD]), dout("b2_p", [NSEQ, 512, 2, 4, HD]), dout("b3_p", [NSEQ, 2048, 2, 4, HD])]
    if with_sample:
        o_as = dout("a_s", [NS, 128, 2, 2, HD])
        o_bs = [dout("b1_s", [NS, 128, 2, 4, HD]), dout("b2_s", [NS, 512, 2, 4, HD]), dout("b3_s", [NS, 2048, 2, 4, HD])]
    s_gu = [dscr("s_gu1", [11, 128, 2, 2, 8, 128]), dscr("s_gu2", [11, 128, 2, 2, 8, 128])]
    s_d = [dscr("s_d1", [NF, 128, D]), dscr("s_d2", [NF, 128, D])]
    s_in = dscr("s_in", [6, 128, 8, 512])
    s_m = dscr("s_m", [8, 128, 22 * 128])
    s_o = dscr("s_o", [2, 128, 8, 512])

    es = ExitStack()
    with es:
        S = Sched(nc, es)

        def sb(name, shape, dt, stack=es):
            return stack.enter_context(nc.sbuf_tensor(name, list(shape), dt))

        psT = [es.enter_context(nc.psum_tensor("psT%d" % i, [128, 1024], BF16)) for i in range(2)]
        psB = [es.enter_context(nc.psum_tensor("psB%d" % i, [128, 512], F32)) for i in range(6)]
        PB = ["P%d" % i for i in range(6)]
        TB = ["T0", "T1"]

        idb = sb("idb", [128, 128], BF16)
        dtab = sb("dtab", [128, NTAB, 128], BF16)
        abt = sb("abt", [128, 16, 3, 4, 2], F32)
        gq_a = sb("gq_a", [128, HD], F32); gk_a = sb("gk_a", [128, HD], F32)
        gq_b = sb("gq_b", [128, HD], F32); gk_b = sb("gk_b", [128, HD], F32)
        esink = sb("esink", [128, 8], F32)
        mhalf = sb("mhalf", [128, 8], F32)
        small = sb("small", [128, 96], F32)

        S.op("pool", lambda: nc.gpsimd.memset(mhalf[:], -0.5), writes=["mhalf"])
        gcol = sb("gcol", [128, 4], F32)
        for ci, v in enumerate([qn_a, kn_a, qn_b, kn_b]):
            for hf in range(2):
                S.dma("sp", "gcol", gcol[hf * 64:(hf + 1) * 64, ci:ci + 1], v.rearrange("(d o) -> d o", o=1), writes=["gcol"],
                      allow_slow_non_contiguous=True)
        S.op("act", lambda: nc.scalar.mul(out=gcol[:, 0:1], in_=gcol[:, 0:1], mul=HD ** -0.5), reads=["gcol"], writes=["gcol"])
        S.op("act", lambda: nc.scalar.mul(out=gcol[:, 2:3], in_=gcol[:, 2:3], mul=HD ** -0.5), reads=["gcol"], writes=["gcol"])

        def bcast_load(dst, src, n, name):
            S.dma("sp", name, dst, src.rearrange("(o n) -> o n", o=1).broadcast_to([128, n]), writes=[name])

        bcast_load(gq_a[:], qn_a, HD, "gq_a"); bcast_load(gk_a[:], kn_a, HD, "gk_a")
        bcast_load(gq_b[:], qn_b, HD, "gq_b"); bcast_load(gk_b[:], kn_b, HD, "gk_b")
        bcast_load(esink[:], sinks, 8, "esink")
        S.dma("sp", "abt", abt[:].rearrange("p a b c d -> p (a b c d)"), c_ab, writes=["abt"])
        S.op("act", lambda: nc.scalar.mul(out=gq_a[:], in_=gq_a[:], mul=HD ** -0.5), reads=["gq_a"], writes=["gq_a"])
        S.op("act", lambda: nc.scalar.mul(out=gq_b[:], in_=gq_b[:], mul=HD ** -0.5), reads=["gq_b"], writes=["gq_b"])
        S.op("act", lambda: nc.scalar.activation(out=esink[:], in_=esink[:], func=AF.Exp), reads=["esink"], writes=["esink"])

        with ExitStack() as ps:
            stg_in = [sb("stg_in%d" % i, [128, 8, 1024], F32, ps) for i in range(2)]
            stg_out = [sb("stg_out%d" % i, [128, 8, 1024], BF16, ps) for i in range(2)]
            gains = sb("gains", [128, 3, 8], F32, ps)
            dt32 = sb("dt32", [128, NTAB, 128], F32, ps)
            id32 = sb("id32", [128, 128], F32, ps)
            for i, g in enumerate([g_f1, g_mx, g_f2]):
                S.dma("sp", "gains", gains[:, i, :], g.rearrange("(c p) -> p c", p=128), writes=["gains"],
                      allow_slow_non_contiguous=True)
            S.dma("sp", "id32", id32[:], c_ident, writes=["id32"])
            S.dma("sp", "dt32", dt32[:], c_dt, writes=["dt32"])
            S.op("dve", lambda: nc.vector.tensor_copy(out=idb[:], in_=id32[:]), reads=["id32"], writes=["idb"])
            S.op("dve", lambda: nc.vector.tensor_copy(out=dtab[:], in_=dt32[:]), reads=["dt32"], writes=["dtab"])

            pcount = [0]
            def block(src_ap, nk, n, gain_idx, perm_f, stores):
                i = pcount[0] % 2
                pcount[0] += 1
                tin = stg_in[i][:, 0:nk, 0:n]
                S.dma("sp", "stg_in%d" % i, tin, src_ap, writes=["stg_in%d" % i])
                flat = stg_out[i][:].rearrange("p k c -> p (k c)")[:, 0:nk * n]
                if perm_f:
                    f = n // 128
                    ov_all = flat.rearrange("p (f k c) -> p k f c", f=f, k=nk)
                    iv_all = tin.rearrange("p k (f c) -> p k f c", c=128)
                else:
                    ov_all = flat.rearrange("p (k c) -> p k c", k=nk)
                    iv_all = tin
                if gain_idx is not None:
                    for kc in range(nk):
                        S.op("act", lambda kc=kc: nc.scalar.activation(out=ov_all[:, kc], in_=iv_all[:, kc], func=AF.Copy,
                                                                       scale=gains[:, gain_idx, kc:kc + 1]),
                             reads=["stg_in%d" % i, "gains"], writes=["stg_out%d" % i], signal=(kc == nk - 1))
                else:
                    e = "dve" if pcount[0] % 2 else "pool"
                    if e == "dve":
                        S.op("dve", lambda: nc.vector.tensor_copy(out=ov_all, in_=iv_all), reads=["stg_in%d" % i], writes=["stg_out%d" % i])
                    else:
                        S.op("pool", lambda: nc.gpsimd.tensor_copy(out=ov_all, in_=iv_all), reads=["stg_in%d" % i], writes=["stg_out%d" % i])
                for (dst, src, res) in stores(flat):
                    S.dma("pool", "stg_out%d" % i, dst, src, reads=["stg_out%d" % i], writes=[res], disjoint=True)

            def wsrc(w, c0, ncol, nk):
                return w.rearrange("(c p) n -> p c n", p=128)[:, 0:nk, c0:c0 + ncol]

            for l in range(2):
                gi = 0 if l == 0 else 2
                for c0, n in ((0, 1024), (1024, 1024), (2048, 768)):
                    for gu, w in enumerate([w_g[l], w_u[l]]):
                        def st(flat, c0=c0, n=n, gu=gu, l=l):
                            out = []
                            for q_ in range(n // 256):
                                fp = c0 // 256 + q_
                                out.append((s_gu[l][fp, :, :, gu, :, :].rearrange("p f k c -> p f (k c)"),
                                            flat[:, q_ * 2048:(q_ + 1) * 2048].rearrange("p (f x) -> p f x", f=2),
                                            "s_gu%d_%d_%d" % (l, fp, gu)))
                            return out
                        block(wsrc(w, c0, n, 8), 8, n, gi, True, st)
                for f0, nf in ((0, 8), (8, 8), (16, 6)):
                    def st(flat, f0=f0, nf=nf, l=l):
                        return [(s_d[l][f0:f0 + nf].rearrange("f p c -> p f c"), flat.rearrange("p (f c) -> p f c", f=nf),
                                 "s_d%d_%d" % (l, f0))]
                    block(w_d[l].rearrange("(f p) c -> p f c", p=128)[:, f0:f0 + nf, :], nf, 1024, None, False, st)
            for bk in range(3):
                def st(flat, bk=bk):
                    v = flat.rearrange("p (k c) -> p k c", k=8)
                    return [(s_in[2 * bk + hf], v[:, :, hf * 512:(hf + 1) * 512], "s_in%d" % (2 * bk + hf)) for hf in range(2)]
                block(wsrc(w_in, bk * 1024, 1024, 8), 8, 1024, 1, False, st)
            for which, c0, off in [("ga", 3072, 6 * 128), ("gb", 4096, 14 * 128)]:
                def st(flat, off=off, which=which):
                    return [(s_m[:, :, off:off + 1024].rearrange("m p c -> p m c"), flat.rearrange("p (m x) -> p m x", m=8), "s_m_" + which)]
                block(wsrc(w_in, c0, 1024, 8), 8, 1024, 1, True, st)
            def st(flat):
                return [(s_m[:, :, 0:512].rearrange("m p c -> p m c"), flat.rearrange("p (m x) -> p m x", m=8), "s_m_upa")]
            block(wsrc(w_upa, 0, 1024, 4), 4, 1024, None, True, st)
            def st(flat):
                return [(s_m[:, :, 512:768].rearrange("m p c -> p m c"), flat.rearrange("p (m x) -> p m x", m=8), "s_m_upb")]
            block(wsrc(w_upb, 0, 1024, 2), 2, 1024, None, True, st)
            def st(flat):
                v = flat.rearrange("p (k c) -> p k c", k=8)
                return [(s_o[ch], v[:, :, ch * 512:(ch + 1) * 512], "s_o%d" % ch) for ch in range(2)]
            block(wsrc(w_o, 0, 1024, 8), 8, 1024, None, False, st)
            S.barrier()

        ms = ExitStack()
        es.enter_context(ms)
        xt = sb("xt", [128, 5, D], F32, ms)
        xmap = [0, 1, 2, 3]
        hb = sb("hb", [128, D], BF16, ms)
        sq = sb("sq", [128, 256], F32, ms)
        hT = sb("hT", [128, 8, 512], BF16, ms)
        big = sb("big", [128, 11, 512], BF16, ms)
        wd = sb("wd", [128, 11, D], BF16, ms)
        ring = [sb("ring%d" % i, [128, 4096], BF16, ms) for i in range(4)]
        sg = [sb("sg%d" % i, [128, 512], F32, ms) for i in range(2)]
        hs = ExitStack()
        stg = [sb("stg%d" % i, [128, 256], F32, ms) for i in range(NSTG)]
        qn = sb("qn", [128, 256], F32, ms)
        qb16 = sb("qb16", [128, 256], BF16, ms)
        pT = [sb("pT%d" % i, [128, 512], BF16, ms) for i in range(2)]
        oacc = sb("oacc", [128, 12, 65], F32, ms)
        otmp = sb("otmp", [128, 4, 65], F32, ms)
        onrm = sb("onrm", [128, 768], BF16, ms)
        oT = sb("oT", [128, 6, 512], BF16, ms)
        rden = sb("rden", [128, 12], F32, ms)
        hb2 = [hb, sb("hb1", [128, D], BF16, ms)]
        sg.append(sb("sg2", [128, 512], F32, ms))
        pT.append(sb("pT2", [128, 512], BF16, ms))
        sqb = [sb("sqb%d" % i, [128, 512], F32, hs) for i in range(3)]
        qbb = [sb("qbb%d" % i, [128, 512], BF16, hs) for i in range(3)]
        KTA = sb("KTA", [128, 1, SEQ], BF16, hs)
        VA = sb("VA", [128, 16, 2, 65], BF16, hs)
        KTB = sb("KTB", [128, 6, SEQ], BF16, hs)
        VB = sb("VB", [128, 16, 3, 4, 65], BF16, hs)
        zs_box = [None]

        S.op("dve", lambda: nc.vector.memset(VA[:].rearrange("p a b c -> p (a b c)"), 1.0), writes=["VA"])
        S.op("dve", lambda: nc.vector.memset(VB[:].rearrange("p a b c d -> p (a b c d)"), 1.0), writes=["VB"])
        for g_ in (1, 2):
            S.op("dve", lambda g_=g_: nc.vector.tensor_copy(out=VB[:, :, g_, :, HD:HD + 1], in_=abt[:, :, g_, :, 1:2]),
                 reads=["abt"], writes=["VB"])

        ring_i = [0]
        st_i = [0]
        bank_i = [0]
        tb_i = [0]
        sg_i = [0]
        pt_i = [0]

        def next_ring():
            i = ring_i[0] % 4
            ring_i[0] += 1
            return i

        def next_bank(lo=0, hi=4):
            i = lo + bank_i[0] % (hi - lo)
            bank_i[0] += 1
            return i

        def next_tb():
            i = tb_i[0] % 2
            tb_i[0] += 1
            return i

        ew_i = [0]

        def ew():
            ew_i[0] += 1
            return "dve" if ew_i[0] % 2 else "act"

        def copy_op(e, out, in_, reads, writes):
            if e == "act":
                S.op("act", lambda: nc.scalar.copy(out=out, in_=in_), reads=reads, writes=writes)
            elif e == "dve":
                S.op("dve", lambda: nc.vector.tensor_copy(out=out, in_=in_), reads=reads, writes=writes)
            else:
                S.op("pool", lambda: nc.gpsimd.tensor_copy(out=out, in_=in_), reads=reads, writes=writes)

        def norm_T(P, nsub, NT):
            for s in range(nsub):
                c = 32 + 3 * s
                S.op("act", lambda: nc.scalar.activation(out=hb2[s % 2][0:P, :], in_=xt[0:P, xmap[s], :], func=AF.Square,
                                                         accum_out=small[0:P, c:c + 1]),
                     reads=["xt%d" % xmap[s]], writes=["hb%d" % (s % 2), "nst%d" % s])
                S.op("pool", lambda: nc.gpsimd.tensor_scalar(out=small[0:P, c + 1:c + 2], in0=small[0:P, c:c + 1], scalar1=1.0 / D,
                                                             scalar2=EPS, op0=ALU.mult, op1=ALU.add),
                     reads=["nst%d" % s], writes=["nst%d" % s])
                S.op("pool", lambda: nc.gpsimd.tensor_tensor(out=small[0:P, c + 2:c + 3], in0=small[0:P, c + 1:c + 2],
                                                             in1=mhalf[0:P, 0:1], op=ALU.pow),
                     reads=["nst%d" % s, "mhalf"], writes=["nst%d" % s])
                if s >= 1:
                    norm_tail(P, s - 1)
            norm_tail(P, nsub - 1)

        def norm_tail(P, s):
            c = 32 + 3 * s
            hbuf = hb2[s % 2]
            S.op("act", lambda: nc.scalar.activation(out=hbuf[0:P, :], in_=xt[0:P, xmap[s], :], func=AF.Copy,
                                                     scale=small[0:P, c + 2:c + 3]),
                 reads=["xt%d" % xmap[s], "nst%d" % s], writes=["hb%d" % (s % 2)])
            t = next_tb()
            for kc in range(8):
                S.op("pe", lambda kc=kc: nc.tensor.transpose(out=psT[t][:, kc * 128:kc * 128 + P],
                                                             in_=hbuf[0:P, kc * 128:(kc + 1) * 128],
                                                             identity=idb[0:P, 0:P]),
                     reads=["hb%d" % (s % 2), "idb"], writes=[TB[t]], signal=(kc == 7))
            copy_op(ew(), hT[:, :, s * 128:s * 128 + P],
                    psT[t][:].rearrange("p (k c) -> p k c", k=8)[:, :, 0:P], [TB[t]], [TB[t], "hT"])

        def ffn(l, P, nsub, NT, on_sub_done=None):
            plan_l = []
            loads_idx = {}
            for half_ in range(2):
                for j_ in range(11):
                    f_ = half_ * 11 + j_
                    if f_ % 2 == 0 or j_ == 0:
                        loads_idx[(half_, f_ // 2)] = len(plan_l)
                        plan_l.append(f_ // 2)
            load_slot = {}
            emitted = [0]

            def ensure(k):
                while emitted[0] <= min(k, len(plan_l) - 1):
                    i_ = emitted[0]
                    r_ = next_ring()
                    S.dma("sp", "ring%d" % r_, ring[r_][:], s_gu[l][plan_l[i_]].rearrange("p f g k c -> p (f g k c)"),
                          reads=["s_gu%d_%d_%d" % (l, plan_l[i_], a_) for a_ in range(2)], writes=["ring%d" % r_])
                    load_slot[i_] = r_
                    emitted[0] += 1

            if LOOK > 0:
                ensure(1)
            norm_T(P, nsub, NT)
            for half in range(2):
                for j in range(11):
                    f = half * 11 + j
                    S.dma("sp", "wd%d" % j, wd[:, j, :], s_d[l][f], reads=["s_d%d_%d" % (l, (f // 8) * 8)], writes=["wd%d" % j])
                for j in range(11):
                    f = half * 11 + j
                    fp, fi = f // 2, f % 2
                    if fi == 0 or j == 0:
                        li = loads_idx[(half, fp)]
                        ensure(li + LOOK)
                        r = load_slot[li]
                        rv = ring[r][:].rearrange("p (f g k c) -> p f g k c", f=2, g=2, k=8)
                    bg, bu = next_bank(0, 2), 2 + next_bank(0, 2)
                    for gu, b in [(0, bg), (1, bu)]:
                        for kc in range(8):
                            S.op("pe", lambda gu=gu, b=b, kc=kc: nc.tensor.matmul(
                                psB[b][:, 0:NT], lhsT=rv[:, fi, gu, kc, :], rhs=hT[:, kc, 0:NT],
                                start=(kc == 0), stop=(kc == 7)),
                                reads=["ring%d" % r, "hT"], writes=[PB[b]], signal=(kc == 7))
                    si = sg_i[0] % 2
                    sg_i[0] += 1
                    S.op("act", lambda: nc.scalar.activation(out=sg[si][:, 0:NT], in_=psB[bg][:, 0:NT], func=AF.Silu),
                         reads=[PB[bg]], writes=[PB[bg], "sg%d" % si])
                    S.op("dve", lambda: nc.vector.tensor_tensor(out=big[:, j, 0:NT], in0=sg[si][:, 0:NT],
                                                                in1=psB[bu][:, 0:NT], op=ALU.mult),
                         reads=["sg%d" % si, PB[bu]], writes=[PB[bu], "big%d" % j])
                for s in range(nsub):
                    for ch in range(2):
                        b = 4 + ch
                        for j in range(11):
                            S.op("pe", lambda j=j, b=b, ch=ch: nc.tensor.matmul(
                                psB[b][0:P, :], lhsT=big[:, j, s * 128:s * 128 + P], rhs=wd[:, j, ch * 512:(ch + 1) * 512],
                                start=(j == 0), stop=(j == 10)),
                                reads=["big%d" % j, "wd%d" % j], writes=[PB[b]], signal=(j == 10))
                        S.op("dve", lambda b=b, ch=ch: nc.vector.scalar_tensor_tensor(
                            out=xt[0:P, xmap[s], ch * 512:(ch + 1) * 512], in0=psB[b][0:P, :], scalar=0.5,
                            in1=xt[0:P, xmap[s], ch * 512:(ch + 1) * 512], op0=ALU.mult, op1=ALU.add),
                            reads=[PB[b], "xt%d" % xmap[s]], writes=[PB[b], "xt%d" % xmap[s]])
                        if half == 1 and ch == 1 and on_sub_done is not None:
                            on_sub_done(s)

        def qk_norm(P, bank, c0, nh, gain_tab):
            W = nh * HD
            S.op("act", lambda: nc.scalar.activation(out=sq[0:P, 0:W], in_=psB[bank][0:P, c0:c0 + W], func=AF.Square),
                 reads=[PB[bank]], writes=[PB[bank], "sq"])
            S.op("dve", lambda: nc.vector.tensor_reduce(out=small[0:P, 8:8 + nh],
                                                        in_=sq[0:P, 0:W].rearrange("p (h d) -> p h d", h=nh),
                                                        axis=AX.X, op=ALU.add), reads=["sq"], writes=["small"])
            S.op("pool", lambda: nc.gpsimd.tensor_scalar(out=small[0:P, 16:16 + nh], in0=small[0:P, 8:8 + nh],
                                                         scalar1=1.0 / HD, scalar2=EPS, op0=ALU.mult, op1=ALU.add),
                 reads=["small"], writes=["small"])
            S.op("pool", lambda: nc.gpsimd.tensor_tensor(out=small[0:P, 24:24 + nh], in0=small[0:P, 16:16 + nh],
                                                         in1=mhalf[0:P, 0:nh], op=ALU.pow),
                 reads=["small", "mhalf"], writes=["small"])
            S.op("dve", lambda: nc.vector.tensor_tensor(
                out=qn[0:P, 0:W].rearrange("p (h d) -> p h d", h=nh),
                in0=psB[bank][0:P, c0:c0 + W].rearrange("p (h d) -> p h d", h=nh),
                in1=small[0:P, 24:24 + nh].unsqueeze(2).broadcast_to([P, nh, HD]), op=ALU.mult),
                reads=[PB[bank], "small"], writes=[PB[bank], "qn"])
            S.op("pool", lambda: nc.gpsimd.tensor_tensor(
                out=qn[0:P, 0:W].rearrange("p (h d) -> p h d", h=nh),
                in0=qn[0:P, 0:W].rearrange("p (h d) -> p h d", h=nh),
                in1=gain_tab[0:P, :].unsqueeze(1).broadcast_to([P, nh, HD]), op=ALU.mult),
                reads=["qn"], writes=["qn"])

        def transpose_to(P, src_tok, ncols, dsts):
            t = next_tb()
            n = ncols // 128
            for c in range(n):
                S.op("pe", lambda c=c: nc.tensor.transpose(out=psT[t][:, c * 128:c * 128 + P],
                                                           in_=src_tok[0:P, c * 128:(c + 1) * 128], identity=idb[0:P, 0:P]),
                     reads=["qb16"], writes=[TB[t]], signal=(c == n - 1))
            for c, (dst, res) in enumerate(dsts):
                copy_op(ew(), dst, psT[t][:, c * 128:c * 128 + P], [TB[t]], [TB[t], res])

        def kv_out(seq, blk, grp, kv, st_idx, nh):
            if seq is None:
                return
            if grp == "A":
                if blk == 15:
                    S.dma("sp", "stq%d" % st_idx, o_ap[seq, :, kv, :, :].rearrange("t h d -> t (h d)"),
                          stg[st_idx][:, 0:nh * HD], reads=["stg%d" % st_idx])
                return
            g = grp
            nb = WINB[g]
            if blk >= 16 - nb:
                t0 = (blk - (16 - nb)) * 128
                S.dma("sp", "stq%d" % st_idx, o_bp[g][seq, t0:t0 + 128, kv, :, :].rearrange("t h d -> t (h d)"),
                      stg[st_idx][:, 0:nh * HD], reads=["stg%d" % st_idx])

        def next_stg():
            i = st_i[0] % NSTG
            st_i[0] += 1
            return i

        def project(P, nsub, NT, seq, blk0):
            for nt in range(6):
                r = next_ring()
                S.dma("sp", "ring%d" % r, ring[r][:], s_in[nt].rearrange("p k c -> p (k c)"),
                      reads=["s_in%d" % nt], writes=["ring%d" % r])
                rv = ring[r][:].rearrange("p (k c) -> p k c", k=8)
                for s in range(nsub):
                    blk = blk0 + s
                    tok = slice(s * 128, s * 128 + P)
                    b = next_bank(0, 4)
                    for kc in range(8):
                        S.op("pe", lambda kc=kc: nc.tensor.matmul(psB[b][0:P, :], lhsT=hT[:, kc, tok], rhs=rv[:, kc, :],
                                                                  start=(kc == 0), stop=(kc == 7)),
                             reads=["hT", "ring%d" % r], writes=[PB[b]], signal=(kc == 7))
                    for uu in range(2):
                        u = nt * 2 + uu
                        c0 = uu * 256
                        if seq is None:
                            zs = zs_box[0]
                            if u in (0, 1, 3, 4, 5, 6, 7, 8):
                                qk_norm(P, b, c0, 4, gq_a if u < 2 else (gq_b if u < 6 else gk_b))
                                S.op("act", lambda: nc.scalar.copy(out=zs[0:P, u * 256:(u + 1) * 256], in_=qn[0:P, :]),
                                     reads=["qn"], writes=["zs"])
                            elif u == 2:
                                qk_norm(P, b, c0, 2, gk_a)
                                S.op("act", lambda: nc.scalar.copy(out=zs[0:P, 512:640], in_=qn[0:P, 0:128]),
                                     reads=["qn"], writes=["zs"])
                                S.op("dve", lambda: nc.vector.tensor_copy(out=zs[0:P, 640:768], in_=psB[b][0:P, c0 + 128:c0 + 256]),
                                     reads=[PB[b]], writes=[PB[b], "zs"])
                            else:
                                S.op("dve", lambda: nc.vector.tensor_copy(out=zs[0:P, u * 256:(u + 1) * 256], in_=psB[b][0:P, c0:c0 + 256]),
                                     reads=[PB[b]], writes=[PB[b], "zs"])
                            continue
                        if u in (0, 1):
                            qk_norm(P, b, c0, 4, gq_a)
                            S.op("act", lambda: nc.scalar.copy(out=qb16[0:P, :], in_=qn[0:P, :]), reads=["qn"], writes=["qb16"])
                            transpose_to(P, qb16, 256, [(big[:, 2 * u + c, tok], "big%d" % (2 * u + c)) for c in range(2)])
                        elif u == 2:
                            qk_norm(P, b, c0, 2, gk_a)
                            si = next_stg()
                            S.op("act", lambda: nc.scalar.copy(out=stg[si][0:P, 0:128], in_=qn[0:P, 0:128]),
                                 reads=["qn"], writes=["stg%d" % si])
                            kv_out(seq, blk, "A", 0, si, 2)
                            S.op("dve", lambda: nc.vector.tensor_copy(
                                out=qb16[0:P, :].rearrange("p (h r d) -> p h r d", h=2, r=2),
                                in_=qn[0:P, 0:128].rearrange("p (h d) -> p h d", h=2).unsqueeze(2).broadcast_to([P, 2, 2, HD])),
                                reads=["qn"], writes=["qb16"])
                            if seq is not None:
                                transpose_to(P, qb16, 256, [(KTA[:, c, blk * 128:blk * 128 + P], "KTA") for c in range(2)])
                            else:
                                transpose_to(P, qb16, 256, [(KTA[:, c, 0:P], "KTA") for c in range(2)])
                            si = next_stg()
                            S.op("act", lambda: nc.scalar.copy(out=stg[si][0:P, 0:128], in_=psB[b][0:P, c0 + 128:c0 + 256]),
                                 reads=[PB[b]], writes=[PB[b], "stg%d" % si])
                            kv_out(seq, blk, "A", 1, si, 2)
                            S.op("dve", lambda: nc.vector.tensor_copy(
                                out=VA[0:P, blk if seq is not None else 0, :, 0:HD],
                                in_=stg[si][0:P, 0:128].rearrange("p (h d) -> p h d", h=2)),
                                reads=["stg%d" % si], writes=["VA"])
                        elif u in (3, 4, 5):
                            g = u - 3
                            qk_norm(P, b, c0, 4, gq_b)
                            S.op("act", lambda: nc.scalar.copy(out=qb16[0:P, :], in_=qn[0:P, :]), reads=["qn"], writes=["qb16"])
                            transpose_to(P, qb16, 256, [(big[:, 4 + 2 * g + c, tok], "big%d" % (4 + 2 * g + c)) for c in range(2)])
                        elif u in (6, 7, 8):
                            g = u - 6
                            qk_norm(P, b, c0, 4, gk_b)
                            si = next_stg()
                            S.op("act", lambda: nc.scalar.copy(out=stg[si][0:P, :], in_=qn[0:P, :]), reads=["qn"], writes=["stg%d" % si])
                            kv_out(seq, blk, g, 0, si, 4)
                            S.op("dve", lambda: nc.vector.tensor_copy(out=qb16[0:P, :], in_=qn[0:P, :]), reads=["qn"], writes=["qb16"])
                            kcol = slice(blk * 128, blk * 128 + P) if seq is not None else slice(0, P)
                            transpose_to(P, qb16, 256, [(KTB[:, 2 * g + c, kcol], "KTB") for c in range(2)])
                        else:
                            g = u - 9
                            si = next_stg()
                            S.op("act", lambda: nc.scalar.copy(out=stg[si][0:P, :], in_=psB[b][0:P, c0:c0 + 256]),
                                 reads=[PB[b]], writes=[PB[b], "stg%d" % si])
                            kv_out(seq, blk, g, 1, si, 4)
                            vb = blk if seq is not None else 0
                            if g == 0 or seq is None:
                                S.op("dve", lambda: nc.vector.tensor_copy(
                                    out=VB[0:P, vb, g, :, 0:HD], in_=stg[si][0:P, :].rearrange("p (h d) -> p h d", h=4)),
                                    reads=["stg%d" % si], writes=["VB"])
                            else:
                                S.op("dve", lambda: nc.vector.tensor_tensor(
                                    out=VB[0:P, vb, g, :, 0:HD], in0=stg[si][0:P, :].rearrange("p (h d) -> p h d", h=4),
                                    in1=abt[0:P, vb, g, :, 1:2].broadcast_to([P, 4, HD]), op=ALU.mult),
                                    reads=["stg%d" % si, "abt"], writes=["VB"])
                                S.op("pool", lambda: nc.gpsimd.tensor_copy(out=VB[0:P, vb, g, :, HD:HD + 1],
                                                                           in_=abt[0:P, vb, g, :, 1:2]),
                                     reads=["abt"], writes=["VB"])

        def project_prompt(seq, blk0):
            P = 128
            tails = []
            gidx = [0]

            def emit_group(nt, s, r, rv):
                blk = blk0 + s
                tok = slice(s * 128, (s + 1) * 128)
                kcol = slice(blk * 128, (blk + 1) * 128)
                b = next_bank(0, 4)
                par = gidx[0] % 3
                gidx[0] += 1
                for kc in range(8):
                    S.op("pe", lambda kc=kc: nc.tensor.matmul(psB[b][:, :], lhsT=hT[:, kc, tok], rhs=rv[:, kc, :],
                                                              start=(kc == 0), stop=(kc == 7)),
                         reads=["hT", "ring%d" % r], writes=[PB[b]], signal=(kc == 7))
                ps3 = psB[b][:, :].rearrange("p (h d) -> p h d", h=8)
                st = "pst%d" % par
                c = 64 + par * 8
                if nt < 5:
                    S.op("act", lambda: nc.scalar.activation(out=sqb[par][:, :], in_=psB[b][:, :], func=AF.Square),
                         reads=[PB[b]], writes=[PB[b], "sqb%d" % par])
                    S.op("dve", lambda: nc.vector.tensor_reduce(out=small[:, c:c + 8],
                                                                in_=sqb[par][:, :].rearrange("p (h d) -> p h d", h=8),
                                                                axis=AX.X, op=ALU.add), reads=["sqb%d" % par], writes=[st])
                    S.op("pool", lambda: nc.gpsimd.tensor_scalar(out=small[:, c:c + 8], in0=small[:, c:c + 8],
                                                                 scalar1=1.0 / HD, scalar2=EPS, op0=ALU.mult, op1=ALU.add),
                         reads=[st], writes=[st])
                    S.op("pool", lambda: nc.gpsimd.tensor_tensor(out=small[:, c:c + 8], in0=small[:, c:c + 8],
                                                                 in1=mhalf[:, 0:8], op=ALU.pow), reads=[st, "mhalf"], writes=[st])
                plan = []

                def stageB():
                  if nt < 5:
                    rs = small[:, c:c + 8].unsqueeze(2).broadcast_to([128, 8, HD])
                    if nt == 0:
                        ov = qbb[par][:, :].rearrange("p (c hf d) -> p hf c d", c=4, hf=2)
                        iv = psB[b][:, :].rearrange("p (hf c d) -> p hf c d", hf=2, c=4)
                        rs4 = small[:, c:c + 8].rearrange("p (hf c) -> p hf c", hf=2).unsqueeze(3).broadcast_to([128, 2, 4, HD])
                        S.op("dve", lambda: nc.vector.tensor_tensor(out=ov, in0=iv, in1=rs4, op=ALU.mult),
                             reads=[PB[b], st], writes=[PB[b], "qbb%d" % par])
                    else:
                        S.op("dve", lambda: nc.vector.tensor_tensor(out=qbb[par][:, :].rearrange("p (h d) -> p h d", h=8),
                                                                    in0=ps3, in1=rs, op=ALU.mult),
                             reads=[PB[b], st], writes=[PB[b], "qbb%d" % par])

                  def k_out(c0, nh, gtab, grp):
                      need = (blk == 15) if grp == "A" else (blk >= 16 - WINB[grp])
                      if not need:
                          return
                      si = next_stg()
                      h0 = c0 // HD
                      S.op("dve", lambda: nc.vector.tensor_tensor(
                          out=stg[si][:, 0:nh * HD].rearrange("p (h d) -> p h d", h=nh), in0=ps3[:, h0:h0 + nh, :],
                          in1=small[:, c + h0:c + h0 + nh].unsqueeze(2).broadcast_to([128, nh, HD]), op=ALU.mult),
                          reads=[PB[b], st], writes=[PB[b], "stg%d" % si])
                      S.op("pool", lambda: nc.gpsimd.tensor_tensor(
                          out=stg[si][:, 0:nh * HD].rearrange("p (h d) -> p h d", h=nh),
                          in0=stg[si][:, 0:nh * HD].rearrange("p (h d) -> p h d", h=nh),
                          in1=gtab[:, :].unsqueeze(1).broadcast_to([128, nh, HD]), op=ALU.mult),
                          reads=["stg%d" % si], writes=["stg%d" % si])
                      kv_out(seq, blk, grp, 0, si, nh)

                  def v_part(c0, nh, grp):
                      need = (blk == 15) if grp == "A" else (blk >= 16 - WINB[grp])
                      src = psB[b][:, c0:c0 + nh * HD]
                      if need:
                          si = next_stg()
                          S.op("act", lambda: nc.scalar.copy(out=stg[si][:, 0:nh * HD], in_=src),
                               reads=[PB[b]], writes=[PB[b], "stg%d" % si])
                          kv_out(seq, blk, grp, 1, si, nh)
                      s3 = src.rearrange("p (h d) -> p h d", h=nh)
                      if grp == "A":
                          copy_op(ew(), VA[:, blk, :, 0:HD], s3, [PB[b]], [PB[b], "VA"])
                      elif grp == 0:
                          copy_op(ew(), VB[:, blk, 0, :, 0:HD], s3, [PB[b]], [PB[b], "VB"])
                      else:
                          S.op("dve", lambda: nc.vector.tensor_tensor(
                              out=VB[:, blk, grp, :, 0:HD], in0=s3, in1=abt[:, blk, grp, :, 1:2].broadcast_to([128, 4, HD]),
                              op=ALU.mult), reads=[PB[b], "abt"], writes=[PB[b], "VB"])

                  if nt == 0:
                      plan.append(([0, 1, 2, 3], big[:, 0:4, tok], ["big0", "big1", "big2", "big3"], 0))
                  elif nt == 1:
                      k_out(0, 2, gk_a, "A")
                      v_part(128, 2, "A")
                      plan.append(([0], KTA[:, 0:1, kcol], ["KTA"], 1))
                      plan.append(([2, 3], big[:, 4:6, tok], ["big4", "big5"], 2))
                  elif nt == 2:
                      plan.append(([0, 1, 2, 3], big[:, 6:10, tok], ["big6", "big7", "big8", "big9"], 2))
                  elif nt == 3:
                      k_out(0, 4, gk_b, 0)
                      k_out(256, 4, gk_b, 1)
                      plan.append(([0, 1, 2, 3], KTB[:, 0:4, kcol], ["KTB"], 3))
                  elif nt == 4:
                      k_out(0, 4, gk_b, 2)
                      v_part(256, 4, 0)
                      plan.append(([0, 1], KTB[:, 4:6, kcol], ["KTB"], 3))
                  else:
                      v_part(0, 4, 1)
                      v_part(256, 4, 2)

                def tail():
                    if not plan:
                        return
                    t = next_tb()
                    allc = [ci for (cs, _, _, _) in plan for ci in cs]
                    for i, ci in enumerate(allc):
                        S.op("pe", lambda ci=ci: nc.tensor.transpose(out=psT[t][:, ci * 128:(ci + 1) * 128],
                                                                     in_=qbb[par][:, ci * 128:(ci + 1) * 128], identity=idb[:, :]),
                             reads=["qbb%d" % par, "idb"], writes=[TB[t]], signal=(i == len(allc) - 1))
                    for (cs, dst, dres, gi) in plan:
                        n = len(cs)
                        src = psT[t][:, cs[0] * 128:(cs[0] + n) * 128].rearrange("p (n c) -> p n c", n=n)
                        if ew() == "act":
                            S.op("act", lambda: nc.scalar.activation(out=dst, in_=src, func=AF.Copy, scale=gcol[:, gi:gi + 1]),
                                 reads=[TB[t], "gcol"], writes=[TB[t]] + dres)
                        else:
                            S.op("dve", lambda: nc.vector.tensor_scalar(out=dst, in0=src, scalar1=gcol[:, gi:gi + 1], scalar2=None,
                                                                        op0=ALU.mult), reads=[TB[t], "gcol"], writes=[TB[t]] + dres)
                return stageB, tail

            ptails = []
            pendB = [None]
            for nt in range(6):
                r = next_ring()
                S.dma("sp", "ring%d" % r, ring[r][:], s_in[nt].rearrange("p k c -> p (k c)"),
                      reads=["s_in%d" % nt], writes=["ring%d" % r])
                rv = ring[r][:].rearrange("p (k c) -> p k c", k=8)
                for s in range(4):
                    sB, tl_ = emit_group(nt, s, r, rv)
                    if pendB[0] is not None:
                        pendB[0]()
                    pendB[0] = sB
                    ptails.append(tl_)
                    if len(ptails) > PDEPTH:
                        ptails.pop(0)()
            pendB[0]()
            while ptails:
                ptails.pop(0)()

        def attention_block(s, blk):
            qcol = slice(s * 128, s * 128 + 128)
            jobs = []
            for h in range(8):
                kvh, par, ch = h // 4, h // 4, h % 4
                pairs = [(kb, TIDX[("A", h, blk - kb)]) for kb in (blk - 1, blk) if kb >= 0]
                jobs.append((h, "A", par, lambda kb: KTA[:, 0, kb * 128:(kb + 1) * 128], big[:, ch, qcol], pairs,
                             lambda kb, kvh=kvh: VA[:, kb, kvh, :], "big%d" % ch))
            for g in range(3):
                for hh in (0, 2, 1, 3):
                    par = hh % 2
                    ch = 2 * g + hh // 2
                    pairs = []
                    for kb in range(max(0, blk - WINB[g]), blk + 1):
                        dl = blk - kb
                        if g <= 1:
                            ti = TIDX[(g, hh, dl)]
                        else:
                            ti = TIDX[(g, hh, "0" if dl == 0 else "m")]
                        pairs.append((kb, ti))
                    jobs.append((8 + hh, g, par, lambda kb, ch=ch: KTB[:, ch, kb * 128:(kb + 1) * 128], big[:, 4 + ch, qcol], pairs,
                                 lambda kb, g=g, hh=hh: VB[:, kb, g, hh, :], "big%d" % (4 + ch)))
            items = []
            for job in jobs:
                npair = len(job[5])
                for pi_ in range(npair):
                    items.append((job, pi_))
            import os as _os
            if _os.environ.get("KATT", "1") == "1":
                batches = []
                cur = []
                for it in items:
                    if cur and (len(cur) == 4 or cur[-1][0][2] != it[0][2]):
                        batches.append(cur)
                        cur = []
                    cur.append(it)
                if cur:
                    batches.append(cur)
            else:
                batches = []
                for job in jobs:
                    its = [(job, pi_) for pi_ in range(len(job[5]))]
                    batches.extend([its[i:i + 4] for i in range(0, len(its), 4)])

            def front(batch):
                n = len(batch)
                b = next_bank(0, 4)
                reads = set()
                for i, (job, pi_) in enumerate(batch):
                    (slot, grp, par, ktf, qap, pairs, vf, qres) = job
                    kb, ti = pairs[pi_]
                    pr = slice(par * 64, par * 64 + 64)
                    S.op("pe", lambda i=i, kb=kb, ktf=ktf, qap=qap, pr=pr: nc.tensor.matmul(
                        psB[b][:, i * 128:(i + 1) * 128], lhsT=ktf(kb)[pr, :], rhs=qap[pr, :], start=True, stop=True),
                        reads=["KTA" if grp == "A" else "KTB", qres], writes=[PB[b]], signal=(i == n - 1))
                si = sg_i[0] % 3
                sg_i[0] += 1
                S.op("act", lambda: nc.scalar.activation(out=sg[si][:, 0:n * 128], in_=psB[b][:, 0:n * 128], func=AF.Exp),
                     reads=[PB[b]], writes=[PB[b], "sg%d" % si])
                pi = pt_i[0] % 3
                pt_i[0] += 1
                e = "dve" if pt_i[0] % 2 else "pool"
                tis = [job[5][pi_][1] for (job, pi_) in batch]
                runs = []
                i = 0
                while i < n:
                    j = i
                    if j + 1 < n and tis[j + 1] == tis[i]:
                        while j + 1 < n and tis[j + 1] == tis[i]:
                            j += 1
                        in1 = dtab[:, tis[i]:tis[i] + 1, :].broadcast_to([128, j + 1 - i, 128])
                    else:
                        while j + 1 < n and tis[j + 1] == tis[j] + 1:
                            j += 1
                        in1 = dtab[:, tis[i]:tis[j] + 1, :]
                    runs.append((i, j + 1, in1))
                    i = j + 1
                for (a, z, in1) in runs:
                    ov = pT[pi][:, a * 128:z * 128].rearrange("p (n c) -> p n c", n=z - a)
                    iv = sg[si][:, a * 128:z * 128].rearrange("p (n c) -> p n c", n=z - a)
                    if e == "dve":
                        S.op("dve", lambda ov=ov, iv=iv, in1=in1: nc.vector.tensor_tensor(out=ov, in0=iv, in1=in1, op=ALU.mult),
                             reads=["sg%d" % si, "dtab"], writes=["pT%d" % pi])
                    else:
                        S.op("pool", lambda ov=ov, iv=iv, in1=in1: nc.gpsimd.tensor_tensor(out=ov, in0=iv, in1=in1, op=ALU.mult),
                             reads=["sg%d" % si, "dtab"], writes=["pT%d" % pi])
                return pi

            def back(batch, pi):
                n = len(batch)
                for i, (job, pi_) in enumerate(batch):
                    (slot, grp, par, ktf, qap, pairs, vf, qres) = job
                    kb, ti = pairs[pi_]
                    npair = len(pairs)
                    vres = "VA" if grp == "A" else "VB"
                    if grp == "A":
                        accb, acol = 4 + slot // 4, (slot % 4) * 65
                    else:
                        accb, acol = 4 + (grp % 2), (slot - 8) * 65
                    last = (pi_ == npair - 1)
                    S.op("pe", lambda i=i, kb=kb, vf=vf, accb=accb, acol=acol, pi_=pi_, last=last: nc.tensor.matmul(
                        psB[accb][:, acol:acol + 65], lhsT=pT[pi][:, i * 128:(i + 1) * 128], rhs=vf(kb),
                        start=(pi_ == 0), stop=last),
                        reads=["pT%d" % pi, vres], writes=[PB[accb]], signal=(i == n - 1 or last))
                    if not last:
                        continue
                    if grp == "A" and slot % 4 == 3:
                        hs_ = slot - 3
                        S.op("dve", lambda hs_=hs_, accb=accb: nc.vector.tensor_copy(
                            out=oacc[:, hs_:hs_ + 4, :], in_=psB[accb][:, 0:260].rearrange("p (h c) -> p h c", h=4)),
                            reads=[PB[accb]], writes=[PB[accb], "oacc"])
                    elif grp != "A" and slot == 11:
                        g = grp
                        pv = psB[accb][:, 0:260].rearrange("p (h c) -> p h c", h=4)
                        if g == 0:
                            S.op("dve", lambda pv=pv: nc.vector.tensor_copy(out=oacc[:, 8:12, :], in_=pv),
                                 reads=[PB[accb]], writes=[PB[accb], "oacc"])
                        else:
                            S.op("dve", lambda pv=pv, g=g: nc.vector.tensor_tensor(
                                out=otmp[:], in0=pv, in1=abt[:, blk, g, :, 0:1].broadcast_to([128, 4, 65]), op=ALU.mult),
                                reads=[PB[accb], "abt"], writes=[PB[accb], "otmp"])
                            S.op("pool", lambda: nc.gpsimd.tensor_tensor(out=oacc[:, 8:12, :], in0=oacc[:, 8:12, :], in1=otmp[:],
                                                                         op=ALU.add), reads=["otmp", "oacc"], writes=["oacc"])

            pend = []
            for batch in batches:
                pi = front(batch)
                pend.append((batch, pi))
                if len(pend) > ADEPTH:
                    back(*pend.pop(0))
            while pend:
                back(*pend.pop(0))

        def finish_o(P, s):
            S.op("dve", lambda: nc.vector.tensor_tensor(out=oacc[0:P, 0:8, 64:65], in0=oacc[0:P, 0:8, 64:65],
                                                        in1=esink[0:P, :].unsqueeze(2), op=ALU.add),
                 reads=["oacc", "esink"], writes=["oacc"])
            S.op("dve", lambda: nc.vector.reciprocal(out=rden[0:P, :].unsqueeze(2), in_=oacc[0:P, :, 64:65]),
                 reads=["oacc"], writes=["rden"])
            S.op("dve", lambda: nc.vector.tensor_tensor(
                out=onrm[0:P, :].rearrange("p (h d) -> p h d", h=12), in0=oacc[0:P, :, 0:HD],
                in1=rden[0:P, :].unsqueeze(2).broadcast_to([P, 12, HD]), op=ALU.mult),
                reads=["oacc", "rden"], writes=["onrm"])
            t = next_tb()
            for c in range(6):
                S.op("pe", lambda c=c: nc.tensor.transpose(out=psT[t][:, c * 128:c * 128 + P],
                                                           in_=onrm[0:P, c * 128:(c + 1) * 128], identity=idb[0:P, 0:P]),
                     reads=["onrm", "idb"], writes=[TB[t]], signal=(c == 5))
            copy_op(ew(), oT[:, :, s * 128:s * 128 + P],
                    psT[t][:, 0:768].rearrange("p (k c) -> p k c", k=6)[:, :, 0:P], [TB[t]], [TB[t], "oT"])

        def merge_out(P, nsub, NT):
            for m in range(8):
                r = next_ring()
                S.dma("sp", "ring%d" % r, ring[r][:, 0:2816], s_m[m], reads=["s_m_ga", "s_m_gb", "s_m_upa", "s_m_upb"], writes=["ring%d" % r])
                rv = ring[r][:, 0:2816].rearrange("p (k c) -> p k c", k=22)
                ga_b, gb_b = (2, 3) if m % 2 == 0 else (4, 5)
                specs = [(ga_b, 6, 8, hT, 0), (gb_b, 14, 8, hT, 0), (0, 0, 4, oT, 0), (1, 4, 2, oT, 4)]
                for (b, k0, nk, src, s0) in specs:
                    for kc in range(nk):
                        S.op("pe", lambda b=b, k0=k0, kc=kc, src=src, s0=s0, nk=nk: nc.tensor.matmul(
                            psB[b][:, 0:NT], lhsT=rv[:, k0 + kc, :], rhs=src[:, s0 + kc, 0:NT],
                            start=(kc == 0), stop=(kc == nk - 1)),
                            reads=["ring%d" % r, "oT" if src is oT else "hT"], writes=[PB[b]], signal=(kc == nk - 1))
                ia, ib = (2 * m) % 3, (2 * m + 1) % 3
                sA, sB = sg[ia], sg[ib]
                S.op("act", lambda: nc.scalar.activation(out=sA[:, 0:NT], in_=psB[ga_b][:, 0:NT], func=AF.Sigmoid),
                     reads=[PB[ga_b]], writes=[PB[ga_b], "sg%d" % ia])
                S.op("act", lambda: nc.scalar.activation(out=sB[:, 0:NT], in_=psB[gb_b][:, 0:NT], func=AF.Sigmoid),
                     reads=[PB[gb_b]], writes=[PB[gb_b], "sg%d" % ib])
                S.op("dve", lambda: nc.vector.tensor_tensor(out=sA[:, 0:NT], in0=sA[:, 0:NT], in1=psB[0][:, 0:NT], op=ALU.mult),
                     reads=["sg%d" % ia, PB[0]], writes=["sg%d" % ia, PB[0]])
                S.op("dve", lambda: nc.vector.tensor_tensor(out=sB[:, 0:NT], in0=sB[:, 0:NT], in1=psB[1][:, 0:NT], op=ALU.mult),
                     reads=["sg%d" % ib, PB[1]], writes=["sg%d" % ib, PB[1]])
                S.op("pool", lambda m=m: nc.gpsimd.tensor_tensor(out=big[:, m, 0:NT], in0=sA[:, 0:NT], in1=sB[:, 0:NT], op=ALU.add),
                     reads=["sg%d" % ia, "sg%d" % ib], writes=["big%d" % m])
            rs = []
            for ch in range(2):
                r = next_ring()
                S.dma("sp", "ring%d" % r, ring[r][:], s_o[ch].rearrange("p k c -> p (k c)"), reads=["s_o%d" % ch], writes=["ring%d" % r])
                rs.append(r)
            for s in range(nsub):
                for ch in range(2):
                    r = rs[ch]
                    rv = ring[r][:].rearrange("p (k c) -> p k c", k=8)
                    b = 4 + ch
                    for kc in range(8):
                        S.op("pe", lambda kc=kc, b=b, rv=rv: nc.tensor.matmul(
                            psB[b][0:P, :], lhsT=big[:, kc, s * 128:s * 128 + P], rhs=rv[:, kc, :], start=(kc == 0), stop=(kc == 7)),
                            reads=["big%d" % kc, "ring%d" % r], writes=[PB[b]], signal=(kc == 7))
                    S.op("dve", lambda b=b, ch=ch: nc.vector.tensor_tensor(
                        out=xt[0:P, xmap[s], ch * 512:(ch + 1) * 512], in0=psB[b][0:P, :], in1=xt[0:P, xmap[s], ch * 512:(ch + 1) * 512], op=ALU.add),
                        reads=[PB[b], "xt%d" % xmap[s]], writes=[PB[b], "xt%d" % xmap[s]])

        import os as _os2
        LOOK = int(_os2.environ.get("KLOOK", "3"))
        ADEPTH = int(_os2.environ.get("KADEPTH", "2"))
        PDEPTH = int(_os2.environ.get("KPDEPTH", "2"))
        copies = []
        if with_sample:
            copies.append((o_as[:, 0:127].rearrange("b t k h d -> b (t k h d)"), c_a[:, 1:128].rearrange("b t k h d -> b (t k h d)")))
            copies.append((o_bs[0][:, 0:127].rearrange("b t k h d -> b (t k h d)"), c_b[0][:, 1:128].rearrange("b t k h d -> b (t k h d)")))
            for bb in range(0, NS, 4):
                copies.append((o_bs[1][bb:bb + 4, 0:511].rearrange("b t k h d -> b (t k h d)"),
                               c_b[1][bb:bb + 4, 1:512].rearrange("b t k h d -> b (t k h d)")))
            for bb in range(NS):
                copies.append((o_bs[2][bb:bb + 1, 0:2047].rearrange("b t k h d -> b (t k h d)"),
                               c_b[2][bb:bb + 1, 1:2048].rearrange("b t k h d -> b (t k h d)")))

        def issue_copies(n):
            for _ in range(n):
                if copies:
                    o, i = copies.pop(0)
                    S.dma("sp", "ccopy", o, i)

        import os
        STAGE = int(os.environ.get("KSTAGE", "9"))
        NTILES = int(os.environ.get("KTILES", "8"))
        tcount = 0
        for seq in range(NSEQ):
            for tl in range(4):
                if STAGE < 1 or tcount >= NTILES:
                    continue
                tcount += 1
                blk0 = tl * 4
                xmap[:] = [(4 * (tcount - 1) + s_) % 5 for s_ in range(4)]
                xfree = (4 * (tcount - 1) + 4) % 5
                if tcount == 1:
                    for s_ in range(4):
                        S.dma("sp", "xi%d" % s_, xt[:, s_, :], x_p[seq, tl * 512 + s_ * 128:tl * 512 + (s_ + 1) * 128, :], writes=["xt%d" % s_])
                ffn(0, 128, 4, 512)
                if STAGE >= 2:
                    norm_T(128, 4, 512)
                    project_prompt(seq, blk0)
                if STAGE >= 3:
                    for s in range(4):
                        issue_copies(1 if s < 3 else 0)
                        attention_block(s, blk0 + s)
                        finish_o(128, s)
                if STAGE >= 4:
                    merge_out(128, 4, 512)
                nxt = None
                if tcount < NTILES and not (seq == NSEQ - 1 and tl == 3):
                    nxt = (seq, tl + 1) if tl < 3 else (seq + 1, 0)

                def sub_done(s_, seq=seq, tl=tl, nxt=nxt):
                    sl_ = xmap[s_]
                    S.dma("pool", "yout%d" % sl_, y_p[seq, tl * 512 + s_ * 128:tl * 512 + (s_ + 1) * 128, :], xt[:, sl_, :],
                          reads=["xt%d" % sl_])
                    if nxt is not None and s_ < 3:
                        S.dma("pool", "xt%d" % sl_, xt[:, sl_, :],
                              x_p[nxt[0], nxt[1] * 512 + (s_ + 1) * 128:nxt[1] * 512 + (s_ + 2) * 128, :], writes=["xt%d" % sl_])

                if nxt is not None:
                    S.dma("pool", "xt%d" % xfree, xt[:, xfree, :], x_p[nxt[0], nxt[1] * 512:nxt[1] * 512 + 128, :], writes=["xt%d" % xfree])
                if STAGE >= 5:
                    ffn(1, 128, 4, 512, sub_done)
                else:
                    for s_ in range(4):
                        sub_done(s_)

        issue_copies(len(copies))
        if not with_sample:
            hs.close()
        if with_sample:
            S.barrier()
            hs.close()
            ss_ = ExitStack()
            es.enter_context(ss_)
            zs = sb("zs", [128, 3072], F32, ss_)
            zs_box[0] = zs
            KVb = sb("KVb", [128, 17 * 512], F32, ss_)
            qrep = sb("qrep", [128, 512], F32, ss_)
            part = sb("part", [128, 8 * 65], F32, ss_)
            ssc2 = sb("ssc2", [128, 17 * 8], F32, ss_)
            pexp = sb("pexp", [128, 17 * 8], F32, ss_)
            sal = sb("sal", [128, 4, 17 * 8], F32, ss_)
            rept = sb("rept", [128, 128], F32, ss_)
            sel = sb("sel", [128, 16], F32, ss_)
            S.dma("sp", "rept", rept[0:16, :], c_rep, writes=["rept"])
            S.dma("sp", "sel", sel[:], c_sel, writes=["sel"])
            S.dma("sp", "sal", sal[:], c_sal, writes=["sal"])
            xmap[:] = [0, 1, 2, 3]
            S.dma("sp", "xi0", xt[0:NS, 0, :], x_s, writes=["xt0"])
            ffn(0, NS, 1, NS)
            norm_T(NS, 1, NS)
            project(NS, 1, NS, None, 0)
            S.dma("sp", "nrow", o_as[:, 127, 0, :, :].rearrange("b h d -> b (h d)"), zs[0:NS, 512:640], reads=["zs"])
            S.dma("sp", "nrow", o_as[:, 127, 1, :, :].rearrange("b h d -> b (h d)"), zs[0:NS, 640:768], reads=["zs"])
            for g in range(3):
                W = [128, 512, 2048][g]
                S.dma("sp", "nrow", o_bs[g][:, W - 1, 0, :, :].rearrange("b h d -> b (h d)"),
                      zs[0:NS, 1536 + g * 256:1536 + (g + 1) * 256], reads=["zs"])
                S.dma("sp", "nrow", o_bs[g][:, W - 1, 1, :, :].rearrange("b h d -> b (h d)"),
                      zs[0:NS, 2304 + g * 256:2304 + (g + 1) * 256], reads=["zs"])

            def sgroup(G, cache, dil, Hkv, Hq, qc0, kc0, vc0):
                RL = 2 * Hkv * HD
                KW = Hkv * HD
                kvv = KVb[:, 0:17 * RL].rearrange("p (j r) -> p j r", r=RL)
                S.op("pool", lambda: nc.gpsimd.memset(kvv[:, 16, :], 0.0), writes=["KVb"])
                src = cache.rearrange("b (jc jj r) k h d -> (b jc) jj r (k h d)", jc=8, jj=16, r=dil)[:, :, 0, :]
                S.dma("sp", "KVb", kvv[:, 0:16, :], src, writes=["KVb"], disjoint=True)
                k7 = KVb[7:128:8, :]
                S.dma("sp", "KVb", k7[:, 16 * RL:16 * RL + KW], zs[0:NS, kc0:kc0 + KW], reads=["zs"], writes=["KVb"], disjoint=True)
                S.dma("sp", "KVb", k7[:, 16 * RL + KW:17 * RL], zs[0:NS, vc0:vc0 + KW], reads=["zs"], writes=["KVb"], disjoint=True)
                W = Hq * HD
                S.op("pe", lambda: nc.tensor.matmul(psB[0][:, 0:W], lhsT=rept[0:NS, :], rhs=zs[0:NS, qc0:qc0 + W], start=True, stop=True),
                     reads=["rept", "zs"], writes=[PB[0]])
                S.op("act", lambda: nc.scalar.copy(out=qrep[:, 0:W], in_=psB[0][:, 0:W]), reads=[PB[0]], writes=[PB[0], "qrep"])
                sc3 = ssc2[:, 0:17 * Hq].rearrange("p (j h) -> p j h", h=Hq)
                pe3 = pexp[:, 0:17 * Hq].rearrange("p (j h) -> p j h", h=Hq)
                part3 = part[:, 0:Hq * 65].rearrange("p (h c) -> p h c", c=65)
                if G != 0:
                    K4 = kvv[:, :, 0:KW]
                    S.op("dve", lambda: nc.vector.tensor_tensor(out=K4, in0=K4, in1=qrep[:, 0:KW].unsqueeze(1).broadcast_to([128, 17, KW]),
                                                                op=ALU.mult), reads=["KVb", "qrep"], writes=["KVb"])
                    S.op("dve", lambda: nc.vector.tensor_reduce(out=sc3, in_=K4.rearrange("p j (h d) -> p j h d", h=Hkv), axis=AX.X, op=ALU.add),
                         reads=["KVb"], writes=["ssc2"])
                else:
                    prodA = KVb[:, 17 * RL:17 * RL + 17 * 256].rearrange("p (j h d) -> p j h d", h=4, d=HD)
                    for kvh in range(2):
                        S.op("dve", lambda kvh=kvh: nc.vector.tensor_tensor(
                            out=prodA, in0=kvv[:, :, kvh * HD:(kvh + 1) * HD].unsqueeze(2).broadcast_to([128, 17, 4, HD]),
                            in1=qrep[:, kvh * 256:(kvh + 1) * 256].rearrange("p (h d) -> p h d", h=4).unsqueeze(1).broadcast_to([128, 17, 4, HD]),
                            op=ALU.mult), reads=["KVb", "qrep"], writes=["KVb"])
                        S.op("dve", lambda kvh=kvh: nc.vector.tensor_reduce(out=sc3[:, :, kvh * 4:(kvh + 1) * 4], in_=prodA, axis=AX.X, op=ALU.add),
                             reads=["KVb"], writes=["ssc2"])
                S.op("dve", lambda: nc.vector.tensor_tensor(out=ssc2[:, 0:17 * Hq], in0=ssc2[:, 0:17 * Hq], in1=sal[:, G, 0:17 * Hq], op=ALU.add),
                     reads=["ssc2", "sal"], writes=["ssc2"])
                S.op("act", lambda: nc.scalar.activation(out=pexp[:, 0:17 * Hq], in_=ssc2[:, 0:17 * Hq], func=AF.Exp),
                     reads=["ssc2"], writes=["pexp"])
                S.op("dve", lambda: nc.vector.tensor_reduce(out=part3[:, :, 64], in_=pe3.rearrange("p j h -> p h j"), axis=AX.X, op=ALU.add),
                     reads=["pexp"], writes=["part"])
                if G != 0:
                    V4 = kvv[:, :, KW:2 * KW].rearrange("p j (h d) -> p j h d", h=Hkv)
                    S.op("dve", lambda: nc.vector.tensor_tensor(out=V4, in0=V4, in1=pe3.unsqueeze(3).broadcast_to([128, 17, Hq, HD]), op=ALU.mult),
                         reads=["KVb", "pexp"], writes=["KVb"])
                    S.op("dve", lambda: nc.vector.tensor_reduce(out=part3[:, :, 0:HD], in_=V4.rearrange("p j h d -> p h d j"), axis=AX.X, op=ALU.add),
                         reads=["KVb"], writes=["part"])
                else:
                    for kvh in range(2):
                        S.op("dve", lambda kvh=kvh: nc.vector.tensor_tensor(
                            out=prodA, in0=kvv[:, :, KW + kvh * HD:KW + (kvh + 1) * HD].unsqueeze(2).broadcast_to([128, 17, 4, HD]),
                            in1=pe3[:, :, kvh * 4:(kvh + 1) * 4].unsqueeze(3).broadcast_to([128, 17, 4, HD]), op=ALU.mult),
                            reads=["KVb", "pexp"], writes=["KVb"])
                        S.op("dve", lambda kvh=kvh: nc.vector.tensor_reduce(out=part3[:, kvh * 4:(kvh + 1) * 4, 0:HD],
                                                                            in_=prodA.rearrange("p j h d -> p h d j"), axis=AX.X, op=ALU.add),
                             reads=["KVb"], writes=["part"])
                if G == 0:
                    for hf in range(2):
                        S.op("pe", lambda hf=hf: nc.tensor.matmul(psB[4 + hf][0:NS, 0:260], lhsT=sel[:, 0:NS], rhs=part[:, hf * 260:(hf + 1) * 260],
                                                                  start=True, stop=True), reads=["sel", "part"], writes=[PB[4 + hf]])
                        S.op("dve", lambda hf=hf: nc.vector.tensor_copy(out=oacc[0:NS, hf * 4:(hf + 1) * 4, :],
                                                                        in_=psB[4 + hf][0:NS, 0:260].rearrange("p (h c) -> p h c", c=65)),
                             reads=[PB[4 + hf]], writes=[PB[4 + hf], "oacc"])
                else:
                    S.op("pe", lambda: nc.tensor.matmul(psB[3][0:NS, 0:260], lhsT=sel[:, 0:NS], rhs=part[:, 0:260],
                                                        start=(G == 1), stop=(G == 3)), reads=["sel", "part"], writes=[PB[3]])
                    if G == 3:
                        S.op("dve", lambda: nc.vector.tensor_copy(out=oacc[0:NS, 8:12, :],
                                                                  in_=psB[3][0:NS, 0:260].rearrange("p (h c) -> p h c", c=65)),
                             reads=[PB[3]], writes=[PB[3], "oacc"])

            sgroup(0, c_a, 1, 2, 8, 0, 512, 640)
            for g in range(3):
                sgroup(1 + g, c_b[g], DILS[g], 4, 4, 768 + g * 256, 1536 + g * 256, 2304 + g * 256)
            finish_o(NS, 0)
            merge_out(NS, 1, NS)
            ffn(1, NS, 1, NS)
            S.dma("pool", "yout0", y_s, xt[0:NS, 0, :], reads=["xt0"])

        S.finish()
        print("ops", S.nops, "waits", S.nwaits, "cnt", S.cnt, "dma sems", len(S.dsem))
    sal_np = np.zeros((128, 4, 17 * 8), np.float64)
    for p in range(128):
        jc = p % 8
        for G in range(4):
            Hq = 8 if G == 0 else 4
            dil = 1 if G == 0 else DILS[G - 1]
            for jj in range(17):
                for h in range(Hq):
                    sl = SLOPES[h] if G == 0 else SLOPES[8 + 4 * (G - 1) + h]
                    if jj < 16:
                        v = -sl * dil * (128 - (jc * 16 + jj))
                    else:
                        v = 0.0 if jc == 7 else -30000.0
                    sal_np[p, G, jj * Hq + h] = v
    rep_np = np.zeros((16, 128), np.float32)
    for p in range(128):
        rep_np[p // 8, p] = 1.0
    consts = {"c_ident": ident_np, "c_dt": dt_np, "c_ab": ab_np.reshape(128, -1), "c_sal": sal_np.astype(np.float32), "c_rep": rep_np,
              "c_sel": np.ascontiguousarray(rep_np.T)}
    return nc, consts


_CACHE = {}


def kernel(**inputs):
    import os
    WS = os.environ.get("KNOSAMPLE", "0") != "1"
    if "prog" not in _CACHE:
        _CACHE["prog"] = build_program(WS)
    nc, consts = _CACHE["prog"]
    f = lambda k: np.ascontiguousarray(np.asarray(inputs[k], dtype=np.float32))
    xp = f("x_prompt"); xs = f("x_sample").reshape(128, D)
    ca = f("cache_a_kv")[0]; cb1 = f("cache_b1_kv")[0]; cb2 = f("cache_b2_kv")[0]; cb3 = f("cache_b3_kv")[0]
    shared = {
        "norm_ffn1": f("norm_ffn1")[0], "norm_mix": f("norm_mix")[0], "norm_ffn2": f("norm_ffn2")[0],
        "w1_gate": f("w1_gate")[0], "w1_up": f("w1_up")[0], "w1_down": f("w1_down")[0],
        "w2_gate": f("w2_gate")[0], "w2_up": f("w2_up")[0], "w2_down": f("w2_down")[0],
        "w_in": f("w_in")[0], "q_norm_a": f("q_norm_a")[0], "k_norm_a": f("k_norm_a")[0],
        "q_norm_b": f("q_norm_b")[0], "k_norm_b": f("k_norm_b")[0], "sinks_a": f("sinks_a")[0].reshape(8),
        "w_up_a": f("w_up_a")[0], "w_up_b": f("w_up_b")[0], "w_o": f("w_o")[0],
    }
    shared.update(consts)
    in_maps = []
    for c in range(NCORES):
        m = dict(shared)
        m["x_prompt"] = xp[c * NSEQ:(c + 1) * NSEQ]
        m["x_sample"] = xs[c * NS:(c + 1) * NS]
        if WS:
            m["cache_a"] = ca[c * NS:(c + 1) * NS]
            m["cache_b1"] = cb1[c * NS:(c + 1) * NS]
            m["cache_b2"] = cb2[c * NS:(c + 1) * NS]
            m["cache_b3"] = cb3[c * NS:(c + 1) * NS]
        in_maps.append(m)
    res = run_bass_kernel_spmd(nc, in_maps, core_ids=list(range(NCORES)))
    R = res.results
    cat = lambda k: np.concatenate([np.asarray(r[k]) for r in R], axis=0)
    y_prompt = cat("y_prompt")
    y_sample = cat("y_sample").reshape(128, 1, D)
    outs = [y_prompt, y_sample]
    for k in ["a_p", "b1_p", "b2_p", "b3_p"] + (["a_s", "b1_s", "b2_s", "b3_s"] if WS else []):
        outs.append(cat(k)[None])
    return tuple(o.astype(np.float32) for o in outs)
```

```python
import numpy as np
from contextlib import ExitStack
import concourse.bass as bass
import concourse.mybir as mybir
from concourse.bass_utils import run_bass_kernel_spmd

F32 = mybir.dt.float32
BF16 = mybir.dt.bfloat16
AF = mybir.ActivationFunctionType
ALU = mybir.AluOpType
AX = mybir.AxisListType

NCORES = 8
D = 1024
DFF = 2816
NF = 22
SEQ = 2048
NSEQ = 2
HD = 64
EPS = 1e-6
NS = 16
NSTG = 4
N_AL = 20
SLOPES = [2.0 ** (-8.0 * i / N_AL) for i in range(1, N_AL + 1)]
DILS = [1, 4, 16]
WINB = [1, 4, 16]


class Sched:
    def __init__(self, nc, es):
        self.nc = nc
        self.es = es
        self.eng = {"pe": nc.tensor, "act": nc.scalar, "dve": nc.vector, "pool": nc.gpsimd, "sp": nc.sync}
        self.sem = {e: es.enter_context(nc.semaphore("s_" + e)) for e in ["pe", "act", "dve", "pool"]}
        self.cnt = {e: 0 for e in self.sem}
        self.pending = {e: [] for e in self.sem}
        self.lastw = {}
        self.readers = {}
        self.seen = {e: {} for e in self.eng}
        self.dsem = {}
        self.nwaits = 0
        self.nops = {e: 0 for e in self.eng}

    def _need(self, reads, writes):
        toks = []
        for r in reads:
            t = self.lastw.get(r)
            if t is not None:
                toks.append(t)
        for w in writes:
            t = self.lastw.get(w)
            if t is not None:
                toks.append(t)
            toks.extend(self.readers.get(w, []))
        return toks

    def _emit_waits(self, e, toks):
        best = {}
        for (own, sem, val) in toks:
            if own == e and (val is None or e == "pe"):
                continue
            if val is None:
                raise RuntimeError("dependency on unsignalled op")
            k = sem.name
            if self.seen[e].get(k, 0) >= val:
                continue
            if k not in best or best[k][1] < val:
                best[k] = (sem, val)
        for k, (sem, val) in best.items():
            self.eng[e].wait_ge(sem, val)
            self.seen[e][k] = val
            self.nwaits += 1

    def _record(self, tok, reads, writes):
        for r in reads:
            lst = self.readers.setdefault(r, [])
            lst[:] = [t for t in lst if t[1].name != tok[1].name]
            lst.append(tok)
        for w in writes:
            self.lastw[w] = tok
            self.readers[w] = []

    def op(self, e, fn, reads=(), writes=(), signal=True):
        toks = self._need(reads, writes)
        self._emit_waits(e, toks)
        inst = fn()
        self.nops[e] += 1
        if signal:
            self.cnt[e] += 1
            inst.then_inc(self.sem[e], 1)
            tok = (e, self.sem[e], self.cnt[e])
            for (rs, ws) in self.pending[e]:
                self._record(tok, rs, ws)
            self.pending[e] = []
            self._record(tok, reads, writes)
        else:
            self.pending[e].append((tuple(reads), tuple(writes)))
            bad = (e, self.sem[e], None)
            for w in writes:
                self.lastw[w] = bad
                self.readers[w] = []
            for r in reads:
                self.readers.setdefault(r, []).append(bad)
        return inst

    def dma(self, q, slot, out, in_, reads=(), writes=(), disjoint=False, **kw):
        if slot not in self.dsem:
            self.dsem[slot] = [self.es.enter_context(self.nc.semaphore("d_" + slot)), 0]
        ds = self.dsem[slot]
        toks = self._need(reads, writes)
        if disjoint:
            toks = [t for t in toks if t[1].name != ds[0].name]
        self._emit_waits(q, toks)
        ds[1] += 16
        self.eng[q].dma_start(out=out, in_=in_, **kw).then_inc(ds[0], 16)
        self.nops[q] += 1
        tok = ("dma", ds[0], ds[1])
        self._record(tok, reads, writes)
        return tok

    def barrier(self):
        for e in ["pe", "act", "dve", "pool", "sp"]:
            for f in ["pe", "act", "dve", "pool"]:
                if f != e and self.cnt[f] > self.seen[e].get(self.sem[f].name, 0):
                    self.eng[e].wait_ge(self.sem[f], self.cnt[f])
                    self.seen[e][self.sem[f].name] = self.cnt[f]
            for slot, (sem, val) in self.dsem.items():
                if val > self.seen[e].get(sem.name, 0):
                    self.eng[e].wait_ge(sem, val)
                    self.seen[e][sem.name] = val

    def finish(self):
        for e in ["sp"]:
            for slot, (sem, val) in self.dsem.items():
                if val > 0:
                    self.eng[e].wait_ge(sem, val)
            for f in ["pe", "act", "dve", "pool"]:
                if self.cnt[f] > 0:
                    self.eng[e].wait_ge(self.sem[f], self.cnt[f])


def host_tables():
    ident = np.eye(128, dtype=np.float32)
    k = np.arange(128)[:, None].astype(np.float64)
    q = np.arange(128)[None, :].astype(np.float64)
    tabs = []
    idx = {}
    for h in range(8):
        s = SLOPES[h]
        idx[("A", h, 1)] = len(tabs); tabs.append(np.where(q <= k, np.exp(-s * (128 + q - k)), 0.0))
        idx[("A", h, 0)] = len(tabs); tabs.append(np.where(q >= k, np.exp(-s * (q - k)), 0.0))
    for g in range(3):
        dil = DILS[g]
        mod = ((q - k) % dil) == 0
        for h in range(4):
            s = SLOPES[8 + g * 4 + h]
            if g == 0:
                idx[(g, h, 1)] = len(tabs); tabs.append(np.where(q <= k, np.exp(-s * (128 + q - k)), 0.0))
                idx[(g, h, 0)] = len(tabs); tabs.append(np.where(q >= k, np.exp(-s * (q - k)), 0.0))
            elif g == 1:
                idx[(g, h, 4)] = len(tabs); tabs.append(np.where((q <= k) & mod, np.exp(-s * (q - k)), 0.0))
                for dl in (3, 2, 1):
                    idx[(g, h, dl)] = len(tabs); tabs.append(np.where(mod, np.exp(-s * (q - k)), 0.0))
                idx[(g, h, 0)] = len(tabs); tabs.append(np.where((q >= k) & mod, np.exp(-s * (q - k)), 0.0))
            else:
                idx[(g, h, "m")] = len(tabs); tabs.append(np.where(mod, np.exp(-s * (q - k)), 0.0))
                idx[(g, h, "0")] = len(tabs); tabs.append(np.where((q >= k) & mod, np.exp(-s * (q - k)), 0.0))
    dt = np.stack(tabs, axis=1).astype(np.float32)
    ab = np.ones((128, 16, 3, 4, 2), np.float64)
    for g in (1, 2):
        for h in range(4):
            s = SLOPES[8 + g * 4 + h]
            for b in range(16):
                ab[:, b, g, h, 0] = np.exp(-s * 128.0 * b)
                ab[:, b, g, h, 1] = np.exp(s * 128.0 * b)
    return ident, dt, idx, ab.astype(np.float32)


def build_program(with_sample=True):
    ident_np, dt_np, TIDX, ab_np = host_tables()
    NTAB = dt_np.shape[1]
    nc = bass.Bass("TRN2", target_bir_lowering=False)

    def din(name, shape, dt=F32):
        return nc.dram_tensor(name, list(shape), dt, kind="ExternalInput").ap()

    def dout(name, shape):
        return nc.dram_tensor(name, list(shape), F32, kind="ExternalOutput").ap()

    def dscr(name, shape, dt=BF16):
        return nc.dram_tensor(name, list(shape), dt, kind="Internal").ap()

    x_p = din("x_prompt", [NSEQ, SEQ, D])
    x_s = din("x_sample", [NS, D])
    if with_sample:
        c_a = din("cache_a", [NS, 128, 2, 2, HD])
        c_b = [din("cache_b1", [NS, 128, 2, 4, HD]), din("cache_b2", [NS, 512, 2, 4, HD]),
               din("cache_b3", [NS, 2048, 2, 4, HD])]
    g_f1 = din("norm_ffn1", [D]); g_mx = din("norm_mix", [D]); g_f2 = din("norm_ffn2", [D])
    w_g = [din("w1_gate", [D, DFF]), din("w2_gate", [D, DFF])]
    w_u = [din("w1_up", [D, DFF]), din("w2_up", [D, DFF])]
    w_d = [din("w1_down", [DFF, D]), din("w2_down", [DFF, D])]
    w_in = din("w_in", [D, 5120])
    qn_a = din("q_norm_a", [HD]); kn_a = din("k_norm_a", [HD]); qn_b = din("q_norm_b", [HD]); kn_b = din("k_norm_b", [HD])
    sinks = din("sinks_a", [8])
    w_upa = din("w_up_a", [512, D]); w_upb = din("w_up_b", [256, D]); w_o = din("w_o", [D, D])
    c_ident = din("c_ident", [128, 128]); c_dt = din("c_dt", [128, NTAB, 128]); c_ab = din("c_ab", [128, 16 * 3 * 4 * 2])
    c_sal = din("c_sal", [128, 4, 17 * 8])
    c_rep = din("c_rep", [16, 128]); c_sel = din("c_sel", [128, 16])
    y_p = dout("y_prompt", [NSEQ, SEQ, D]); y_s = dout("y_sample", [NS, D])
    o_ap = dout("a_p", [NSEQ, 128, 2, 2, HD])
    o_bp = [dout("b1_p", [NSEQ, 128, 2, 4, HD]), dout("b2_p", [NSEQ, 512, 2, 4, HD]), dout("b3_p", [NSEQ, 2048, 2, 4, HD])]
    if with_sample:
        o_as = dout("a_s", [NS, 128, 2, 2, HD])
        o_bs = [dout("b1_s", [NS, 128, 2, 4, HD]), dout("b2_s", [NS, 512, 2, 4, HD]), dout("b3_s", [NS, 2048, 2, 4, HD])]
    s_gu = [dscr("s_gu1", [11, 128, 2, 2, 8, 128]), dscr("s_gu2", [11, 128, 2, 2, 8, 128])]
    s_d = [dscr("s_d1", [NF, 128, D]), dscr("s_d2", [NF, 128, D])]
    s_in = dscr("s_in", [6, 128, 8, 512])
    s_m = dscr("s_m", [8, 128, 22 * 128])
    s_o = dscr("s_o", [2, 128, 8, 512])

    es = ExitStack()
    with es:
        S = Sched(nc, es)

        def sb(name, shape, dt, stack=es):
            return stack.enter_context(nc.sbuf_tensor(name, list(shape), dt))

        psT = [es.enter_context(nc.psum_tensor("psT%d" % i, [128, 1024], BF16)) for i in range(2)]
        psB = [es.enter_context(nc.psum_tensor("psB%d" % i, [128, 512], F32)) for i in range(6)]
        PB = ["P%d" % i for i in range(6)]
        TB = ["T0", "T1"]

        idb = sb("idb", [128, 128], BF16)
        dtab = sb("dtab", [128, NTAB, 128], BF16)
        abt = sb("abt", [128, 16, 3, 4, 2], F32)
        gq_a = sb("gq_a", [128, HD], F32); gk_a = sb("gk_a", [128, HD], F32)
        gq_b = sb("gq_b", [128, HD], F32); gk_b = sb("gk_b", [128, HD], F32)
        esink = sb("esink", [128, 8], F32)
        mhalf = sb("mhalf", [128, 8], F32)
        small = sb("small", [128, 96], F32)

        S.op("pool", lambda: nc.gpsimd.memset(mhalf[:], -0.5), writes=["mhalf"])
        gcol = sb("gcol", [128, 4], F32)
        for ci, v in enumerate([qn_a, kn_a, qn_b, kn_b]):
            for hf in range(2):
                S.dma("sp", "gcol", gcol[hf * 64:(hf + 1) * 64, ci:ci + 1], v.rearrange("(d o) -> d o", o=1), writes=["gcol"],
                      allow_slow_non_contiguous=True)
        S.op("act", lambda: nc.scalar.mul(out=gcol[:, 0:1], in_=gcol[:, 0:1], mul=HD ** -0.5), reads=["gcol"], writes=["gcol"])
        S.op("act", lambda: nc.scalar.mul(out=gcol[:, 2:3], in_=gcol[:, 2:3], mul=HD ** -0.5), reads=["gcol"], writes=["gcol"])

        def bcast_load(dst, src, n, name):
            S.dma("sp", name, dst, src.rearrange("(o n) -> o n", o=1).broadcast_to([128, n]), writes=[name])

        bcast_load(gq_a[:], qn_a, HD, "gq_a"); bcast_load(gk_a[:], kn_a, HD, "gk_a")
        bcast_load(gq_b[:], qn_b, HD, "gq_b"); bcast_load(gk_b[:], kn_b, HD, "gk_b")
        bcast_load(esink[:], sinks, 8, "esink")
        S.dma("sp", "abt", abt[:].rearrange("p a b c d -> p (a b c d)"), c_ab, writes=["abt"])
        S.op("act", lambda: nc.scalar.mul(out=gq_a[:], in_=gq_a[:], mul=HD ** -0.5), reads=["gq_a"], writes=["gq_a"])
        S.op("act", lambda: nc.scalar.mul(out=gq_b[:], in_=gq_b[:], mul=HD ** -0.5), reads=["gq_b"], writes=["gq_b"])
        S.op("act", lambda: nc.scalar.activation(out=esink[:], in_=esink[:], func=AF.Exp), reads=["esink"], writes=["esink"])

        with ExitStack() as ps:
            stg_in = [sb("stg_in%d" % i, [128, 8, 1024], F32, ps) for i in range(2)]
            stg_out = [sb("stg_out%d" % i, [128, 8, 1024], BF16, ps) for i in range(2)]
            gains = sb("gains", [128, 3, 8], F32, ps)
            dt32 = sb("dt32", [128, NTAB, 128], F32, ps)
            id32 = sb("id32", [128, 128], F32, ps)
            for i, g in enumerate([g_f1, g_mx, g_f2]):
                S.dma("sp", "gains", gains[:, i, :], g.rearrange("(c p) -> p c", p=128), writes=["gains"],
                      allow_slow_non_contiguous=True)
            S.dma("sp", "id32", id32[:], c_ident, writes=["id32"])
            S.dma("sp", "dt32", dt32[:], c_dt, writes=["dt32"])
            S.op("dve", lambda: nc.vector.tensor_copy(out=idb[:], in_=id32[:]), reads=["id32"], writes=["idb"])
            S.op("dve", lambda: nc.vector.tensor_copy(out=dtab[:], in_=dt32[:]), reads=["dt32"], writes=["dtab"])

            pcount = [0]
            def block(src_ap, nk, n, gain_idx, perm_f, stores):
                i = pcount[0] % 2
                pcount[0] += 1
                tin = stg_in[i][:, 0:nk, 0:n]
                S.dma("sp", "stg_in%d" % i, tin, src_ap, writes=["stg_in%d" % i])
                flat = stg_out[i][:].rearrange("p k c -> p (k c)")[:, 0:nk * n]
                if perm_f:
                    f = n // 128
                    ov_all = flat.rearrange("p (f k c) -> p k f c", f=f, k=nk)
                    iv_all = tin.rearrange("p k (f c) -> p k f c", c=128)
                else:
                    ov_all = flat.rearrange("p (k c) -> p k c", k=nk)
                    iv_all = tin
                if gain_idx is not None:
                    for kc in range(nk):
                        S.op("act", lambda kc=kc: nc.scalar.activation(out=ov_all[:, kc], in_=iv_all[:, kc], func=AF.Copy,
                                                                       scale=gains[:, gain_idx, kc:kc + 1]),
                             reads=["stg_in%d" % i, "gains"], writes=["stg_out%d" % i], signal=(kc == nk - 1))
                else:
                    e = "dve" if pcount[0] % 2 else "pool"
                    if e == "dve":
                        S.op("dve", lambda: nc.vector.tensor_copy(out=ov_all, in_=iv_all), reads=["stg_in%d" % i], writes=["stg_out%d" % i])
                    else:
                        S.op("pool", lambda: nc.gpsimd.tensor_copy(out=ov_all, in_=iv_all), reads=["stg_in%d" % i], writes=["stg_out%d" % i])
                for (dst, src, res) in stores(flat):
                    S.dma("pool", "stg_out%d" % i, dst, src, reads=["stg_out%d" % i], writes=[res], disjoint=True)

            def wsrc(w, c0, ncol, nk):
                return w.rearrange("(c p) n -> p c n", p=128)[:, 0:nk, c0:c0 + ncol]

            for l in range(2):
                gi = 0 if l == 0 else 2
                for c0, n in ((0, 1024), (1024, 1024), (2048, 768)):
                    for gu, w in enumerate([w_g[l], w_u[l]]):
                        def st(flat, c0=c0, n=n, gu=gu, l=l):
                            out = []
                            for q_ in range(n // 256):
                                fp = c0 // 256 + q_
                                out.append((s_gu[l][fp, :, :, gu, :, :].rearrange("p f k c -> p f (k c)"),
                                            flat[:, q_ * 2048:(q_ + 1) * 2048].rearrange("p (f x) -> p f x", f=2),
                                            "s_gu%d_%d_%d" % (l, fp, gu)))
                            return out
                        block(wsrc(w, c0, n, 8), 8, n, gi, True, st)
                for f0, nf in ((0, 8), (8, 8), (16, 6)):
                    def st(flat, f0=f0, nf=nf, l=l):
                        return [(s_d[l][f0:f0 + nf].rearrange("f p c -> p f c"), flat.rearrange("p (f c) -> p f c", f=nf),
                                 "s_d%d_%d" % (l, f0))]
                    block(w_d[l].rearrange("(f p) c -> p f c", p=128)[:, f0:f0 + nf, :], nf, 1024, None, False, st)
            for bk in range(3):
                def st(flat, bk=bk):
                    v = flat.rearrange("p (k c) -> p k c", k=8)
                    return [(s_in[2 * bk + hf], v[:, :, hf * 512:(hf + 1) * 512], "s_in%d" % (2 * bk + hf)) for hf in range(2)]
                block(wsrc(w_in, bk * 1024, 1024, 8), 8, 1024, 1, False, st)
            for which, c0, off in [("ga", 3072, 6 * 128), ("gb", 4096, 14 * 128)]:
                def st(flat, off=off, which=which):
                    return [(s_m[:, :, off:off + 1024].rearrange("m p c -> p m c"), flat.rearrange("p (m x) -> p m x", m=8), "s_m_" + which)]
                block(wsrc(w_in, c0, 1024, 8), 8, 1024, 1, True, st)
            def st(flat):
                return [(s_m[:, :, 0:512].rearrange("m p c -> p m c"), flat.rearrange("p (m x) -> p m x", m=8), "s_m_upa")]
            block(wsrc(w_upa, 0, 1024, 4), 4, 1024, None, True, st)
            def st(flat):
                return [(s_m[:, :, 512:768].rearrange("m p c -> p m c"), flat.rearrange("p (m x) -> p m x", m=8), "s_m_upb")]
            block(wsrc(w_upb, 0, 1024, 2), 2, 1024, None, True, st)
            def st(flat):
                v = flat.rearrange("p (k c) -> p k c", k=8)
                return [(s_o[ch], v[:, :, ch * 512:(ch + 1) * 512], "s_o%d" % ch) for ch in range(2)]
            block(wsrc(w_o, 0, 1024, 8), 8, 1024, None, False, st)
            S.barrier()

        ms = ExitStack()
        es.enter_context(ms)
        xt = sb("xt", [128, 5, D], F32, ms)
        xmap = [0, 1, 2, 3]
        hb = sb("hb", [128, D], BF16, ms)
        hT = sb("hT", [128, 8, 512], BF16, ms)
        big = sb("big", [128, 11, 512], BF16, ms)
        wd = sb("wd", [128, 11, D], BF16, ms)
        ring = [sb("ring%d" % i, [128, 4096], BF16, ms) for i in range(4)]
        sg = [sb("sg%d" % i, [128, 512], F32, ms) for i in range(2)]
        hs = ExitStack()
        stg = [sb("stg%d" % i, [128, 256], F32, ms) for i in range(NSTG)]
        pT = [sb("pT%d" % i, [128, 512], BF16, ms) for i in range(2)]
        oacc = sb("oacc", [128, 12, 65], F32, ms)
        otmp = sb("otmp", [128, 4, 65], F32, ms)
        onrm = sb("onrm", [128, 768], BF16, ms)
        oT = sb("oT", [128, 6, 512], BF16, ms)
        rden = sb("rden", [128, 12], F32, ms)
        hb2 = [hb, sb("hb1", [128, D], BF16, ms)]
        sg.append(sb("sg2", [128, 512], F32, ms))
        pT.append(sb("pT2", [128, 512], BF16, ms))
        sg.append(sb("sg3", [128, 512], F32, ms))
        pT.append(sb("pT3", [128, 512], BF16, ms))
        sqb = [sb("sqb%d" % i, [128, 512], F32, hs) for i in range(3)]
        qbb = [sb("qbb%d" % i, [128, 512], BF16, hs) for i in range(3)]
        KTA = sb("KTA", [128, 1, SEQ], BF16, hs)
        VA = sb("VA", [128, 16, 2, 65], BF16, hs)
        KTB = sb("KTB", [128, 6, SEQ], BF16, hs)
        VB = sb("VB", [128, 16, 3, 4, 65], BF16, hs)
        zs_box = [None]

        S.op("dve", lambda: nc.vector.memset(VA[:].rearrange("p a b c -> p (a b c)"), 1.0), writes=["VA"])
        S.op("dve", lambda: nc.vector.memset(VB[:].rearrange("p a b c d -> p (a b c d)"), 1.0), writes=["VB"])
        for g_ in (1, 2):
            S.op("dve", lambda g_=g_: nc.vector.tensor_copy(out=VB[:, :, g_, :, HD:HD + 1], in_=abt[:, :, g_, :, 1:2]),
                 reads=["abt"], writes=["VB"])

        ring_i = [0]
        st_i = [0]
        bank_i = [0]
        tb_i = [0]
        sg_i = [0]
        pt_i = [0]

        def next_ring():
            i = ring_i[0] % 4
            ring_i[0] += 1
            return i

        def next_bank(lo=0, hi=4):
            i = lo + bank_i[0] % (hi - lo)
            bank_i[0] += 1
            return i

        def next_tb():
            i = tb_i[0] % 2
            tb_i[0] += 1
            return i

        ew_i = [0]

        def ew():
            ew_i[0] += 1
            return "dve" if ew_i[0] % 2 else "act"

        def copy_op(e, out, in_, reads, writes):
            if e == "act":
                S.op("act", lambda: nc.scalar.copy(out=out, in_=in_), reads=reads, writes=writes)
            elif e == "dve":
                S.op("dve", lambda: nc.vector.tensor_copy(out=out, in_=in_), reads=reads, writes=writes)
            else:
                S.op("pool", lambda: nc.gpsimd.tensor_copy(out=out, in_=in_), reads=reads, writes=writes)

        def norm_T(P, nsub, NT):
            for s in range(nsub):
                c = 32 + 3 * s
                S.op("act", lambda: nc.scalar.activation(out=hb2[s % 2][0:P, :], in_=xt[0:P, xmap[s], :], func=AF.Square,
                                                         accum_out=small[0:P, c:c + 1]),
                     reads=["xt%d" % xmap[s]], writes=["hb%d" % (s % 2), "nst%d" % s])
                S.op("pool", lambda: nc.gpsimd.tensor_scalar(out=small[0:P, c + 1:c + 2], in0=small[0:P, c:c + 1], scalar1=1.0 / D,
                                                             scalar2=EPS, op0=ALU.mult, op1=ALU.add),
                     reads=["nst%d" % s], writes=["nst%d" % s])
                S.op("pool", lambda: nc.gpsimd.tensor_tensor(out=small[0:P, c + 2:c + 3], in0=small[0:P, c + 1:c + 2],
                                                             in1=mhalf[0:P, 0:1], op=ALU.pow),
                     reads=["nst%d" % s, "mhalf"], writes=["nst%d" % s])
                if s >= 1:
                    norm_tail(P, s - 1)
            norm_tail(P, nsub - 1)

        def norm_tail(P, s):
            c = 32 + 3 * s
            hbuf = hb2[s % 2]
            S.op("act", lambda: nc.scalar.activation(out=hbuf[0:P, :], in_=xt[0:P, xmap[s], :], func=AF.Copy,
                                                     scale=small[0:P, c + 2:c + 3]),
                 reads=["xt%d" % xmap[s], "nst%d" % s], writes=["hb%d" % (s % 2)])
            t = next_tb()
            for kc in range(8):
                S.op("pe", lambda kc=kc: nc.tensor.transpose(out=psT[t][:, kc * 128:kc * 128 + P],
                                                             in_=hbuf[0:P, kc * 128:(kc + 1) * 128],
                                                             identity=idb[0:P, 0:P]),
                     reads=["hb%d" % (s % 2), "idb"], writes=[TB[t]], signal=(kc == 7))
            copy_op(ew(), hT[:, :, s * 128:s * 128 + P],
                    psT[t][:].rearrange("p (k c) -> p k c", k=8)[:, :, 0:P], [TB[t]], [TB[t], "hT"])

        def ffn(l, P, nsub, NT, on_sub_done=None):
            plan_l = []
            loads_idx = {}
            for half_ in range(2):
                for j_ in range(11):
                    f_ = half_ * 11 + j_
                    if f_ % 2 == 0 or j_ == 0:
                        loads_idx[(half_, f_ // 2)] = len(plan_l)
                        plan_l.append(f_ // 2)
            load_slot = {}
            emitted = [0]

            def ensure(k):
                while emitted[0] <= min(k, len(plan_l) - 1):
                    i_ = emitted[0]
                    r_ = next_ring()
                    S.dma("sp", "ring%d" % r_, ring[r_][:], s_gu[l][plan_l[i_]].rearrange("p f g k c -> p (f g k c)"),
                          reads=["s_gu%d_%d_%d" % (l, plan_l[i_], a_) for a_ in range(2)], writes=["ring%d" % r_])
                    load_slot[i_] = r_
                    emitted[0] += 1

            if LOOK > 0:
                ensure(1)
            norm_T(P, nsub, NT)
            for half in range(2):
                for j in range(11):
                    f = half * 11 + j
                    S.dma("sp", "wd%d" % j, wd[:, j, :], s_d[l][f], reads=["s_d%d_%d" % (l, (f // 8) * 8)], writes=["wd%d" % j])
                for j in range(11):
                    f = half * 11 + j
                    fp, fi = f // 2, f % 2
                    if fi == 0 or j == 0:
                        li = loads_idx[(half, fp)]
                        ensure(li + LOOK)
                        r = load_slot[li]
                        rv = ring[r][:].rearrange("p (f g k c) -> p f g k c", f=2, g=2, k=8)
                    bg, bu = next_bank(0, 2), 2 + next_bank(0, 2)
                    for gu, b in [(0, bg), (1, bu)]:
                        for kc in range(8):
                            S.op("pe", lambda gu=gu, b=b, kc=kc: nc.tensor.matmul(
                                psB[b][:, 0:NT], lhsT=rv[:, fi, gu, kc, :], rhs=hT[:, kc, 0:NT],
                                start=(kc == 0), stop=(kc == 7)),
                                reads=["ring%d" % r, "hT"], writes=[PB[b]], signal=(kc == 7))
                    si = sg_i[0] % 2
                    sg_i[0] += 1
                    S.op("act", lambda: nc.scalar.activation(out=sg[si][:, 0:NT], in_=psB[bg][:, 0:NT], func=AF.Silu),
                         reads=[PB[bg]], writes=[PB[bg], "sg%d" % si])
                    S.op("dve", lambda: nc.vector.tensor_tensor(out=big[:, j, 0:NT], in0=sg[si][:, 0:NT],
                                                                in1=psB[bu][:, 0:NT], op=ALU.mult),
                         reads=["sg%d" % si, PB[bu]], writes=[PB[bu], "big%d" % j])
                for s in range(nsub):
                    for ch in range(2):
                        b = 4 + ch
                        for j in range(11):
                            S.op("pe", lambda j=j, b=b, ch=ch: nc.tensor.matmul(
                                psB[b][0:P, :], lhsT=big[:, j, s * 128:s * 128 + P], rhs=wd[:, j, ch * 512:(ch + 1) * 512],
                                start=(j == 0), stop=(j == 10)),
                                reads=["big%d" % j, "wd%d" % j], writes=[PB[b]], signal=(j == 10))
                        S.op("dve", lambda b=b, ch=ch: nc.vector.scalar_tensor_tensor(
                            out=xt[0:P, xmap[s], ch * 512:(ch + 1) * 512], in0=psB[b][0:P, :], scalar=0.5,
                            in1=xt[0:P, xmap[s], ch * 512:(ch + 1) * 512], op0=ALU.mult, op1=ALU.add),
                            reads=[PB[b], "xt%d" % xmap[s]], writes=[PB[b], "xt%d" % xmap[s]])
                        if half == 1 and ch == 1 and on_sub_done is not None:
                            on_sub_done(s)

        def qk_norm(P, bank, c0, nh, gain_tab):
            W = nh * HD
            S.op("act", lambda: nc.scalar.activation(out=sq[0:P, 0:W], in_=psB[bank][0:P, c0:c0 + W], func=AF.Square),
                 reads=[PB[bank]], writes=[PB[bank], "sq"])
            S.op("dve", lambda: nc.vector.tensor_reduce(out=small[0:P, 8:8 + nh],
                                                        in_=sq[0:P, 0:W].rearrange("p (h d) -> p h d", h=nh),
                                                        axis=AX.X, op=ALU.add), reads=["sq"], writes=["small"])
            S.op("pool", lambda: nc.gpsimd.tensor_scalar(out=small[0:P, 16:16 + nh], in0=small[0:P, 8:8 + nh],
                                                         scalar1=1.0 / HD, scalar2=EPS, op0=ALU.mult, op1=ALU.add),
                 reads=["small"], writes=["small"])
            S.op("pool", lambda: nc.gpsimd.tensor_tensor(out=small[0:P, 24:24 + nh], in0=small[0:P, 16:16 + nh],
                                                         in1=mhalf[0:P, 0:nh], op=ALU.pow),
                 reads=["small", "mhalf"], writes=["small"])
            S.op("dve", lambda: nc.vector.tensor_tensor(
                out=qn[0:P, 0:W].rearrange("p (h d) -> p h d", h=nh),
                in0=psB[bank][0:P, c0:c0 + W].rearrange("p (h d) -> p h d", h=nh),
                in1=small[0:P, 24:24 + nh].unsqueeze(2).broadcast_to([P, nh, HD]), op=ALU.mult),
                reads=[PB[bank], "small"], writes=[PB[bank], "qn"])
            S.op("pool", lambda: nc.gpsimd.tensor_tensor(
                out=qn[0:P, 0:W].rearrange("p (h d) -> p h d", h=nh),
                in0=qn[0:P, 0:W].rearrange("p (h d) -> p h d", h=nh),
                in1=gain_tab[0:P, :].unsqueeze(1).broadcast_to([P, nh, HD]), op=ALU.mult),
                reads=["qn"], writes=["qn"])

        def transpose_to(P, src_tok, ncols, dsts):
            t = next_tb()
            n = ncols // 128
            for c in range(n):
                S.op("pe", lambda c=c: nc.tensor.transpose(out=psT[t][:, c * 128:c * 128 + P],
                                                           in_=src_tok[0:P, c * 128:(c + 1) * 128], identity=idb[0:P, 0:P]),
                     reads=["qb16"], writes=[TB[t]], signal=(c == n - 1))
            for c, (dst, res) in enumerate(dsts):
                copy_op(ew(), dst, psT[t][:, c * 128:c * 128 + P], [TB[t]], [TB[t], res])

        def kv_out(seq, blk, grp, kv, st_idx, nh):
            if seq is None:
                return
            if grp == "A":
                if blk == 15:
                    S.dma("sp", "stq%d" % st_idx, o_ap[seq, :, kv, :, :].rearrange("t h d -> t (h d)"),
                          stg[st_idx][:, 0:nh * HD], reads=["stg%d" % st_idx])
                return
            g = grp
            nb = WINB[g]
            if blk >= 16 - nb:
                t0 = (blk - (16 - nb)) * 128
                S.dma("sp", "stq%d" % st_idx, o_bp[g][seq, t0:t0 + 128, kv, :, :].rearrange("t h d -> t (h d)"),
                      stg[st_idx][:, 0:nh * HD], reads=["stg%d" % st_idx])

        def next_stg():
            i = st_i[0] % NSTG
            st_i[0] += 1
            return i

        def project(P, nsub, NT, seq, blk0):
            for nt in range(6):
                r = next_ring()
                S.dma("sp", "ring%d" % r, ring[r][:], s_in[nt].rearrange("p k c -> p (k c)"),
                      reads=["s_in%d" % nt], writes=["ring%d" % r])
                rv = ring[r][:].rearrange("p (k c) -> p k c", k=8)
                for s in range(nsub):
                    blk = blk0 + s
                    tok = slice(s * 128, s * 128 + P)
                    b = next_bank(0, 4)
                    for kc in range(8):
                        S.op("pe", lambda kc=kc: nc.tensor.matmul(psB[b][0:P, :], lhsT=hT[:, kc, tok], rhs=rv[:, kc, :],
                                                                  start=(kc == 0), stop=(kc == 7)),
                             reads=["hT", "ring%d" % r], writes=[PB[b]], signal=(kc == 7))
                    for uu in range(2):
                        u = nt * 2 + uu
                        c0 = uu * 256
                        if seq is None:
                            zs = zs_box[0]
                            if u in (0, 1, 3, 4, 5, 6, 7, 8):
                                qk_norm(P, b, c0, 4, gq_a if u < 2 else (gq_b if u < 6 else gk_b))
                                S.op("act", lambda: nc.scalar.copy(out=zs[0:P, u * 256:(u + 1) * 256], in_=qn[0:P, :]),
                                     reads=["qn"], writes=["zs"])
                            elif u == 2:
                                qk_norm(P, b, c0, 2, gk_a)
                                S.op("act", lambda: nc.scalar.copy(out=zs[0:P, 512:640], in_=qn[0:P, 0:128]),
                                     reads=["qn"], writes=["zs"])
                                S.op("dve", lambda: nc.vector.tensor_copy(out=zs[0:P, 640:768], in_=psB[b][0:P, c0 + 128:c0 + 256]),
                                     reads=[PB[b]], writes=[PB[b], "zs"])
                            else:
                                S.op("dve", lambda: nc.vector.tensor_copy(out=zs[0:P, u * 256:(u + 1) * 256], in_=psB[b][0:P, c0:c0 + 256]),
                                     reads=[PB[b]], writes=[PB[b], "zs"])
                            continue
                        if u in (0, 1):
                            qk_norm(P, b, c0, 4, gq_a)
                            S.op("act", lambda: nc.scalar.copy(out=qb16[0:P, :], in_=qn[0:P, :]), reads=["qn"], writes=["qb16"])
                            transpose_to(P, qb16, 256, [(big[:, 2 * u + c, tok], "big%d" % (2 * u + c)) for c in range(2)])
                        elif u == 2:
                            qk_norm(P, b, c0, 2, gk_a)
                            si = next_stg()
                            S.op("act", lambda: nc.scalar.copy(out=stg[si][0:P, 0:128], in_=qn[0:P, 0:128]),
                                 reads=["qn"], writes=["stg%d" % si])
                            kv_out(seq, blk, "A", 0, si, 2)
                            S.op("dve", lambda: nc.vector.tensor_copy(
                                out=qb16[0:P, :].rearrange("p (h r d) -> p h r d", h=2, r=2),
                                in_=qn[0:P, 0:128].rearrange("p (h d) -> p h d", h=2).unsqueeze(2).broadcast_to([P, 2, 2, HD])),
                                reads=["qn"], writes=["qb16"])
                            if seq is not None:
                                transpose_to(P, qb16, 256, [(KTA[:, c, blk * 128:blk * 128 + P], "KTA") for c in range(2)])
                            else:
                                transpose_to(P, qb16, 256, [(KTA[:, c, 0:P], "KTA") for c in range(2)])
                            si = next_stg()
                            S.op("act", lambda: nc.scalar.copy(out=stg[si][0:P, 0:128], in_=psB[b][0:P, c0 + 128:c0 + 256]),
                                 reads=[PB[b]], writes=[PB[b], "stg%d" % si])
                            kv_out(seq, blk, "A", 1, si, 2)
                            S.op("dve", lambda: nc.vector.tensor_copy(
                                out=VA[0:P, blk if seq is not None else 0, :, 0:HD],
                                in_=stg[si][0:P, 0:128].rearrange("p (h d) -> p h d", h=2)),
                                reads=["stg%d" % si], writes=["VA"])
                        elif u in (3, 4, 5):
                            g = u - 3
                            qk_norm(P, b, c0, 4, gq_b)
                            S.op("act", lambda: nc.scalar.copy(out=qb16[0:P, :], in_=qn[0:P, :]), reads=["qn"], writes=["qb16"])
                            transpose_to(P, qb16, 256, [(big[:, 4 + 2 * g + c, tok], "big%d" % (4 + 2 * g + c)) for c in range(2)])
                        elif u in (6, 7, 8):
                            g = u - 6
                            qk_norm(P, b, c0, 4, gk_b)
                            si = next_stg()
                            S.op("act", lambda: nc.scalar.copy(out=stg[si][0:P, :], in_=qn[0:P, :]), reads=["qn"], writes=["stg%d" % si])
                            kv_out(seq, blk, g, 0, si, 4)
                            S.op("dve", lambda: nc.vector.tensor_copy(out=qb16[0:P, :], in_=qn[0:P, :]), reads=["qn"], writes=["qb16"])
                            kcol = slice(blk * 128, blk * 128 + P) if seq is not None else slice(0, P)
                            transpose_to(P, qb16, 256, [(KTB[:, 2 * g + c, kcol], "KTB") for c in range(2)])
                        else:
                            g = u - 9
                            si = next_stg()
                            S.op("act", lambda: nc.scalar.copy(out=stg[si][0:P, :], in_=psB[b][0:P, c0:c0 + 256]),
                                 reads=[PB[b]], writes=[PB[b], "stg%d" % si])
                            kv_out(seq, blk, g, 1, si, 4)
                            vb = blk if seq is not None else 0
                            if g == 0 or seq is None:
                                S.op("dve", lambda: nc.vector.tensor_copy(
                                    out=VB[0:P, vb, g, :, 0:HD], in_=stg[si][0:P, :].rearrange("p (h d) -> p h d", h=4)),
                                    reads=["stg%d" % si], writes=["VB"])
                            else:
                                S.op("dve", lambda: nc.vector.tensor_tensor(
                                    out=VB[0:P, vb, g, :, 0:HD], in0=stg[si][0:P, :].rearrange("p (h d) -> p h d", h=4),
                                    in1=abt[0:P, vb, g, :, 1:2].broadcast_to([P, 4, HD]), op=ALU.mult),
                                    reads=["stg%d" % si, "abt"], writes=["VB"])
                                S.op("pool", lambda: nc.gpsimd.tensor_copy(out=VB[0:P, vb, g, :, HD:HD + 1],
                                                                           in_=abt[0:P, vb, g, :, 1:2]),
                                     reads=["abt"], writes=["VB"])

        def project_prompt(seq, blk0):
            P = 128
            tails = []
            gidx = [0]

            def emit_group(nt, s, r, rv):
                blk = blk0 + s
                tok = slice(s * 128, (s + 1) * 128)
                kcol = slice(blk * 128, (blk + 1) * 128)
                b = next_bank(0, 4)
                par = gidx[0] % 3
                gidx[0] += 1
                for kc in range(8):
                    S.op("pe", lambda kc=kc: nc.tensor.matmul(psB[b][:, :], lhsT=hT[:, kc, tok], rhs=rv[:, kc, :],
                                                              start=(kc == 0), stop=(kc == 7)),
                         reads=["hT", "ring%d" % r], writes=[PB[b]], signal=(kc == 7))
                ps3 = psB[b][:, :].rearrange("p (h d) -> p h d", h=8)
                st = "pst%d" % par
                c = 64 + par * 8
                if nt < 5:
                    S.op("act", lambda: nc.scalar.activation(out=sqb[par][:, :], in_=psB[b][:, :], func=AF.Square),
                         reads=[PB[b]], writes=[PB[b], "sqb%d" % par])
                    S.op("dve", lambda: nc.vector.tensor_reduce(out=small[:, c:c + 8],
                                                                in_=sqb[par][:, :].rearrange("p (h d) -> p h d", h=8),
                                                                axis=AX.X, op=ALU.add), reads=["sqb%d" % par], writes=[st])
                    S.op("pool", lambda: nc.gpsimd.tensor_scalar(out=small[:, c:c + 8], in0=small[:, c:c + 8],
                                                                 scalar1=1.0 / HD, scalar2=EPS, op0=ALU.mult, op1=ALU.add),
                         reads=[st], writes=[st])
                    S.op("pool", lambda: nc.gpsimd.tensor_tensor(out=small[:, c:c + 8], in0=small[:, c:c + 8],
                                                                 in1=mhalf[:, 0:8], op=ALU.pow), reads=[st, "mhalf"], writes=[st])
                plan = []

                def stageB():
                  if nt < 5:
                    rs = small[:, c:c + 8].unsqueeze(2).broadcast_to([128, 8, HD])
                    if nt == 0:
                        ov = qbb[par][:, :].rearrange("p (c hf d) -> p hf c d", c=4, hf=2)
                        iv = psB[b][:, :].rearrange("p (hf c d) -> p hf c d", hf=2, c=4)
                        rs4 = small[:, c:c + 8].rearrange("p (hf c) -> p hf c", hf=2).unsqueeze(3).broadcast_to([128, 2, 4, HD])
                        S.op("dve", lambda: nc.vector.tensor_tensor(out=ov, in0=iv, in1=rs4, op=ALU.mult),
                             reads=[PB[b], st], writes=[PB[b], "qbb%d" % par])
                    else:
                        S.op("dve", lambda: nc.vector.tensor_tensor(out=qbb[par][:, :].rearrange("p (h d) -> p h d", h=8),
                                                                    in0=ps3, in1=rs, op=ALU.mult),
                             reads=[PB[b], st], writes=[PB[b], "qbb%d" % par])

                  def k_out(c0, nh, gtab, grp):
                      need = (blk == 15) if grp == "A" else (blk >= 16 - WINB[grp])
                      if not need:
                          return
                      si = next_stg()
                      h0 = c0 // HD
                      S.op("dve", lambda: nc.vector.tensor_tensor(
                          out=stg[si][:, 0:nh * HD].rearrange("p (h d) -> p h d", h=nh), in0=ps3[:, h0:h0 + nh, :],
                          in1=small[:, c + h0:c + h0 + nh].unsqueeze(2).broadcast_to([128, nh, HD]), op=ALU.mult),
                          reads=[PB[b], st], writes=[PB[b], "stg%d" % si])
                      S.op("dve", lambda: nc.vector.tensor_tensor(
                          out=stg[si][:, 0:nh * HD].rearrange("p (h d) -> p h d", h=nh),
                          in0=stg[si][:, 0:nh * HD].rearrange("p (h d) -> p h d", h=nh),
                          in1=gtab[:, :].unsqueeze(1).broadcast_to([128, nh, HD]), op=ALU.mult),
                          reads=["stg%d" % si], writes=["stg%d" % si])
                      kv_out(seq, blk, grp, 0, si, nh)

                  def v_part(c0, nh, grp):
                      need = (blk == 15) if grp == "A" else (blk >= 16 - WINB[grp])
                      src = psB[b][:, c0:c0 + nh * HD]
                      if need:
                          si = next_stg()
                          S.op("act", lambda: nc.scalar.copy(out=stg[si][:, 0:nh * HD], in_=src),
                               reads=[PB[b]], writes=[PB[b], "stg%d" % si])
                          kv_out(seq, blk, grp, 1, si, nh)
                      s3 = src.rearrange("p (h d) -> p h d", h=nh)
                      if grp == "A":
                          copy_op(ew(), VA[:, blk, :, 0:HD], s3, [PB[b]], [PB[b], "VA"])
                      elif grp == 0:
                          copy_op(ew(), VB[:, blk, 0, :, 0:HD], s3, [PB[b]], [PB[b], "VB"])
                      else:
                          S.op("dve", lambda: nc.vector.tensor_tensor(
                              out=VB[:, blk, grp, :, 0:HD], in0=s3, in1=abt[:, blk, grp, :, 1:2].broadcast_to([128, 4, HD]),
                              op=ALU.mult), reads=[PB[b], "abt"], writes=[PB[b], "VB"])

                  if nt == 0:
                      plan.append(([0, 1, 2, 3], big[:, 0:4, tok], ["big0", "big1", "big2", "big3"], 0))
                  elif nt == 1:
                      k_out(0, 2, gk_a, "A")
                      v_part(128, 2, "A")
                      plan.append(([0], KTA[:, 0:1, kcol], ["KTA"], 1))
                      plan.append(([2, 3], big[:, 4:6, tok], ["big4", "big5"], 2))
                  elif nt == 2:
                      plan.append(([0, 1, 2, 3], big[:, 6:10, tok], ["big6", "big7", "big8", "big9"], 2))
                  elif nt == 3:
                      k_out(0, 4, gk_b, 0)
                      k_out(256, 4, gk_b, 1)
                      plan.append(([0, 1, 2, 3], KTB[:, 0:4, kcol], ["KTB"], 3))
                  elif nt == 4:
                      k_out(0, 4, gk_b, 2)
                      v_part(256, 4, 0)
                      plan.append(([0, 1], KTB[:, 4:6, kcol], ["KTB"], 3))
                  else:
                      v_part(0, 4, 1)
                      v_part(256, 4, 2)

                def tail():
                    if not plan:
                        return
                    t = next_tb()
                    allc = [ci for (cs, _, _, _) in plan for ci in cs]
                    for i, ci in enumerate(allc):
                        S.op("pe", lambda ci=ci: nc.tensor.transpose(out=psT[t][:, ci * 128:(ci + 1) * 128],
                                                                     in_=qbb[par][:, ci * 128:(ci + 1) * 128], identity=idb[:, :]),
                             reads=["qbb%d" % par, "idb"], writes=[TB[t]], signal=(i == len(allc) - 1))
                    for (cs, dst, dres, gi) in plan:
                        n = len(cs)
                        src = psT[t][:, cs[0] * 128:(cs[0] + n) * 128].rearrange("p (n c) -> p n c", n=n)
                        if ew() == "act":
                            S.op("act", lambda: nc.scalar.activation(out=dst, in_=src, func=AF.Copy, scale=gcol[:, gi:gi + 1]),
                                 reads=[TB[t], "gcol"], writes=[TB[t]] + dres)
                        else:
                            S.op("dve", lambda: nc.vector.tensor_scalar(out=dst, in0=src, scalar1=gcol[:, gi:gi + 1], scalar2=None,
                                                                        op0=ALU.mult), reads=[TB[t], "gcol"], writes=[TB[t]] + dres)
                return stageB, tail

            ptails = []
            pendB = [None]
            for nt in range(6):
                r = next_ring()
                S.dma("sp", "ring%d" % r, ring[r][:], s_in[nt].rearrange("p k c -> p (k c)"),
                      reads=["s_in%d" % nt], writes=["ring%d" % r])
                rv = ring[r][:].rearrange("p (k c) -> p k c", k=8)
                for s in range(4):
                    sB, tl_ = emit_group(nt, s, r, rv)
                    if pendB[0] is not None:
                        pendB[0]()
                    pendB[0] = sB
                    ptails.append(tl_)
                    if len(ptails) > PDEPTH:
                        ptails.pop(0)()
            pendB[0]()
            while ptails:
                ptails.pop(0)()

        def attention_block(s, blk):
            qcol = slice(s * 128, s * 128 + 128)
            jobs = []
            for h in range(8):
                kvh, par, ch = h // 4, h // 4, h % 4
                pairs = [(kb, TIDX[("A", h, blk - kb)]) for kb in (blk - 1, blk) if kb >= 0]
                jobs.append((h, "A", par, lambda kb: KTA[:, 0, kb * 128:(kb + 1) * 128], big[:, ch, qcol], pairs,
                             lambda kb, kvh=kvh: VA[:, kb, kvh, :], "big%d" % ch))
            for g in range(3):
                for hh in (0, 2, 1, 3):
                    par = hh % 2
                    ch = 2 * g + hh // 2
                    pairs = []
                    for kb in range(max(0, blk - WINB[g]), blk + 1):
                        dl = blk - kb
                        if g <= 1:
                            ti = TIDX[(g, hh, dl)]
                        else:
                            ti = TIDX[(g, hh, "0" if dl == 0 else "m")]
                        pairs.append((kb, ti))
                    jobs.append((8 + hh, g, par, lambda kb, ch=ch: KTB[:, ch, kb * 128:(kb + 1) * 128], big[:, 4 + ch, qcol], pairs,
                                 lambda kb, g=g, hh=hh: VB[:, kb, g, hh, :], "big%d" % (4 + ch)))
            items = []
            for job in jobs:
                npair = len(job[5])
                for pi_ in range(npair):
                    items.append((job, pi_))
            import os as _os
            if _os.environ.get("KATT", "1") == "1":
                batches = []
                cur = []
                for it in items:
                    if cur and (len(cur) == 4 or cur[-1][0][2] != it[0][2]):
                        batches.append(cur)
                        cur = []
                    cur.append(it)
                if cur:
                    batches.append(cur)
            else:
                batches = []
                for job in jobs:
                    its = [(job, pi_) for pi_ in range(len(job[5]))]
                    batches.extend([its[i:i + 4] for i in range(0, len(its), 4)])

            def front(batch):
                n = len(batch)
                b = next_bank(0, 4)
                reads = set()
                for i, (job, pi_) in enumerate(batch):
                    (slot, grp, par, ktf, qap, pairs, vf, qres) = job
                    kb, ti = pairs[pi_]
                    pr = slice(par * 64, par * 64 + 64)
                    S.op("pe", lambda i=i, kb=kb, ktf=ktf, qap=qap, pr=pr: nc.tensor.matmul(
                        psB[b][:, i * 128:(i + 1) * 128], lhsT=ktf(kb)[pr, :], rhs=qap[pr, :], start=True, stop=True),
                        reads=["KTA" if grp == "A" else "KTB", qres], writes=[PB[b]], signal=(i == n - 1))
                si = sg_i[0] % 4
                sg_i[0] += 1
                S.op("act", lambda: nc.scalar.activation(out=sg[si][:, 0:n * 128], in_=psB[b][:, 0:n * 128], func=AF.Exp),
                     reads=[PB[b]], writes=[PB[b], "sg%d" % si])
                pi = pt_i[0] % 4
                pt_i[0] += 1
                e = "dve" if pt_i[0] % 2 else "pool"
                tis = [job[5][pi_][1] for (job, pi_) in batch]
                runs = []
                i = 0
                while i < n:
                    j = i
                    if j + 1 < n and tis[j + 1] == tis[i]:
                        while j + 1 < n and tis[j + 1] == tis[i]:
                            j += 1
                        in1 = dtab[:, tis[i]:tis[i] + 1, :].broadcast_to([128, j + 1 - i, 128])
                    else:
                        while j + 1 < n and tis[j + 1] == tis[j] + 1:
                            j += 1
                        in1 = dtab[:, tis[i]:tis[j] + 1, :]
                    runs.append((i, j + 1, in1))
                    i = j + 1
                for (a, z, in1) in runs:
                    ov = pT[pi][:, a * 128:z * 128].rearrange("p (n c) -> p n c", n=z - a)
                    iv = sg[si][:, a * 128:z * 128].rearrange("p (n c) -> p n c", n=z - a)
                    if e == "dve":
                        S.op("dve", lambda ov=ov, iv=iv, in1=in1: nc.vector.tensor_tensor(out=ov, in0=iv, in1=in1, op=ALU.mult),
                             reads=["sg%d" % si, "dtab"], writes=["pT%d" % pi])
                    else:
                        S.op("pool", lambda ov=ov, iv=iv, in1=in1: nc.gpsimd.tensor_tensor(out=ov, in0=iv, in1=in1, op=ALU.mult),
                             reads=["sg%d" % si, "dtab"], writes=["pT%d" % pi])
                return pi

            def back(batch, pi):
                n = len(batch)
                for i, (job, pi_) in enumerate(batch):
                    (slot, grp, par, ktf, qap, pairs, vf, qres) = job
                    kb, ti = pairs[pi_]
                    npair = len(pairs)
                    vres = "VA" if grp == "A" else "VB"
                    if grp == "A":
                        accb, acol = 4 + slot // 4, (slot % 4) * 65
                    else:
                        accb, acol = 4 + (grp % 2), (slot - 8) * 65
                    last = (pi_ == npair - 1)
                    S.op("pe", lambda i=i, kb=kb, vf=vf, accb=accb, acol=acol, pi_=pi_, last=last: nc.tensor.matmul(
                        psB[accb][:, acol:acol + 65], lhsT=pT[pi][:, i * 128:(i + 1) * 128], rhs=vf(kb),
                        start=(pi_ == 0), stop=last),
                        reads=["pT%d" % pi, vres], writes=[PB[accb]], signal=(i == n - 1 or last))
                    if not last:
                        continue
                    if grp == "A" and slot % 4 == 3:
                        hs_ = slot - 3
                        S.op("dve", lambda hs_=hs_, accb=accb: nc.vector.tensor_copy(
                            out=oacc[:, hs_:hs_ + 4, :], in_=psB[accb][:, 0:260].rearrange("p (h c) -> p h c", h=4)),
                            reads=[PB[accb]], writes=[PB[accb], "oacc"])
                    elif grp != "A" and slot == 11:
                        g = grp
                        pv = psB[accb][:, 0:260].rearrange("p (h c) -> p h c", h=4)
                        if g == 0:
                            S.op("dve", lambda pv=pv: nc.vector.tensor_copy(out=oacc[:, 8:12, :], in_=pv),
                                 reads=[PB[accb]], writes=[PB[accb], "oacc"])
                        else:
                            S.op("dve", lambda pv=pv, g=g: nc.vector.tensor_tensor(
                                out=otmp[:], in0=pv, in1=abt[:, blk, g, :, 0:1].broadcast_to([128, 4, 65]), op=ALU.mult),
                                reads=[PB[accb], "abt"], writes=[PB[accb], "otmp"])
                            S.op("pool", lambda: nc.gpsimd.tensor_tensor(out=oacc[:, 8:12, :], in0=oacc[:, 8:12, :], in1=otmp[:],
                                                                         op=ALU.add), reads=["otmp", "oacc"], writes=["oacc"])

            pend = []
            for batch in batches:
                pi = front(batch)
                pend.append((batch, pi))
                if len(pend) > ADEPTH:
                    back(*pend.pop(0))
            while pend:
                back(*pend.pop(0))

        def finish_o(P, s):
            S.op("dve", lambda: nc.vector.tensor_tensor(out=oacc[0:P, 0:8, 64:65], in0=oacc[0:P, 0:8, 64:65],
                                                        in1=esink[0:P, :].unsqueeze(2), op=ALU.add),
                 reads=["oacc", "esink"], writes=["oacc"])
            S.op("dve", lambda: nc.vector.reciprocal(out=rden[0:P, :].unsqueeze(2), in_=oacc[0:P, :, 64:65]),
                 reads=["oacc"], writes=["rden"])
            S.op("dve", lambda: nc.vector.tensor_tensor(
                out=onrm[0:P, :].rearrange("p (h d) -> p h d", h=12), in0=oacc[0:P, :, 0:HD],
                in1=rden[0:P, :].unsqueeze(2).broadcast_to([P, 12, HD]), op=ALU.mult),
                reads=["oacc", "rden"], writes=["onrm"])
            t = next_tb()
            for c in range(6):
                S.op("pe", lambda c=c: nc.tensor.transpose(out=psT[t][:, c * 128:c * 128 + P],
                                                           in_=onrm[0:P, c * 128:(c + 1) * 128], identity=idb[0:P, 0:P]),
                     reads=["onrm", "idb"], writes=[TB[t]], signal=(c == 5))
            copy_op(ew(), oT[:, :, s * 128:s * 128 + P],
                    psT[t][:, 0:768].rearrange("p (k c) -> p k c", k=6)[:, :, 0:P], [TB[t]], [TB[t], "oT"])

        def merge_out(P, nsub, NT):
            for m in range(8):
                r = next_ring()
                S.dma("sp", "ring%d" % r, ring[r][:, 0:2816], s_m[m], reads=["s_m_ga", "s_m_gb", "s_m_upa", "s_m_upb"], writes=["ring%d" % r])
                rv = ring[r][:, 0:2816].rearrange("p (k c) -> p k c", k=22)
                ga_b, gb_b = (2, 3) if m % 2 == 0 else (4, 5)
                specs = [(ga_b, 6, 8, hT, 0), (gb_b, 14, 8, hT, 0), (0, 0, 4, oT, 0), (1, 4, 2, oT, 4)]
                for (b, k0, nk, src, s0) in specs:
                    for kc in range(nk):
                        S.op("pe", lambda b=b, k0=k0, kc=kc, src=src, s0=s0, nk=nk: nc.tensor.matmul(
                            psB[b][:, 0:NT], lhsT=rv[:, k0 + kc, :], rhs=src[:, s0 + kc, 0:NT],
                            start=(kc == 0), stop=(kc == nk - 1)),
                            reads=["ring%d" % r, "oT" if src is oT else "hT"], writes=[PB[b]], signal=(kc == nk - 1))
                ia, ib = (2 * m) % 3, (2 * m + 1) % 3
                sA, sB = sg[ia], sg[ib]
                S.op("act", lambda: nc.scalar.activation(out=sA[:, 0:NT], in_=psB[ga_b][:, 0:NT], func=AF.Sigmoid),
                     reads=[PB[ga_b]], writes=[PB[ga_b], "sg%d" % ia])
                S.op("act", lambda: nc.scalar.activation(out=sB[:, 0:NT], in_=psB[gb_b][:, 0:NT], func=AF.Sigmoid),
                     reads=[PB[gb_b]], writes=[PB[gb_b], "sg%d" % ib])
                S.op("dve", lambda: nc.vector.tensor_tensor(out=sA[:, 0:NT], in0=sA[:, 0:NT], in1=psB[0][:, 0:NT], op=ALU.mult),
                     reads=["sg%d" % ia, PB[0]], writes=["sg%d" % ia, PB[0]])
                S.op("dve", lambda: nc.vector.tensor_tensor(out=sB[:, 0:NT], in0=sB[:, 0:NT], in1=psB[1][:, 0:NT], op=ALU.mult),
                     reads=["sg%d" % ib, PB[1]], writes=["sg%d" % ib, PB[1]])
                S.op("pool", lambda m=m: nc.gpsimd.tensor_tensor(out=big[:, m, 0:NT], in0=sA[:, 0:NT], in1=sB[:, 0:NT], op=ALU.add),
                     reads=["sg%d" % ia, "sg%d" % ib], writes=["big%d" % m])
            rs = []
            for ch in range(2):
                r = next_ring()
                S.dma("sp", "ring%d" % r, ring[r][:], s_o[ch].rearrange("p k c -> p (k c)"), reads=["s_o%d" % ch], writes=["ring%d" % r])
                rs.append(r)
            for s in range(nsub):
                for ch in range(2):
                    r = rs[ch]
                    rv = ring[r][:].rearrange("p (k c) -> p k c", k=8)
                    b = 4 + ch
                    for kc in range(8):
                        S.op("pe", lambda kc=kc, b=b, rv=rv: nc.tensor.matmul(
                            psB[b][0:P, :], lhsT=big[:, kc, s * 128:s * 128 + P], rhs=rv[:, kc, :], start=(kc == 0), stop=(kc == 7)),
                            reads=["big%d" % kc, "ring%d" % r], writes=[PB[b]], signal=(kc == 7))
                    S.op("dve", lambda b=b, ch=ch: nc.vector.tensor_tensor(
                        out=xt[0:P, xmap[s], ch * 512:(ch + 1) * 512], in0=psB[b][0:P, :], in1=xt[0:P, xmap[s], ch * 512:(ch + 1) * 512], op=ALU.add),
                        reads=[PB[b], "xt%d" % xmap[s]], writes=[PB[b], "xt%d" % xmap[s]])

        import os as _os2
        LOOK = int(_os2.environ.get("KLOOK", "3"))
        ADEPTH = int(_os2.environ.get("KADEPTH", "3"))
        PDEPTH = int(_os2.environ.get("KPDEPTH", "2"))
        copies = []
        if with_sample:
            copies.append((o_as[:, 0:127].rearrange("b t k h d -> b (t k h d)"), c_a[:, 1:128].rearrange("b t k h d -> b (t k h d)")))
            copies.append((o_bs[0][:, 0:127].rearrange("b t k h d -> b (t k h d)"), c_b[0][:, 1:128].rearrange("b t k h d -> b (t k h d)")))
            for bb in range(0, NS, 4):
                copies.append((o_bs[1][bb:bb + 4, 0:511].rearrange("b t k h d -> b (t k h d)"),
                               c_b[1][bb:bb + 4, 1:512].rearrange("b t k h d -> b (t k h d)")))
            for bb in range(NS):
                copies.append((o_bs[2][bb:bb + 1, 0:2047].rearrange("b t k h d -> b (t k h d)"),
                               c_b[2][bb:bb + 1, 1:2048].rearrange("b t k h d -> b (t k h d)")))

        def issue_copies(n):
            for _ in range(n):
                if copies:
                    o, i = copies.pop(0)
                    S.dma("sp", "ccopy", o, i)

        import os
        STAGE = int(os.environ.get("KSTAGE", "9"))
        NTILES = int(os.environ.get("KTILES", "8"))
        tcount = 0
        for seq in range(NSEQ):
            for tl in range(4):
                if STAGE < 1 or tcount >= NTILES:
                    continue
                tcount += 1
                blk0 = tl * 4
                xmap[:] = [(4 * (tcount - 1) + s_) % 5 for s_ in range(4)]
                xfree = (4 * (tcount - 1) + 4) % 5
                if tcount == 1:
                    for s_ in range(4):
                        S.dma("sp", "xi%d" % s_, xt[:, s_, :], x_p[seq, tl * 512 + s_ * 128:tl * 512 + (s_ + 1) * 128, :], writes=["xt%d" % s_])
                ffn(0, 128, 4, 512)
                if STAGE >= 2:
                    norm_T(128, 4, 512)
                    project_prompt(seq, blk0)
                if STAGE >= 3:
                    for s in range(4):
                        issue_copies(1 if s < 3 else 0)
                        attention_block(s, blk0 + s)
                        finish_o(128, s)
                if STAGE >= 4:
                    merge_out(128, 4, 512)
                nxt = None
                if tcount < NTILES and not (seq == NSEQ - 1 and tl == 3):
                    nxt = (seq, tl + 1) if tl < 3 else (seq + 1, 0)

                def sub_done(s_, seq=seq, tl=tl, nxt=nxt):
                    sl_ = xmap[s_]
                    S.dma("pool", "yout%d" % sl_, y_p[seq, tl * 512 + s_ * 128:tl * 512 + (s_ + 1) * 128, :], xt[:, sl_, :],
                          reads=["xt%d" % sl_])
                    if nxt is not None and s_ < 3:
                        S.dma("pool", "xt%d" % sl_, xt[:, sl_, :],
                              x_p[nxt[0], nxt[1] * 512 + (s_ + 1) * 128:nxt[1] * 512 + (s_ + 2) * 128, :], writes=["xt%d" % sl_])

                if nxt is not None:
                    S.dma("pool", "xt%d" % xfree, xt[:, xfree, :], x_p[nxt[0], nxt[1] * 512:nxt[1] * 512 + 128, :], writes=["xt%d" % xfree])
                if STAGE >= 5:
                    ffn(1, 128, 4, 512, sub_done)
                else:
                    for s_ in range(4):
                        sub_done(s_)

        issue_copies(len(copies))
        if not with_sample:
            hs.close()
        if with_sample:
            S.barrier()
            hs.close()
            ss_ = ExitStack()
            es.enter_context(ss_)
            zs = sb("zs", [128, 3072], F32, ss_)
            sq = sb("sq", [128, 256], F32, ss_)
            qn = sb("qn", [128, 256], F32, ss_)
            qb16 = sb("qb16", [128, 256], BF16, ss_)
            zs_box[0] = zs
            KVb = sb("KVb", [128, 17 * 512], F32, ss_)
            qrep = sb("qrep", [128, 512], F32, ss_)
            part = sb("part", [128, 8 * 65], F32, ss_)
            ssc2 = sb("ssc2", [128, 17 * 8], F32, ss_)
            pexp = sb("pexp", [128, 17 * 8], F32, ss_)
            sal = sb("sal", [128, 4, 17 * 8], F32, ss_)
            rept = sb("rept", [128, 128], F32, ss_)
            sel = sb("sel", [128, 16], F32, ss_)
            S.dma("sp", "rept", rept[0:16, :], c_rep, writes=["rept"])
            S.dma("sp", "sel", sel[:], c_sel, writes=["sel"])
            S.dma("sp", "sal", sal[:], c_sal, writes=["sal"])
            xmap[:] = [0, 1, 2, 3]
            S.dma("sp", "xi0", xt[0:NS, 0, :], x_s, writes=["xt0"])
            ffn(0, NS, 1, NS)
            norm_T(NS, 1, NS)
            project(NS, 1, NS, None, 0)
            S.dma("sp", "nrow", o_as[:, 127, 0, :, :].rearrange("b h d -> b (h d)"), zs[0:NS, 512:640], reads=["zs"])
            S.dma("sp", "nrow", o_as[:, 127, 1, :, :].rearrange("b h d -> b (h d)"), zs[0:NS, 640:768], reads=["zs"])
            for g in range(3):
                W = [128, 512, 2048][g]
                S.dma("sp", "nrow", o_bs[g][:, W - 1, 0, :, :].rearrange("b h d -> b (h d)"),
                      zs[0:NS, 1536 + g * 256:1536 + (g + 1) * 256], reads=["zs"])
                S.dma("sp", "nrow", o_bs[g][:, W - 1, 1, :, :].rearrange("b h d -> b (h d)"),
                      zs[0:NS, 2304 + g * 256:2304 + (g + 1) * 256], reads=["zs"])

            def sgroup(G, cache, dil, Hkv, Hq, qc0, kc0, vc0):
                RL = 2 * Hkv * HD
                KW = Hkv * HD
                kvv = KVb[:, 0:17 * RL].rearrange("p (j r) -> p j r", r=RL)
                S.op("pool", lambda: nc.gpsimd.memset(kvv[:, 16, :], 0.0), writes=["KVb"])
                src = cache.rearrange("b (jc jj r) k h d -> (b jc) jj r (k h d)", jc=8, jj=16, r=dil)[:, :, 0, :]
                S.dma("sp", "KVb", kvv[:, 0:16, :], src, writes=["KVb"], disjoint=True)
                k7 = KVb[7:128:8, :]
                S.dma("sp", "KVb", k7[:, 16 * RL:16 * RL + KW], zs[0:NS, kc0:kc0 + KW], reads=["zs"], writes=["KVb"], disjoint=True)
                S.dma("sp", "KVb", k7[:, 16 * RL + KW:17 * RL], zs[0:NS, vc0:vc0 + KW], reads=["zs"], writes=["KVb"], disjoint=True)
                W = Hq * HD
                S.op("pe", lambda: nc.tensor.matmul(psB[0][:, 0:W], lhsT=rept[0:NS, :], rhs=zs[0:NS, qc0:qc0 + W], start=True, stop=True),
                     reads=["rept", "zs"], writes=[PB[0]])
                S.op("act", lambda: nc.scalar.copy(out=qrep[:, 0:W], in_=psB[0][:, 0:W]), reads=[PB[0]], writes=[PB[0], "qrep"])
                sc3 = ssc2[:, 0:17 * Hq].rearrange("p (j h) -> p j h", h=Hq)
                pe3 = pexp[:, 0:17 * Hq].rearrange("p (j h) -> p j h", h=Hq)
                part3 = part[:, 0:Hq * 65].rearrange("p (h c) -> p h c", c=65)
                if G != 0:
                    K4 = kvv[:, :, 0:KW]
                    S.op("dve", lambda: nc.vector.tensor_tensor(out=K4, in0=K4, in1=qrep[:, 0:KW].unsqueeze(1).broadcast_to([128, 17, KW]),
                                                                op=ALU.mult), reads=["KVb", "qrep"], writes=["KVb"])
                    S.op("dve", lambda: nc.vector.tensor_reduce(out=sc3, in_=K4.rearrange("p j (h d) -> p j h d", h=Hkv), axis=AX.X, op=ALU.add),
                         reads=["KVb"], writes=["ssc2"])
                else:
                    prodA = KVb[:, 17 * RL:17 * RL + 17 * 256].rearrange("p (j h d) -> p j h d", h=4, d=HD)
                    for kvh in range(2):
                        S.op("dve", lambda kvh=kvh: nc.vector.tensor_tensor(
                            out=prodA, in0=kvv[:, :, kvh * HD:(kvh + 1) * HD].unsqueeze(2).broadcast_to([128, 17, 4, HD]),
                            in1=qrep[:, kvh * 256:(kvh + 1) * 256].rearrange("p (h d) -> p h d", h=4).unsqueeze(1).broadcast_to([128, 17, 4, HD]),
                            op=ALU.mult), reads=["KVb", "qrep"], writes=["KVb"])
                        S.op("dve", lambda kvh=kvh: nc.vector.tensor_reduce(out=sc3[:, :, kvh * 4:(kvh + 1) * 4], in_=prodA, axis=AX.X, op=ALU.add),
                             reads=["KVb"], writes=["ssc2"])
                S.op("dve", lambda: nc.vector.tensor_tensor(out=ssc2[:, 0:17 * Hq], in0=ssc2[:, 0:17 * Hq], in1=sal[:, G, 0:17 * Hq], op=ALU.add),
                     reads=["ssc2", "sal"], writes=["ssc2"])
                S.op("act", lambda: nc.scalar.activation(out=pexp[:, 0:17 * Hq], in_=ssc2[:, 0:17 * Hq], func=AF.Exp),
                     reads=["ssc2"], writes=["pexp"])
                S.op("dve", lambda: nc.vector.tensor_reduce(out=part3[:, :, 64], in_=pe3.rearrange("p j h -> p h j"), axis=AX.X, op=ALU.add),
                     reads=["pexp"], writes=["part"])
                if G != 0:
                    V4 = kvv[:, :, KW:2 * KW].rearrange("p j (h d) -> p j h d", h=Hkv)
                    S.op("dve", lambda: nc.vector.tensor_tensor(out=V4, in0=V4, in1=pe3.unsqueeze(3).broadcast_to([128, 17, Hq, HD]), op=ALU.mult),
                         reads=["KVb", "pexp"], writes=["KVb"])
                    S.op("dve", lambda: nc.vector.tensor_reduce(out=part3[:, :, 0:HD], in_=V4.rearrange("p j h d -> p h d j"), axis=AX.X, op=ALU.add),
                         reads=["KVb"], writes=["part"])
                else:
                    for kvh in range(2):
                        S.op("dve", lambda kvh=kvh: nc.vector.tensor_tensor(
                            out=prodA, in0=kvv[:, :, KW + kvh * HD:KW + (kvh + 1) * HD].unsqueeze(2).broadcast_to([128, 17, 4, HD]),
                            in1=pe3[:, :, kvh * 4:(kvh + 1) * 4].unsqueeze(3).broadcast_to([128, 17, 4, HD]), op=ALU.mult),
                            reads=["KVb", "pexp"], writes=["KVb"])
                        S.op("dve", lambda kvh=kvh: nc.vector.tensor_reduce(out=part3[:, kvh * 4:(kvh + 1) * 4, 0:HD],
                                                                            in_=prodA.rearrange("p j h d -> p h d j"), axis=AX.X, op=ALU.add),
                             reads=["KVb"], writes=["part"])
                if G == 0:
                    for hf in range(2):
                        S.op("pe", lambda hf=hf: nc.tensor.matmul(psB[4 + hf][0:NS, 0:260], lhsT=sel[:, 0:NS], rhs=part[:, hf * 260:(hf + 1) * 260],
                                                                  start=True, stop=True), reads=["sel", "part"], writes=[PB[4 + hf]])
                        S.op("dve", lambda hf=hf: nc.vector.tensor_copy(out=oacc[0:NS, hf * 4:(hf + 1) * 4, :],
                                                                        in_=psB[4 + hf][0:NS, 0:260].rearrange("p (h c) -> p h c", c=65)),
                             reads=[PB[4 + hf]], writes=[PB[4 + hf], "oacc"])
                else:
                    S.op("pe", lambda: nc.tensor.matmul(psB[3][0:NS, 0:260], lhsT=sel[:, 0:NS], rhs=part[:, 0:260],
                                                        start=(G == 1), stop=(G == 3)), reads=["sel", "part"], writes=[PB[3]])
                    if G == 3:
                        S.op("dve", lambda: nc.vector.tensor_copy(out=oacc[0:NS, 8:12, :],
                                                                  in_=psB[3][0:NS, 0:260].rearrange("p (h c) -> p h c", c=65)),
                             reads=[PB[3]], writes=[PB[3], "oacc"])

            sgroup(0, c_a, 1, 2, 8, 0, 512, 640)
            for g in range(3):
                sgroup(1 + g, c_b[g], DILS[g], 4, 4, 768 + g * 256, 1536 + g * 256, 2304 + g * 256)
            finish_o(NS, 0)
            merge_out(NS, 1, NS)
            ffn(1, NS, 1, NS)
            S.dma("pool", "yout0", y_s, xt[0:NS, 0, :], reads=["xt0"])

        S.finish()
        print("ops", S.nops, "waits", S.nwaits, "cnt", S.cnt, "dma sems", len(S.dsem))
    sal_np = np.zeros((128, 4, 17 * 8), np.float64)
    for p in range(128):
        jc = p % 8
        for G in range(4):
            Hq = 8 if G == 0 else 4
            dil = 1 if G == 0 else DILS[G - 1]
            for jj in range(17):
                for h in range(Hq):
                    sl = SLOPES[h] if G == 0 else SLOPES[8 + 4 * (G - 1) + h]
                    if jj < 16:
                        v = -sl * dil * (128 - (jc * 16 + jj))
                    else:
                        v = 0.0 if jc == 7 else -30000.0
                    sal_np[p, G, jj * Hq + h] = v
    rep_np = np.zeros((16, 128), np.float32)
    for p in range(128):
        rep_np[p // 8, p] = 1.0
    consts = {"c_ident": ident_np, "c_dt": dt_np, "c_ab": ab_np.reshape(128, -1), "c_sal": sal_np.astype(np.float32), "c_rep": rep_np,
              "c_sel": np.ascontiguousarray(rep_np.T)}
    return nc, consts


_CACHE = {}


def kernel(**inputs):
    import os
    WS = os.environ.get("KNOSAMPLE", "0") != "1"
    if "prog" not in _CACHE:
        _CACHE["prog"] = build_program(WS)
    nc, consts = _CACHE["prog"]
    f = lambda k: np.ascontiguousarray(np.asarray(inputs[k], dtype=np.float32))
    xp = f("x_prompt"); xs = f("x_sample").reshape(128, D)
    ca = f("cache_a_kv")[0]; cb1 = f("cache_b1_kv")[0]; cb2 = f("cache_b2_kv")[0]; cb3 = f("cache_b3_kv")[0]
    shared = {
        "norm_ffn1": f("norm_ffn1")[0], "norm_mix": f("norm_mix")[0], "norm_ffn2": f("norm_ffn2")[0],
        "w1_gate": f("w1_gate")[0], "w1_up": f("w1_up")[0], "w1_down": f("w1_down")[0],
        "w2_gate": f("w2_gate")[0], "w2_up": f("w2_up")[0], "w2_down": f("w2_down")[0],
        "w_in": f("w_in")[0], "q_norm_a": f("q_norm_a")[0], "k_norm_a": f("k_norm_a")[0],
        "q_norm_b": f("q_norm_b")[0], "k_norm_b": f("k_norm_b")[0], "sinks_a": f("sinks_a")[0].reshape(8),
        "w_up_a": f("w_up_a")[0], "w_up_b": f("w_up_b")[0], "w_o": f("w_o")[0],
    }
    shared.update(consts)
    in_maps = []
    for c in range(NCORES):
        m = dict(shared)
        m["x_prompt"] = xp[c * NSEQ:(c + 1) * NSEQ]
        m["x_sample"] = xs[c * NS:(c + 1) * NS]
        if WS:
            m["cache_a"] = ca[c * NS:(c + 1) * NS]
            m["cache_b1"] = cb1[c * NS:(c + 1) * NS]
            m["cache_b2"] = cb2[c * NS:(c + 1) * NS]
            m["cache_b3"] = cb3[c * NS:(c + 1) * NS]
        in_maps.append(m)
    res = run_bass_kernel_spmd(nc, in_maps, core_ids=list(range(NCORES)))
    R = res.results
    cat = lambda k: np.concatenate([np.asarray(r[k]) for r in R], axis=0)
    y_prompt = cat("y_prompt")
    y_sample = cat("y_sample").reshape(128, 1, D)
    outs = [y_prompt, y_sample]
    for k in ["a_p", "b1_p", "b2_p", "b3_p"] + (["a_s", "b1_s", "b2_s", "b3_s"] if WS else []):
        outs.append(cat(k)[None])
    return tuple(o.astype(np.float32) for o in outs)
```

```python
import numpy as np
from contextlib import ExitStack
import concourse.bass as bass
import concourse.mybir as mybir
from concourse.bass_utils import run_bass_kernel_spmd

F32 = mybir.dt.float32
BF16 = mybir.dt.bfloat16
AF = mybir.ActivationFunctionType
ALU = mybir.AluOpType
AX = mybir.AxisListType

NCORES = 8
D = 1024
DFF = 2816
NF = 22
SEQ = 2048
NSEQ = 2
HD = 64
EPS = 1e-6
NS = 16
NSTG = 4
N_AL = 20
SLOPES = [2.0 ** (-8.0 * i / N_AL) for i in range(1, N_AL + 1)]
DILS = [1, 4, 16]
WINB = [1, 4, 16]


class Sched:
    def __init__(self, nc, es):
        self.nc = nc
        self.es = es
        self.eng = {"pe": nc.tensor, "act": nc.scalar, "dve": nc.vector, "pool": nc.gpsimd, "sp": nc.sync}
        self.sem = {e: es.enter_context(nc.semaphore("s_" + e)) for e in ["pe", "act", "dve", "pool"]}
        self.cnt = {e: 0 for e in self.sem}
        self.pending = {e: [] for e in self.sem}
        self.lastw = {}
        self.readers = {}
        self.seen = {e: {} for e in self.eng}
        self.dsem = {}
        self.nwaits = 0
        self.nops = {e: 0 for e in self.eng}

    def _need(self, reads, writes):
        toks = []
        for r in reads:
            t = self.lastw.get(r)
            if t is not None:
                toks.append(t)
        for w in writes:
            t = self.lastw.get(w)
            if t is not None:
                toks.append(t)
            toks.extend(self.readers.get(w, []))
        return toks

    def _emit_waits(self, e, toks):
        best = {}
        for (own, sem, val) in toks:
            if own == e and (val is None or e == "pe"):
                continue
            if val is None:
                raise RuntimeError("dependency on unsignalled op")
            k = sem.name
            if self.seen[e].get(k, 0) >= val:
                continue
            if k not in best or best[k][1] < val:
                best[k] = (sem, val)
        for k, (sem, val) in best.items():
            self.eng[e].wait_ge(sem, val)
            self.seen[e][k] = val
            self.nwaits += 1

    def _record(self, tok, reads, writes):
        for r in reads:
            lst = self.readers.setdefault(r, [])
            lst[:] = [t for t in lst if t[1].name != tok[1].name]
            lst.append(tok)
        for w in writes:
            self.lastw[w] = tok
            self.readers[w] = []

    def op(self, e, fn, reads=(), writes=(), signal=True):
        toks = self._need(reads, writes)
        self._emit_waits(e, toks)
        inst = fn()
        self.nops[e] += 1
        if signal:
            self.cnt[e] += 1
            inst.then_inc(self.sem[e], 1)
            tok = (e, self.sem[e], self.cnt[e])
            for (rs, ws) in self.pending[e]:
                self._record(tok, rs, ws)
            self.pending[e] = []
            self._record(tok, reads, writes)
        else:
            self.pending[e].append((tuple(reads), tuple(writes)))
            bad = (e, self.sem[e], None)
            for w in writes:
                self.lastw[w] = bad
                self.readers[w] = []
            for r in reads:
                self.readers.setdefault(r, []).append(bad)
        return inst

    def dma(self, q, slot, out, in_, reads=(), writes=(), disjoint=False, **kw):
        if slot not in self.dsem:
            self.dsem[slot] = [self.es.enter_context(self.nc.semaphore("d_" + slot)), 0]
        ds = self.dsem[slot]
        toks = self._need(reads, writes)
        if disjoint:
            toks = [t for t in toks if t[1].name != ds[0].name]
        self._emit_waits(q, toks)
        ds[1] += 16
        self.eng[q].dma_start(out=out, in_=in_, **kw).then_inc(ds[0], 16)
        self.nops[q] += 1
        tok = ("dma", ds[0], ds[1])
        self._record(tok, reads, writes)
        return tok

    def barrier(self):
        for e in ["pe", "act", "dve", "pool", "sp"]:
            for f in ["pe", "act", "dve", "pool"]:
                if f != e and self.cnt[f] > self.seen[e].get(self.sem[f].name, 0):
                    self.eng[e].wait_ge(self.sem[f], self.cnt[f])
                    self.seen[e][self.sem[f].name] = self.cnt[f]
            for slot, (sem, val) in self.dsem.items():
                if val > self.seen[e].get(sem.name, 0):
                    self.eng[e].wait_ge(sem, val)
                    self.seen[e][sem.name] = val

    def finish(self):
        for e in ["sp"]:
            for slot, (sem, val) in self.dsem.items():
                if val > 0:
                    self.eng[e].wait_ge(sem, val)
            for f in ["pe", "act", "dve", "pool"]:
                if self.cnt[f] > 0:
                    self.eng[e].wait_ge(self.sem[f], self.cnt[f])


def host_tables():
    ident = np.eye(128, dtype=np.float32)
    k = np.arange(128)[:, None].astype(np.float64)
    q = np.arange(128)[None, :].astype(np.float64)
    tabs = []
    idx = {}
    for h in range(8):
        s = SLOPES[h]
        idx[("A", h, 1)] = len(tabs); tabs.append(np.where(q <= k, np.exp(-s * (128 + q - k)), 0.0))
        idx[("A", h, 0)] = len(tabs); tabs.append(np.where(q >= k, np.exp(-s * (q - k)), 0.0))
    for g in range(3):
        dil = DILS[g]
        mod = ((q - k) % dil) == 0
        for h in range(4):
            s = SLOPES[8 + g * 4 + h]
            if g == 0:
                idx[(g, h, 1)] = len(tabs); tabs.append(np.where(q <= k, np.exp(-s * (128 + q - k)), 0.0))
                idx[(g, h, 0)] = len(tabs); tabs.append(np.where(q >= k, np.exp(-s * (q - k)), 0.0))
            elif g == 1:
                idx[(g, h, 4)] = len(tabs); tabs.append(np.where((q <= k) & mod, np.exp(-s * (q - k)), 0.0))
                for dl in (3, 2, 1):
                    idx[(g, h, dl)] = len(tabs); tabs.append(np.where(mod, np.exp(-s * (q - k)), 0.0))
                idx[(g, h, 0)] = len(tabs); tabs.append(np.where((q >= k) & mod, np.exp(-s * (q - k)), 0.0))
            else:
                idx[(g, h, "m")] = len(tabs); tabs.append(np.where(mod, np.exp(-s * (q - k)), 0.0))
                idx[(g, h, "0")] = len(tabs); tabs.append(np.where((q >= k) & mod, np.exp(-s * (q - k)), 0.0))
    dt = np.stack(tabs, axis=1).astype(np.float32)
    ab = np.ones((128, 16, 3, 4, 2), np.float64)
    for g in (1, 2):
        for h in range(4):
            s = SLOPES[8 + g * 4 + h]
            for b in range(16):
                ab[:, b, g, h, 0] = np.exp(-s * 128.0 * b)
                ab[:, b, g, h, 1] = np.exp(s * 128.0 * b)
    return ident, dt, idx, ab.astype(np.float32)


def build_program(with_sample=True):
    ident_np, dt_np, TIDX, ab_np = host_tables()
    NTAB = dt_np.shape[1]
    nc = bass.Bass("TRN2", target_bir_lowering=False)

    def din(name, shape, dt=F32):
        return nc.dram_tensor(name, list(shape), dt, kind="ExternalInput").ap()

    def dout(name, shape):
        return nc.dram_tensor(name, list(shape), F32, kind="ExternalOutput").ap()

    def dscr(name, shape, dt=BF16):
        return nc.dram_tensor(name, list(shape), dt, kind="Internal").ap()

    x_p = din("x_prompt", [NSEQ, SEQ, D])
    x_s = din("x_sample", [NS, D])
    if with_sample:
        c_a = din("cache_a", [NS, 128, 2, 2, HD])
        c_b = [din("cache_b1", [NS, 128, 2, 4, HD]), din("cache_b2", [NS, 512, 2, 4, HD]),
               din("cache_b3", [NS, 2048, 2, 4, HD])]
    g_f1 = din("norm_ffn1", [D]); g_mx = din("norm_mix", [D]); g_f2 = din("norm_ffn2", [D])
    w_g = [din("w1_gate", [D, DFF]), din("w2_gate", [D, DFF])]
    w_u = [din("w1_up", [D, DFF]), din("w2_up", [D, DFF])]
    w_d = [din("w1_down", [DFF, D]), din("w2_down", [DFF, D])]
    w_in = din("w_in", [D, 5120])
    qn_a = din("q_norm_a", [HD]); kn_a = din("k_norm_a", [HD]); qn_b = din("q_norm_b", [HD]); kn_b = din("k_norm_b", [HD])
    sinks = din("sinks_a", [8])
    w_upa = din("w_up_a", [512, D]); w_upb = din("w_up_b", [256, D]); w_o = din("w_o", [D, D])
    c_ident = din("c_ident", [128, 128]); c_dt = din("c_dt", [128, NTAB, 128]); c_ab = din("c_ab", [128, 16 * 3 * 4 * 2])
    c_sal = din("c_sal", [128, 4, 17 * 8])
    c_rep = din("c_rep", [16, 128]); c_sel = din("c_sel", [128, 16])
    y_p = dout("y_prompt", [NSEQ, SEQ, D]); y_s = dout("y_sample", [NS, D])
    o_ap = dout("a_p", [NSEQ, 128, 2, 2, HD])
    o_bp = [dout("b1_p", [NSEQ, 128, 2, 4, HD]), dout("b2_p", [NSEQ, 512, 2, 4, HD]), dout("b3_p", [NSEQ, 2048, 2, 4, HD])]
    if with_sample:
        o_as = dout("a_s", [NS, 128, 2, 2, HD])
        o_bs = [dout("b1_s", [NS, 128, 2, 4, HD]), dout("b2_s", [NS, 512, 2, 4, HD]), dout("b3_s", [NS, 2048, 2, 4, HD])]
    s_gu = [dscr("s_gu1", [11, 128, 2, 2, 8, 128]), dscr("s_gu2", [11, 128, 2, 2, 8, 128])]
    s_d = [dscr("s_d1", [NF, 128, D]), dscr("s_d2", [NF, 128, D])]
    s_in = dscr("s_in", [6, 128, 8, 512])
    s_m = dscr("s_m", [8, 128, 22 * 128])
    s_o = dscr("s_o", [2, 128, 8, 512])

    es = ExitStack()
    with es:
        S = Sched(nc, es)

        def sb(name, shape, dt, stack=es):
            return stack.enter_context(nc.sbuf_tensor(name, list(shape), dt))

        psT = [es.enter_context(nc.psum_tensor("psT%d" % i, [128, 1024], BF16)) for i in range(2)]
        psB = [es.enter_context(nc.psum_tensor("psB%d" % i, [128, 512], F32)) for i in range(6)]
        PB = ["P%d" % i for i in range(6)]
        TB = ["T0", "T1"]

        idb = sb("idb", [128, 128], BF16)
        dtab = sb("dtab", [128, NTAB, 128], BF16)
        abt = sb("abt", [128, 16, 3, 4, 2], F32)
        gq_a = sb("gq_a", [128, HD], F32); gk_a = sb("gk_a", [128, HD], F32)
        gq_b = sb("gq_b", [128, HD], F32); gk_b = sb("gk_b", [128, HD], F32)
        esink = sb("esink", [128, 8], F32)
        mhalf = sb("mhalf", [128, 8], F32)
        small = sb("small", [128, 96], F32)

        S.op("pool", lambda: nc.gpsimd.memset(mhalf[:], -0.5), writes=["mhalf"])
        gcol = sb("gcol", [128, 4], F32)
        for ci, v in enumerate([qn_a, kn_a, qn_b, kn_b]):
            for hf in range(2):
                S.dma("sp", "gcol", gcol[hf * 64:(hf + 1) * 64, ci:ci + 1], v.rearrange("(d o) -> d o", o=1), writes=["gcol"],
                      allow_slow_non_contiguous=True)
        S.op("act", lambda: nc.scalar.mul(out=gcol[:, 0:1], in_=gcol[:, 0:1], mul=HD ** -0.5), reads=["gcol"], writes=["gcol"])
        S.op("act", lambda: nc.scalar.mul(out=gcol[:, 2:3], in_=gcol[:, 2:3], mul=HD ** -0.5), reads=["gcol"], writes=["gcol"])

        def bcast_load(dst, src, n, name):
            S.dma("sp", name, dst, src.rearrange("(o n) -> o n", o=1).broadcast_to([128, n]), writes=[name])

        bcast_load(gq_a[:], qn_a, HD, "gq_a"); bcast_load(gk_a[:], kn_a, HD, "gk_a")
        bcast_load(gq_b[:], qn_b, HD, "gq_b"); bcast_load(gk_b[:], kn_b, HD, "gk_b")
        bcast_load(esink[:], sinks, 8, "esink")
        S.dma("sp", "abt", abt[:].rearrange("p a b c d -> p (a b c d)"), c_ab, writes=["abt"])
        S.op("act", lambda: nc.scalar.mul(out=gq_a[:], in_=gq_a[:], mul=HD ** -0.5), reads=["gq_a"], writes=["gq_a"])
        S.op("act", lambda: nc.scalar.mul(out=gq_b[:], in_=gq_b[:], mul=HD ** -0.5), reads=["gq_b"], writes=["gq_b"])
        S.op("act", lambda: nc.scalar.activation(out=esink[:], in_=esink[:], func=AF.Exp), reads=["esink"], writes=["esink"])

        with ExitStack() as ps:
            stg_in = [sb("stg_in%d" % i, [128, 8, 1024], F32, ps) for i in range(2)]
            stg_out = [sb("stg_out%d" % i, [128, 8, 1024], BF16, ps) for i in range(2)]
            gains = sb("gains", [128, 3, 8], F32, ps)
            dt32 = sb("dt32", [128, NTAB, 128], F32, ps)
            id32 = sb("id32", [128, 128], F32, ps)
            for i, g in enumerate([g_f1, g_mx, g_f2]):
                S.dma("sp", "gains", gains[:, i, :], g.rearrange("(c p) -> p c", p=128), writes=["gains"],
                      allow_slow_non_contiguous=True)
            S.dma("sp", "id32", id32[:], c_ident, writes=["id32"])
            S.dma("sp", "dt32", dt32[:], c_dt, writes=["dt32"])
            S.op("dve", lambda: nc.vector.tensor_copy(out=idb[:], in_=id32[:]), reads=["id32"], writes=["idb"])
            S.op("dve", lambda: nc.vector.tensor_copy(out=dtab[:], in_=dt32[:]), reads=["dt32"], writes=["dtab"])

            pcount = [0]
            def block(src_ap, nk, n, gain_idx, perm_f, stores):
                i = pcount[0] % 2
                pcount[0] += 1
                tin = stg_in[i][:, 0:nk, 0:n]
                S.dma("sp", "stg_in%d" % i, tin, src_ap, writes=["stg_in%d" % i])
                flat = stg_out[i][:].rearrange("p k c -> p (k c)")[:, 0:nk * n]
                if perm_f:
                    f = n // 128
                    ov_all = flat.rearrange("p (f k c) -> p k f c", f=f, k=nk)
                    iv_all = tin.rearrange("p k (f c) -> p k f c", c=128)
                else:
                    ov_all = flat.rearrange("p (k c) -> p k c", k=nk)
                    iv_all = tin
                if gain_idx is not None:
                    for kc in range(nk):
                        S.op("act", lambda kc=kc: nc.scalar.activation(out=ov_all[:, kc], in_=iv_all[:, kc], func=AF.Copy,
                                                                       scale=gains[:, gain_idx, kc:kc + 1]),
                             reads=["stg_in%d" % i, "gains"], writes=["stg_out%d" % i], signal=(kc == nk - 1))
                else:
                    e = "dve" if pcount[0] % 2 else "pool"
                    if e == "dve":
                        S.op("dve", lambda: nc.vector.tensor_copy(out=ov_all, in_=iv_all), reads=["stg_in%d" % i], writes=["stg_out%d" % i])
                    else:
                        S.op("pool", lambda: nc.gpsimd.tensor_copy(out=ov_all, in_=iv_all), reads=["stg_in%d" % i], writes=["stg_out%d" % i])
                for (dst, src, res) in stores(flat):
                    S.dma("pool", "stg_out%d" % i, dst, src, reads=["stg_out%d" % i], writes=[res], disjoint=True)

            def wsrc(w, c0, ncol, nk):
                return w.rearrange("(c p) n -> p c n", p=128)[:, 0:nk, c0:c0 + ncol]

            for l in range(2):
                gi = 0 if l == 0 else 2
                for c0, n in ((0, 1024), (1024, 1024), (2048, 768)):
                    for gu, w in enumerate([w_g[l], w_u[l]]):
                        def st(flat, c0=c0, n=n, gu=gu, l=l):
                            out = []
                            for q_ in range(n // 256):
                                fp = c0 // 256 + q_
                                out.append((s_gu[l][fp, :, :, gu, :, :].rearrange("p f k c -> p f (k c)"),
                                            flat[:, q_ * 2048:(q_ + 1) * 2048].rearrange("p (f x) -> p f x", f=2),
                                            "s_gu%d_%d_%d" % (l, fp, gu)))
                            return out
                        block(wsrc(w, c0, n, 8), 8, n, gi, True, st)
                for f0, nf in ((0, 8), (8, 8), (16, 6)):
                    def st(flat, f0=f0, nf=nf, l=l):
                        return [(s_d[l][f0:f0 + nf].rearrange("f p c -> p f c"), flat.rearrange("p (f c) -> p f c", f=nf),
                                 "s_d%d_%d" % (l, f0))]
                    block(w_d[l].rearrange("(f p) c -> p f c", p=128)[:, f0:f0 + nf, :], nf, 1024, None, False, st)
            for bk in range(3):
                def st(flat, bk=bk):
                    v = flat.rearrange("p (k c) -> p k c", k=8)
                    return [(s_in[2 * bk + hf], v[:, :, hf * 512:(hf + 1) * 512], "s_in%d" % (2 * bk + hf)) for hf in range(2)]
                block(wsrc(w_in, bk * 1024, 1024, 8), 8, 1024, 1, False, st)
            for which, c0, off in [("ga", 3072, 6 * 128), ("gb", 4096, 14 * 128)]:
                def st(flat, off=off, which=which):
                    return [(s_m[:, :, off:off + 1024].rearrange("m p c -> p m c"), flat.rearrange("p (m x) -> p m x", m=8), "s_m_" + which)]
                block(wsrc(w_in, c0, 1024, 8), 8, 1024, 1, True, st)
            def st(flat):
                return [(s_m[:, :, 0:512].rearrange("m p c -> p m c"), flat.rearrange("p (m x) -> p m x", m=8), "s_m_upa")]
            block(wsrc(w_upa, 0, 1024, 4), 4, 1024, None, True, st)
            def st(flat):
                return [(s_m[:, :, 512:768].rearrange("m p c -> p m c"), flat.rearrange("p (m x) -> p m x", m=8), "s_m_upb")]
            block(wsrc(w_upb, 0, 1024, 2), 2, 1024, None, True, st)
            def st(flat):
                v = flat.rearrange("p (k c) -> p k c", k=8)
                return [(s_o[ch], v[:, :, ch * 512:(ch + 1) * 512], "s_o%d" % ch) for ch in range(2)]
            block(wsrc(w_o, 0, 1024, 8), 8, 1024, None, False, st)
            S.barrier()

        ms = ExitStack()
        es.enter_context(ms)
        xt = sb("xt", [128, 5, D], F32, ms)
        xmap = [0, 1, 2, 3]
        hb = sb("hb", [128, D], BF16, ms)
        hT = sb("hT", [128, 8, 512], BF16, ms)
        big = sb("big", [128, 11, 512], BF16, ms)
        wd = sb("wd", [128, 11, D], BF16, ms)
        ring = [sb("ring%d" % i, [128, 4096], BF16, ms) for i in range(4)]
        sg = [sb("sg%d" % i, [128, 512], F32, ms) for i in range(2)]
        hs = ExitStack()
        stg = [sb("stg%d" % i, [128, 256], F32, ms) for i in range(NSTG)]
        pT = [sb("pT%d" % i, [128, 512], BF16, ms) for i in range(2)]
        oacc = sb("oacc", [128, 12, 65], F32, ms)
        otmp = sb("otmp", [128, 4, 65], F32, ms)
        onrm = sb("onrm", [128, 768], BF16, ms)
        oT = sb("oT", [128, 6, 512], BF16, ms)
        rden = sb("rden", [128, 12], F32, ms)
        hb2 = [hb, sb("hb1", [128, D], BF16, ms)]
        sg.append(sb("sg2", [128, 512], F32, ms))
        pT.append(sb("pT2", [128, 512], BF16, ms))
        sg.append(sb("sg3", [128, 512], F32, ms))
        pT.append(sb("pT3", [128, 512], BF16, ms))
        sqb = [sb("sqb%d" % i, [128, 512], F32, hs) for i in range(3)]
        qbb = [sb("qbb%d" % i, [128, 512], BF16, hs) for i in range(3)]
        KTA = sb("KTA", [128, 1, SEQ], BF16, hs)
        VA = sb("VA", [128, 16, 2, 65], BF16, hs)
        KTB = sb("KTB", [128, 6, SEQ], BF16, hs)
        VB = sb("VB", [128, 16, 3, 4, 65], BF16, hs)
        zs_box = [None]

        S.op("dve", lambda: nc.vector.memset(VA[:].rearrange("p a b c -> p (a b c)"), 1.0), writes=["VA"])
        S.op("dve", lambda: nc.vector.memset(VB[:].rearrange("p a b c d -> p (a b c d)"), 1.0), writes=["VB"])
        for g_ in (1, 2):
            S.op("dve", lambda g_=g_: nc.vector.tensor_copy(out=VB[:, :, g_, :, HD:HD + 1], in_=abt[:, :, g_, :, 1:2]),
                 reads=["abt"], writes=["VB"])

        ring_i = [0]
        st_i = [0]
        bank_i = [0]
        tb_i = [0]
        sg_i = [0]
        pt_i = [0]

        def next_ring():
            i = ring_i[0] % 4
            ring_i[0] += 1
            return i

        def next_bank(lo=0, hi=4):
            i = lo + bank_i[0] % (hi - lo)
            bank_i[0] += 1
            return i

        def next_tb():
            i = tb_i[0] % 2
            tb_i[0] += 1
            return i

        ew_i = [0]

        def ew():
            ew_i[0] += 1
            return "dve" if ew_i[0] % 2 else "act"

        def copy_op(e, out, in_, reads, writes):
            if e == "act":
                S.op("act", lambda: nc.scalar.copy(out=out, in_=in_), reads=reads, writes=writes)
            elif e == "dve":
                S.op("dve", lambda: nc.vector.tensor_copy(out=out, in_=in_), reads=reads, writes=writes)
            else:
                S.op("pool", lambda: nc.gpsimd.tensor_copy(out=out, in_=in_), reads=reads, writes=writes)

        def norm_T(P, nsub, NT):
            for s in range(nsub):
                c = 32 + 3 * s
                S.op("act", lambda: nc.scalar.activation(out=hb2[s % 2][0:P, :], in_=xt[0:P, xmap[s], :], func=AF.Square,
                                                         accum_out=small[0:P, c:c + 1]),
                     reads=["xt%d" % xmap[s]], writes=["hb%d" % (s % 2), "nst%d" % s])
                S.op("pool", lambda: nc.gpsimd.tensor_scalar(out=small[0:P, c + 1:c + 2], in0=small[0:P, c:c + 1], scalar1=1.0 / D,
                                                             scalar2=EPS, op0=ALU.mult, op1=ALU.add),
                     reads=["nst%d" % s], writes=["nst%d" % s])
                S.op("pool", lambda: nc.gpsimd.tensor_tensor(out=small[0:P, c + 2:c + 3], in0=small[0:P, c + 1:c + 2],
                                                             in1=mhalf[0:P, 0:1], op=ALU.pow),
                     reads=["nst%d" % s, "mhalf"], writes=["nst%d" % s])
                if s >= 1:
                    norm_tail(P, s - 1)
            norm_tail(P, nsub - 1)

        def norm_tail(P, s):
            c = 32 + 3 * s
            hbuf = hb2[s % 2]
            S.op("act", lambda: nc.scalar.activation(out=hbuf[0:P, :], in_=xt[0:P, xmap[s], :], func=AF.Copy,
                                                     scale=small[0:P, c + 2:c + 3]),
                 reads=["xt%d" % xmap[s], "nst%d" % s], writes=["hb%d" % (s % 2)])
            t = next_tb()
            for kc in range(8):
                S.op("pe", lambda kc=kc: nc.tensor.transpose(out=psT[t][:, kc * 128:kc * 128 + P],
                                                             in_=hbuf[0:P, kc * 128:(kc + 1) * 128],
                                                             identity=idb[0:P, 0:P]),
                     reads=["hb%d" % (s % 2), "idb"], writes=[TB[t]], signal=(kc == 7))
            copy_op(ew(), hT[:, :, s * 128:s * 128 + P],
                    psT[t][:].rearrange("p (k c) -> p k c", k=8)[:, :, 0:P], [TB[t]], [TB[t], "hT"])

        def ffn(l, P, nsub, NT, on_sub_done=None):
            plan_l = []
            loads_idx = {}
            for half_ in range(2):
                for j_ in range(11):
                    f_ = half_ * 11 + j_
                    if f_ % 2 == 0 or j_ == 0:
                        loads_idx[(half_, f_ // 2)] = len(plan_l)
                        plan_l.append(f_ // 2)
            load_slot = {}
            emitted = [0]

            def ensure(k):
                while emitted[0] <= min(k, len(plan_l) - 1):
                    i_ = emitted[0]
                    r_ = next_ring()
                    S.dma("sp", "ring%d" % r_, ring[r_][:], s_gu[l][plan_l[i_]].rearrange("p f g k c -> p (f g k c)"),
                          reads=["s_gu%d_%d_%d" % (l, plan_l[i_], a_) for a_ in range(2)], writes=["ring%d" % r_])
                    load_slot[i_] = r_
                    emitted[0] += 1

            if LOOK > 0:
                ensure(1)
            norm_T(P, nsub, NT)
            for half in range(2):
                for j in range(11):
                    f = half * 11 + j
                    S.dma("sp", "wd%d" % j, wd[:, j, :], s_d[l][f], reads=["s_d%d_%d" % (l, (f // 8) * 8)], writes=["wd%d" % j])
                for j in range(11):
                    f = half * 11 + j
                    fp, fi = f // 2, f % 2
                    if fi == 0 or j == 0:
                        li = loads_idx[(half, fp)]
                        ensure(li + LOOK)
                        r = load_slot[li]
                        rv = ring[r][:].rearrange("p (f g k c) -> p f g k c", f=2, g=2, k=8)
                    bg, bu = next_bank(0, 2), 2 + next_bank(0, 2)
                    for gu, b in [(0, bg), (1, bu)]:
                        for kc in range(8):
                            S.op("pe", lambda gu=gu, b=b, kc=kc: nc.tensor.matmul(
                                psB[b][:, 0:NT], lhsT=rv[:, fi, gu, kc, :], rhs=hT[:, kc, 0:NT],
                                start=(kc == 0), stop=(kc == 7)),
                                reads=["ring%d" % r, "hT"], writes=[PB[b]], signal=(kc == 7))
                    si = sg_i[0] % 2
                    sg_i[0] += 1
                    S.op("act", lambda: nc.scalar.activation(out=sg[si][:, 0:NT], in_=psB[bg][:, 0:NT], func=AF.Silu),
                         reads=[PB[bg]], writes=[PB[bg], "sg%d" % si])
                    S.op("dve", lambda: nc.vector.tensor_tensor(out=big[:, j, 0:NT], in0=sg[si][:, 0:NT],
                                                                in1=psB[bu][:, 0:NT], op=ALU.mult),
                         reads=["sg%d" % si, PB[bu]], writes=[PB[bu], "big%d" % j])
                for s in range(nsub):
                    for ch in range(2):
                        b = 4 + ch
                        for j in range(11):
                            S.op("pe", lambda j=j, b=b, ch=ch: nc.tensor.matmul(
                                psB[b][0:P, :], lhsT=big[:, j, s * 128:s * 128 + P], rhs=wd[:, j, ch * 512:(ch + 1) * 512],
                                start=(j == 0), stop=(j == 10)),
                                reads=["big%d" % j, "wd%d" % j], writes=[PB[b]], signal=(j == 10))
                        S.op("dve", lambda b=b, ch=ch: nc.vector.scalar_tensor_tensor(
                            out=xt[0:P, xmap[s], ch * 512:(ch + 1) * 512], in0=psB[b][0:P, :], scalar=0.5,
                            in1=xt[0:P, xmap[s], ch * 512:(ch + 1) * 512], op0=ALU.mult, op1=ALU.add),
                            reads=[PB[b], "xt%d" % xmap[s]], writes=[PB[b], "xt%d" % xmap[s]])
                        if half == 1 and ch == 1 and on_sub_done is not None:
                            on_sub_done(s)

        def qk_norm(P, bank, c0, nh, gain_tab):
            W = nh * HD
            S.op("act", lambda: nc.scalar.activation(out=sq[0:P, 0:W], in_=psB[bank][0:P, c0:c0 + W], func=AF.Square),
                 reads=[PB[bank]], writes=[PB[bank], "sq"])
            S.op("dve", lambda: nc.vector.tensor_reduce(out=small[0:P, 8:8 + nh],
                                                        in_=sq[0:P, 0:W].rearrange("p (h d) -> p h d", h=nh),
                                                        axis=AX.X, op=ALU.add), reads=["sq"], writes=["small"])
            S.op("pool", lambda: nc.gpsimd.tensor_scalar(out=small[0:P, 16:16 + nh], in0=small[0:P, 8:8 + nh],
                                                         scalar1=1.0 / HD, scalar2=EPS, op0=ALU.mult, op1=ALU.add),
                 reads=["small"], writes=["small"])
            S.op("pool", lambda: nc.gpsimd.tensor_tensor(out=small[0:P, 24:24 + nh], in0=small[0:P, 16:16 + nh],
                                                         in1=mhalf[0:P, 0:nh], op=ALU.pow),
                 reads=["small", "mhalf"], writes=["small"])
            S.op("dve", lambda: nc.vector.tensor_tensor(
                out=qn[0:P, 0:W].rearrange("p (h d) -> p h d", h=nh),
                in0=psB[bank][0:P, c0:c0 + W].rearrange("p (h d) -> p h d", h=nh),
                in1=small[0:P, 24:24 + nh].unsqueeze(2).broadcast_to([P, nh, HD]), op=ALU.mult),
                reads=[PB[bank], "small"], writes=[PB[bank], "qn"])
            S.op("pool", lambda: nc.gpsimd.tensor_tensor(
                out=qn[0:P, 0:W].rearrange("p (h d) -> p h d", h=nh),
                in0=qn[0:P, 0:W].rearrange("p (h d) -> p h d", h=nh),
                in1=gain_tab[0:P, :].unsqueeze(1).broadcast_to([P, nh, HD]), op=ALU.mult),
                reads=["qn"], writes=["qn"])

        def transpose_to(P, src_tok, ncols, dsts):
            t = next_tb()
            n = ncols // 128
            for c in range(n):
                S.op("pe", lambda c=c: nc.tensor.transpose(out=psT[t][:, c * 128:c * 128 + P],
                                                           in_=src_tok[0:P, c * 128:(c + 1) * 128], identity=idb[0:P, 0:P]),
                     reads=["qb16"], writes=[TB[t]], signal=(c == n - 1))
            for c, (dst, res) in enumerate(dsts):
                copy_op(ew(), dst, psT[t][:, c * 128:c * 128 + P], [TB[t]], [TB[t], res])

        def kv_out(seq, blk, grp, kv, st_idx, nh):
            if seq is None:
                return
            if grp == "A":
                if blk == 15:
                    S.dma("sp", "stq%d" % st_idx, o_ap[seq, :, kv, :, :].rearrange("t h d -> t (h d)"),
                          stg[st_idx][:, 0:nh * HD], reads=["stg%d" % st_idx])
                return
            g = grp
            nb = WINB[g]
            if blk >= 16 - nb:
                t0 = (blk - (16 - nb)) * 128
                S.dma("sp", "stq%d" % st_idx, o_bp[g][seq, t0:t0 + 128, kv, :, :].rearrange("t h d -> t (h d)"),
                      stg[st_idx][:, 0:nh * HD], reads=["stg%d" % st_idx])

        def next_stg():
            i = st_i[0] % NSTG
            st_i[0] += 1
            return i

        def project(P, nsub, NT, seq, blk0):
            for nt in range(6):
                r = next_ring()
                S.dma("sp", "ring%d" % r, ring[r][:], s_in[nt].rearrange("p k c -> p (k c)"),
                      reads=["s_in%d" % nt], writes=["ring%d" % r])
                rv = ring[r][:].rearrange("p (k c) -> p k c", k=8)
                for s in range(nsub):
                    blk = blk0 + s
                    tok = slice(s * 128, s * 128 + P)
                    b = next_bank(0, 4)
                    for kc in range(8):
                        S.op("pe", lambda kc=kc: nc.tensor.matmul(psB[b][0:P, :], lhsT=hT[:, kc, tok], rhs=rv[:, kc, :],
                                                                  start=(kc == 0), stop=(kc == 7)),
                             reads=["hT", "ring%d" % r], writes=[PB[b]], signal=(kc == 7))
                    for uu in range(2):
                        u = nt * 2 + uu
                        c0 = uu * 256
                        if seq is None:
                            zs = zs_box[0]
                            if u in (0, 1, 3, 4, 5, 6, 7, 8):
                                qk_norm(P, b, c0, 4, gq_a if u < 2 else (gq_b if u < 6 else gk_b))
                                S.op("act", lambda: nc.scalar.copy(out=zs[0:P, u * 256:(u + 1) * 256], in_=qn[0:P, :]),
                                     reads=["qn"], writes=["zs"])
                            elif u == 2:
                                qk_norm(P, b, c0, 2, gk_a)
                                S.op("act", lambda: nc.scalar.copy(out=zs[0:P, 512:640], in_=qn[0:P, 0:128]),
                                     reads=["qn"], writes=["zs"])
                                S.op("dve", lambda: nc.vector.tensor_copy(out=zs[0:P, 640:768], in_=psB[b][0:P, c0 + 128:c0 + 256]),
                                     reads=[PB[b]], writes=[PB[b], "zs"])
                            else:
                                S.op("dve", lambda: nc.vector.tensor_copy(out=zs[0:P, u * 256:(u + 1) * 256], in_=psB[b][0:P, c0:c0 + 256]),
                                     reads=[PB[b]], writes=[PB[b], "zs"])
                            continue
                        if u in (0, 1):
                            qk_norm(P, b, c0, 4, gq_a)
                            S.op("act", lambda: nc.scalar.copy(out=qb16[0:P, :], in_=qn[0:P, :]), reads=["qn"], writes=["qb16"])
                            transpose_to(P, qb16, 256, [(big[:, 2 * u + c, tok], "big%d" % (2 * u + c)) for c in range(2)])
                        elif u == 2:
                            qk_norm(P, b, c0, 2, gk_a)
                            si = next_stg()
                            S.op("act", lambda: nc.scalar.copy(out=stg[si][0:P, 0:128], in_=qn[0:P, 0:128]),
                                 reads=["qn"], writes=["stg%d" % si])
                            kv_out(seq, blk, "A", 0, si, 2)
                            S.op("dve", lambda: nc.vector.tensor_copy(
                                out=qb16[0:P, :].rearrange("p (h r d) -> p h r d", h=2, r=2),
                                in_=qn[0:P, 0:128].rearrange("p (h d) -> p h d", h=2).unsqueeze(2).broadcast_to([P, 2, 2, HD])),
                                reads=["qn"], writes=["qb16"])
                            if seq is not None:
                                transpose_to(P, qb16, 256, [(KTA[:, c, blk * 128:blk * 128 + P], "KTA") for c in range(2)])
                            else:
                                transpose_to(P, qb16, 256, [(KTA[:, c, 0:P], "KTA") for c in range(2)])
                            si = next_stg()
                            S.op("act", lambda: nc.scalar.copy(out=stg[si][0:P, 0:128], in_=psB[b][0:P, c0 + 128:c0 + 256]),
                                 reads=[PB[b]], writes=[PB[b], "stg%d" % si])
                            kv_out(seq, blk, "A", 1, si, 2)
                            S.op("dve", lambda: nc.vector.tensor_copy(
                                out=VA[0:P, blk if seq is not None else 0, :, 0:HD],
                                in_=stg[si][0:P, 0:128].rearrange("p (h d) -> p h d", h=2)),
                                reads=["stg%d" % si], writes=["VA"])
                        elif u in (3, 4, 5):
                            g = u - 3
                            qk_norm(P, b, c0, 4, gq_b)
                            S.op("act", lambda: nc.scalar.copy(out=qb16[0:P, :], in_=qn[0:P, :]), reads=["qn"], writes=["qb16"])
                            transpose_to(P, qb16, 256, [(big[:, 4 + 2 * g + c, tok], "big%d" % (4 + 2 * g + c)) for c in range(2)])
                        elif u in (6, 7, 8):
                            g = u - 6
                            qk_norm(P, b, c0, 4, gk_b)
                            si = next_stg()
                            S.op("act", lambda: nc.scalar.copy(out=stg[si][0:P, :], in_=qn[0:P, :]), reads=["qn"], writes=["stg%d" % si])
                            kv_out(seq, blk, g, 0, si, 4)
                            S.op("dve", lambda: nc.vector.tensor_copy(out=qb16[0:P, :], in_=qn[0:P, :]), reads=["qn"], writes=["qb16"])
                            kcol = slice(blk * 128, blk * 128 + P) if seq is not None else slice(0, P)
                            transpose_to(P, qb16, 256, [(KTB[:, 2 * g + c, kcol], "KTB") for c in range(2)])
                        else:
                            g = u - 9
                            si = next_stg()
                            S.op("act", lambda: nc.scalar.copy(out=stg[si][0:P, :], in_=psB[b][0:P, c0:c0 + 256]),
                                 reads=[PB[b]], writes=[PB[b], "stg%d" % si])
                            kv_out(seq, blk, g, 1, si, 4)
                            vb = blk if seq is not None else 0
                            if g == 0 or seq is None:
                                S.op("dve", lambda: nc.vector.tensor_copy(
                                    out=VB[0:P, vb, g, :, 0:HD], in_=stg[si][0:P, :].rearrange("p (h d) -> p h d", h=4)),
                                    reads=["stg%d" % si], writes=["VB"])
                            else:
                                S.op("dve", lambda: nc.vector.tensor_tensor(
                                    out=VB[0:P, vb, g, :, 0:HD], in0=stg[si][0:P, :].rearrange("p (h d) -> p h d", h=4),
                                    in1=abt[0:P, vb, g, :, 1:2].broadcast_to([P, 4, HD]), op=ALU.mult),
                                    reads=["stg%d" % si, "abt"], writes=["VB"])
                                S.op("pool", lambda: nc.gpsimd.tensor_copy(out=VB[0:P, vb, g, :, HD:HD + 1],
                                                                           in_=abt[0:P, vb, g, :, 1:2]),
                                     reads=["abt"], writes=["VB"])

        def project_prompt(seq, blk0):
            P = 128
            tails = []
            gidx = [0]

            def emit_group(nt, s, r, rv):
                blk = blk0 + s
                tok = slice(s * 128, (s + 1) * 128)
                kcol = slice(blk * 128, (blk + 1) * 128)
                b = next_bank(0, 4)
                par = gidx[0] % 3
                gidx[0] += 1
                for kc in range(8):
                    S.op("pe", lambda kc=kc: nc.tensor.matmul(psB[b][:, :], lhsT=hT[:, kc, tok], rhs=rv[:, kc, :],
                                                              start=(kc == 0), stop=(kc == 7)),
                         reads=["hT", "ring%d" % r], writes=[PB[b]], signal=(kc == 7))
                ps3 = psB[b][:, :].rearrange("p (h d) -> p h d", h=8)
                st = "pst%d" % par
                c = 64 + par * 8
                if nt < 5:
                    S.op("act", lambda: nc.scalar.activation(out=sqb[par][:, :], in_=psB[b][:, :], func=AF.Square),
                         reads=[PB[b]], writes=[PB[b], "sqb%d" % par])
                    S.op("dve", lambda: nc.vector.tensor_reduce(out=small[:, c:c + 8],
                                                                in_=sqb[par][:, :].rearrange("p (h d) -> p h d", h=8),
                                                                axis=AX.X, op=ALU.add), reads=["sqb%d" % par], writes=[st])
                    S.op("pool", lambda: nc.gpsimd.tensor_scalar(out=small[:, c:c + 8], in0=small[:, c:c + 8],
                                                                 scalar1=1.0 / HD, scalar2=EPS, op0=ALU.mult, op1=ALU.add),
                         reads=[st], writes=[st])
                    S.op("pool", lambda: nc.gpsimd.tensor_tensor(out=small[:, c:c + 8], in0=small[:, c:c + 8],
                                                                 in1=mhalf[:, 0:8], op=ALU.pow), reads=[st, "mhalf"], writes=[st])
                plan = []

                def stageB():
                  if nt < 5:
                    rs = small[:, c:c + 8].unsqueeze(2).broadcast_to([128, 8, HD])
                    if nt == 0:
                        ov = qbb[par][:, :].rearrange("p (c hf d) -> p hf c d", c=4, hf=2)
                        iv = psB[b][:, :].rearrange("p (hf c d) -> p hf c d", hf=2, c=4)
                        rs4 = small[:, c:c + 8].rearrange("p (hf c) -> p hf c", hf=2).unsqueeze(3).broadcast_to([128, 2, 4, HD])
                        S.op("dve", lambda: nc.vector.tensor_tensor(out=ov, in0=iv, in1=rs4, op=ALU.mult),
                             reads=[PB[b], st], writes=[PB[b], "qbb%d" % par])
                    else:
                        S.op("dve", lambda: nc.vector.tensor_tensor(out=qbb[par][:, :].rearrange("p (h d) -> p h d", h=8),
                                                                    in0=ps3, in1=rs, op=ALU.mult),
                             reads=[PB[b], st], writes=[PB[b], "qbb%d" % par])

                  def k_out(c0, nh, gtab, grp):
                      need = (blk == 15) if grp == "A" else (blk >= 16 - WINB[grp])
                      if not need:
                          return
                      si = next_stg()
                      h0 = c0 // HD
                      S.op("dve", lambda: nc.vector.tensor_tensor(
                          out=stg[si][:, 0:nh * HD].rearrange("p (h d) -> p h d", h=nh), in0=ps3[:, h0:h0 + nh, :],
                          in1=small[:, c + h0:c + h0 + nh].unsqueeze(2).broadcast_to([128, nh, HD]), op=ALU.mult),
                          reads=[PB[b], st], writes=[PB[b], "stg%d" % si])
                      S.op("dve", lambda: nc.vector.tensor_tensor(
                          out=stg[si][:, 0:nh * HD].rearrange("p (h d) -> p h d", h=nh),
                          in0=stg[si][:, 0:nh * HD].rearrange("p (h d) -> p h d", h=nh),
                          in1=gtab[:, :].unsqueeze(1).broadcast_to([128, nh, HD]), op=ALU.mult),
                          reads=["stg%d" % si], writes=["stg%d" % si])
                      kv_out(seq, blk, grp, 0, si, nh)

                  def v_part(c0, nh, grp):
                      need = (blk == 15) if grp == "A" else (blk >= 16 - WINB[grp])
                      src = psB[b][:, c0:c0 + nh * HD]
                      if need:
                          si = next_stg()
                          S.op("act", lambda: nc.scalar.copy(out=stg[si][:, 0:nh * HD], in_=src),
                               reads=[PB[b]], writes=[PB[b], "stg%d" % si])
                          kv_out(seq, blk, grp, 1, si, nh)
                      s3 = src.rearrange("p (h d) -> p h d", h=nh)
                      if grp == "A":
                          copy_op(ew(), VA[:, blk, :, 0:HD], s3, [PB[b]], [PB[b], "VA"])
                      elif grp == 0:
                          copy_op(ew(), VB[:, blk, 0, :, 0:HD], s3, [PB[b]], [PB[b], "VB"])
                      else:
                          S.op("dve", lambda: nc.vector.tensor_tensor(
                              out=VB[:, blk, grp, :, 0:HD], in0=s3, in1=abt[:, blk, grp, :, 1:2].broadcast_to([128, 4, HD]),
                              op=ALU.mult), reads=[PB[b], "abt"], writes=[PB[b], "VB"])

                  if nt == 0:
                      plan.append(([0, 1, 2, 3], big[:, 0:4, tok], ["big0", "big1", "big2", "big3"], 0))
                  elif nt == 1:
                      k_out(0, 2, gk_a, "A")
                      v_part(128, 2, "A")
                      plan.append(([0], KTA[:, 0:1, kcol], ["KTA"], 1))
                      plan.append(([2, 3], big[:, 4:6, tok], ["big4", "big5"], 2))
                  elif nt == 2:
                      plan.append(([0, 1, 2, 3], big[:, 6:10, tok], ["big6", "big7", "big8", "big9"], 2))
                  elif nt == 3:
                      k_out(0, 4, gk_b, 0)
                      k_out(256, 4, gk_b, 1)
                      plan.append(([0, 1, 2, 3], KTB[:, 0:4, kcol], ["KTB"], 3))
                  elif nt == 4:
                      k_out(0, 4, gk_b, 2)
                      v_part(256, 4, 0)
                      plan.append(([0, 1], KTB[:, 4:6, kcol], ["KTB"], 3))
                  else:
                      v_part(0, 4, 1)
                      v_part(256, 4, 2)

                def tail():
                    if not plan:
                        return
                    t = next_tb()
                    allc = [ci for (cs, _, _, _) in plan for ci in cs]
                    for i, ci in enumerate(allc):
                        S.op("pe", lambda ci=ci: nc.tensor.transpose(out=psT[t][:, ci * 128:(ci + 1) * 128],
                                                                     in_=qbb[par][:, ci * 128:(ci + 1) * 128], identity=idb[:, :]),
                             reads=["qbb%d" % par, "idb"], writes=[TB[t]], signal=(i == len(allc) - 1))
                    for (cs, dst, dres, gi) in plan:
                        n = len(cs)
                        src = psT[t][:, cs[0] * 128:(cs[0] + n) * 128].rearrange("p (n c) -> p n c", n=n)
                        if ew() == "act":
                            S.op("act", lambda: nc.scalar.activation(out=dst, in_=src, func=AF.Copy, scale=gcol[:, gi:gi + 1]),
                                 reads=[TB[t], "gcol"], writes=[TB[t]] + dres)
                        else:
                            S.op("dve", lambda: nc.vector.tensor_scalar(out=dst, in0=src, scalar1=gcol[:, gi:gi + 1], scalar2=None,
                                                                        op0=ALU.mult), reads=[TB[t], "gcol"], writes=[TB[t]] + dres)
                return stageB, tail

            ptails = []
            pendB = [None]
            for nt in range(6):
                r = next_ring()
                S.dma("sp", "ring%d" % r, ring[r][:], s_in[nt].rearrange("p k c -> p (k c)"),
                      reads=["s_in%d" % nt], writes=["ring%d" % r])
                rv = ring[r][:].rearrange("p (k c) -> p k c", k=8)
                for s in range(4):
                    sB, tl_ = emit_group(nt, s, r, rv)
                    if pendB[0] is not None:
                        pendB[0]()
                    pendB[0] = sB
                    ptails.append(tl_)
                    if len(ptails) > PDEPTH:
                        ptails.pop(0)()
            pendB[0]()
            while ptails:
                ptails.pop(0)()

        def attention_block(s, blk):
            qcol = slice(s * 128, s * 128 + 128)
            jobs = []
            for h in range(8):
                kvh, par, ch = h // 4, h // 4, h % 4
                pairs = [(kb, TIDX[("A", h, blk - kb)]) for kb in (blk - 1, blk) if kb >= 0]
                jobs.append((h, "A", par, lambda kb: KTA[:, 0, kb * 128:(kb + 1) * 128], big[:, ch, qcol], pairs,
                             lambda kb, kvh=kvh: VA[:, kb, kvh, :], "big%d" % ch))
            for g in range(3):
                for hh in (0, 2, 1, 3):
                    par = hh % 2
                    ch = 2 * g + hh // 2
                    pairs = []
                    for kb in range(max(0, blk - WINB[g]), blk + 1):
                        dl = blk - kb
                        if g <= 1:
                            ti = TIDX[(g, hh, dl)]
                        else:
                            ti = TIDX[(g, hh, "0" if dl == 0 else "m")]
                        pairs.append((kb, ti))
                    jobs.append((8 + hh, g, par, lambda kb, ch=ch: KTB[:, ch, kb * 128:(kb + 1) * 128], big[:, 4 + ch, qcol], pairs,
                                 lambda kb, g=g, hh=hh: VB[:, kb, g, hh, :], "big%d" % (4 + ch)))
            items = []
            for job in jobs:
                npair = len(job[5])
                for pi_ in range(npair):
                    items.append((job, pi_))
            import os as _os
            if _os.environ.get("KATT", "1") == "1":
                batches = []
                cur = []
                for it in items:
                    if cur and (len(cur) == 4 or cur[-1][0][2] != it[0][2]):
                        batches.append(cur)
                        cur = []
                    cur.append(it)
                if cur:
                    batches.append(cur)
            else:
                batches = []
                for job in jobs:
                    its = [(job, pi_) for pi_ in range(len(job[5]))]
                    batches.extend([its[i:i + 4] for i in range(0, len(its), 4)])

            def front(batch):
                n = len(batch)
                b = next_bank(0, 4)
                reads = set()
                for i, (job, pi_) in enumerate(batch):
                    (slot, grp, par, ktf, qap, pairs, vf, qres) = job
                    kb, ti = pairs[pi_]
                    pr = slice(par * 64, par * 64 + 64)
                    S.op("pe", lambda i=i, kb=kb, ktf=ktf, qap=qap, pr=pr: nc.tensor.matmul(
                        psB[b][:, i * 128:(i + 1) * 128], lhsT=ktf(kb)[pr, :], rhs=qap[pr, :], start=True, stop=True),
                        reads=["KTA" if grp == "A" else "KTB", qres], writes=[PB[b]], signal=(i == n - 1))
                si = sg_i[0] % 4
                sg_i[0] += 1
                sgv = sg[si][:, 0:256].bitcast(BF16)
                S.op("act", lambda: nc.scalar.activation(out=sgv[:, 0:n * 128], in_=psB[b][:, 0:n * 128], func=AF.Exp),
                     reads=[PB[b]], writes=[PB[b], "sg%d" % si])
                pi = pt_i[0] % 4
                pt_i[0] += 1
                e = "dve" if (pt_i[0] % MASKMOD) else "pool"
                tis = [job[5][pi_][1] for (job, pi_) in batch]
                runs = []
                i = 0
                while i < n:
                    j = i
                    if j + 1 < n and tis[j + 1] == tis[i]:
                        while j + 1 < n and tis[j + 1] == tis[i]:
                            j += 1
                        in1 = dtab[:, tis[i]:tis[i] + 1, :].broadcast_to([128, j + 1 - i, 128])
                    else:
                        while j + 1 < n and tis[j + 1] == tis[j] + 1:
                            j += 1
                        in1 = dtab[:, tis[i]:tis[j] + 1, :]
                    runs.append((i, j + 1, in1))
                    i = j + 1
                for (a, z, in1) in runs:
                    ov = pT[pi][:, a * 128:z * 128].rearrange("p (n c) -> p n c", n=z - a)
                    iv = sgv[:, a * 128:z * 128].rearrange("p (n c) -> p n c", n=z - a)
                    if e == "dve":
                        S.op("dve", lambda ov=ov, iv=iv, in1=in1: nc.vector.tensor_tensor(out=ov, in0=iv, in1=in1, op=ALU.mult),
                             reads=["sg%d" % si, "dtab"], writes=["pT%d" % pi])
                    else:
                        S.op("pool", lambda ov=ov, iv=iv, in1=in1: nc.gpsimd.tensor_tensor(out=ov, in0=iv, in1=in1, op=ALU.mult),
                             reads=["sg%d" % si, "dtab"], writes=["pT%d" % pi])
                return pi

            def back(batch, pi):
                n = len(batch)
                for i, (job, pi_) in enumerate(batch):
                    (slot, grp, par, ktf, qap, pairs, vf, qres) = job
                    kb, ti = pairs[pi_]
                    npair = len(pairs)
                    vres = "VA" if grp == "A" else "VB"
                    if grp == "A":
                        accb, acol = 4 + slot // 4, (slot % 4) * 65
                    else:
                        accb, acol = 4 + (grp % 2), (slot - 8) * 65
                    last = (pi_ == npair - 1)
                    S.op("pe", lambda i=i, kb=kb, vf=vf, accb=accb, acol=acol, pi_=pi_, last=last: nc.tensor.matmul(
                        psB[accb][:, acol:acol + 65], lhsT=pT[pi][:, i * 128:(i + 1) * 128], rhs=vf(kb),
                        start=(pi_ == 0), stop=last),
                        reads=["pT%d" % pi, vres], writes=[PB[accb]], signal=(i == n - 1 or last))
                    if not last:
                        continue
                    if grp == "A" and slot % 4 == 3:
                        hs_ = slot - 3
                        S.op("dve", lambda hs_=hs_, accb=accb: nc.vector.tensor_copy(
                            out=oacc[:, hs_:hs_ + 4, :], in_=psB[accb][:, 0:260].rearrange("p (h c) -> p h c", h=4)),
                            reads=[PB[accb]], writes=[PB[accb], "oacc"])
                    elif grp != "A" and slot == 11:
                        g = grp
                        pv = psB[accb][:, 0:260].rearrange("p (h c) -> p h c", h=4)
                        if g == 0:
                            S.op("dve", lambda pv=pv: nc.vector.tensor_copy(out=oacc[:, 8:12, :], in_=pv),
                                 reads=[PB[accb]], writes=[PB[accb], "oacc"])
                        else:
                            S.op("dve", lambda pv=pv, g=g: nc.vector.tensor_tensor(
                                out=otmp[:], in0=pv, in1=abt[:, blk, g, :, 0:1].broadcast_to([128, 4, 65]), op=ALU.mult),
                                reads=[PB[accb], "abt"], writes=[PB[accb], "otmp"])
                            S.op("pool", lambda: nc.gpsimd.tensor_tensor(out=oacc[:, 8:12, :], in0=oacc[:, 8:12, :], in1=otmp[:],
                                                                         op=ALU.add), reads=["otmp", "oacc"], writes=["oacc"])

            pend = []
            for bi in range(0, len(batches), 2):
                grp_ = batches[bi:bi + 2]
                for batch in grp_:
                    pend.append((batch, front(batch)))
                while len(pend) > ADEPTH:
                    back(*pend.pop(0))
            while pend:
                back(*pend.pop(0))

        def finish_o(P, s):
            S.op("dve", lambda: nc.vector.tensor_tensor(out=oacc[0:P, 0:8, 64:65], in0=oacc[0:P, 0:8, 64:65],
                                                        in1=esink[0:P, :].unsqueeze(2), op=ALU.add),
                 reads=["oacc", "esink"], writes=["oacc"])
            S.op("dve", lambda: nc.vector.reciprocal(out=rden[0:P, :].unsqueeze(2), in_=oacc[0:P, :, 64:65]),
                 reads=["oacc"], writes=["rden"])
            S.op("dve", lambda: nc.vector.tensor_tensor(
                out=onrm[0:P, :].rearrange("p (h d) -> p h d", h=12), in0=oacc[0:P, :, 0:HD],
                in1=rden[0:P, :].unsqueeze(2).broadcast_to([P, 12, HD]), op=ALU.mult),
                reads=["oacc", "rden"], writes=["onrm"])
            t = next_tb()
            for c in range(6):
                S.op("pe", lambda c=c: nc.tensor.transpose(out=psT[t][:, c * 128:c * 128 + P],
                                                           in_=onrm[0:P, c * 128:(c + 1) * 128], identity=idb[0:P, 0:P]),
                     reads=["onrm", "idb"], writes=[TB[t]], signal=(c == 5))
            copy_op(ew(), oT[:, :, s * 128:s * 128 + P],
                    psT[t][:, 0:768].rearrange("p (k c) -> p k c", k=6)[:, :, 0:P], [TB[t]], [TB[t], "oT"])

        def merge_out(P, nsub, NT):
            for m in range(8):
                r = next_ring()
                S.dma("sp", "ring%d" % r, ring[r][:, 0:2816], s_m[m], reads=["s_m_ga", "s_m_gb", "s_m_upa", "s_m_upb"], writes=["ring%d" % r])
                rv = ring[r][:, 0:2816].rearrange("p (k c) -> p k c", k=22)
                ga_b, gb_b = (2, 3) if m % 2 == 0 else (4, 5)
                specs = [(ga_b, 6, 8, hT, 0), (gb_b, 14, 8, hT, 0), (0, 0, 4, oT, 0), (1, 4, 2, oT, 4)]
                for (b, k0, nk, src, s0) in specs:
                    for kc in range(nk):
                        S.op("pe", lambda b=b, k0=k0, kc=kc, src=src, s0=s0, nk=nk: nc.tensor.matmul(
                            psB[b][:, 0:NT], lhsT=rv[:, k0 + kc, :], rhs=src[:, s0 + kc, 0:NT],
                            start=(kc == 0), stop=(kc == nk - 1)),
                            reads=["ring%d" % r, "oT" if src is oT else "hT"], writes=[PB[b]], signal=(kc == nk - 1))
                ia, ib = (2 * m) % 3, (2 * m + 1) % 3
                sA, sB = sg[ia], sg[ib]
                S.op("act", lambda: nc.scalar.activation(out=sA[:, 0:NT], in_=psB[ga_b][:, 0:NT], func=AF.Sigmoid),
                     reads=[PB[ga_b]], writes=[PB[ga_b], "sg%d" % ia])
                S.op("act", lambda: nc.scalar.activation(out=sB[:, 0:NT], in_=psB[gb_b][:, 0:NT], func=AF.Sigmoid),
                     reads=[PB[gb_b]], writes=[PB[gb_b], "sg%d" % ib])
                S.op("dve", lambda: nc.vector.tensor_tensor(out=sA[:, 0:NT], in0=sA[:, 0:NT], in1=psB[0][:, 0:NT], op=ALU.mult),
                     reads=["sg%d" % ia, PB[0]], writes=["sg%d" % ia, PB[0]])
                S.op("dve", lambda: nc.vector.tensor_tensor(out=sB[:, 0:NT], in0=sB[:, 0:NT], in1=psB[1][:, 0:NT], op=ALU.mult),
                     reads=["sg%d" % ib, PB[1]], writes=["sg%d" % ib, PB[1]])
                S.op("pool", lambda m=m: nc.gpsimd.tensor_tensor(out=big[:, m, 0:NT], in0=sA[:, 0:NT], in1=sB[:, 0:NT], op=ALU.add),
                     reads=["sg%d" % ia, "sg%d" % ib], writes=["big%d" % m])
            rs = []
            for ch in range(2):
                r = next_ring()
                S.dma("sp", "ring%d" % r, ring[r][:], s_o[ch].rearrange("p k c -> p (k c)"), reads=["s_o%d" % ch], writes=["ring%d" % r])
                rs.append(r)
            for s in range(nsub):
                for ch in range(2):
                    r = rs[ch]
                    rv = ring[r][:].rearrange("p (k c) -> p k c", k=8)
                    b = 4 + ch
                    for kc in range(8):
                        S.op("pe", lambda kc=kc, b=b, rv=rv: nc.tensor.matmul(
                            psB[b][0:P, :], lhsT=big[:, kc, s * 128:s * 128 + P], rhs=rv[:, kc, :], start=(kc == 0), stop=(kc == 7)),
                            reads=["big%d" % kc, "ring%d" % r], writes=[PB[b]], signal=(kc == 7))
                    S.op("dve", lambda b=b, ch=ch: nc.vector.tensor_tensor(
                        out=xt[0:P, xmap[s], ch * 512:(ch + 1) * 512], in0=psB[b][0:P, :], in1=xt[0:P, xmap[s], ch * 512:(ch + 1) * 512], op=ALU.add),
                        reads=[PB[b], "xt%d" % xmap[s]], writes=[PB[b], "xt%d" % xmap[s]])

        import os as _os2
        LOOK = int(_os2.environ.get("KLOOK", "3"))
        ADEPTH = int(_os2.environ.get("KADEPTH", "2"))
        MASKMOD = int(_os2.environ.get("KMASKMOD", "5"))
        PDEPTH = int(_os2.environ.get("KPDEPTH", "2"))
        copies = []
        if with_sample:
            copies.append((o_as[:, 0:127].rearrange("b t k h d -> b (t k h d)"), c_a[:, 1:128].rearrange("b t k h d -> b (t k h d)")))
            copies.append((o_bs[0][:, 0:127].rearrange("b t k h d -> b (t k h d)"), c_b[0][:, 1:128].rearrange("b t k h d -> b (t k h d)")))
            for bb in range(0, NS, 4):
                copies.append((o_bs[1][bb:bb + 4, 0:511].rearrange("b t k h d -> b (t k h d)"),
                               c_b[1][bb:bb + 4, 1:512].rearrange("b t k h d -> b (t k h d)")))
            for bb in range(NS):
                copies.append((o_bs[2][bb:bb + 1, 0:2047].rearrange("b t k h d -> b (t k h d)"),
                               c_b[2][bb:bb + 1, 1:2048].rearrange("b t k h d -> b (t k h d)")))

        def issue_copies(n):
            for _ in range(n):
                if copies:
                    o, i = copies.pop(0)
                    S.dma("sp", "ccopy", o, i)

        import os
        STAGE = int(os.environ.get("KSTAGE", "9"))
        NTILES = int(os.environ.get("KTILES", "8"))
        tcount = 0
        for seq in range(NSEQ):
            for tl in range(4):
                if STAGE < 1 or tcount >= NTILES:
                    continue
                tcount += 1
                blk0 = tl * 4
                xmap[:] = [(4 * (tcount - 1) + s_) % 5 for s_ in range(4)]
                xfree = (4 * (tcount - 1) + 4) % 5
                if tcount == 1:
                    for s_ in range(4):
                        S.dma("sp", "xi%d" % s_, xt[:, s_, :], x_p[seq, tl * 512 + s_ * 128:tl * 512 + (s_ + 1) * 128, :], writes=["xt%d" % s_])
                ffn(0, 128, 4, 512)
                if STAGE >= 2:
                    norm_T(128, 4, 512)
                    project_prompt(seq, blk0)
                if STAGE >= 3:
                    for s in range(4):
                        issue_copies(1 if s < 3 else 0)
                        attention_block(s, blk0 + s)
                        finish_o(128, s)
                if STAGE >= 4:
                    merge_out(128, 4, 512)
                nxt = None
                if tcount < NTILES and not (seq == NSEQ - 1 and tl == 3):
                    nxt = (seq, tl + 1) if tl < 3 else (seq + 1, 0)

                def sub_done(s_, seq=seq, tl=tl, nxt=nxt):
                    sl_ = xmap[s_]
                    S.dma("pool", "yout%d" % sl_, y_p[seq, tl * 512 + s_ * 128:tl * 512 + (s_ + 1) * 128, :], xt[:, sl_, :],
                          reads=["xt%d" % sl_])
                    if nxt is not None and s_ < 3:
                        S.dma("pool", "xt%d" % sl_, xt[:, sl_, :],
                              x_p[nxt[0], nxt[1] * 512 + (s_ + 1) * 128:nxt[1] * 512 + (s_ + 2) * 128, :], writes=["xt%d" % sl_])

                if nxt is not None:
                    S.dma("pool", "xt%d" % xfree, xt[:, xfree, :], x_p[nxt[0], nxt[1] * 512:nxt[1] * 512 + 128, :], writes=["xt%d" % xfree])
                if STAGE >= 5:
                    ffn(1, 128, 4, 512, sub_done)
                else:
                    for s_ in range(4):
                        sub_done(s_)

        issue_copies(len(copies))
        if not with_sample:
            hs.close()
        if with_sample:
            S.barrier()
            hs.close()
            ss_ = ExitStack()
            es.enter_context(ss_)
            zs = sb("zs", [128, 3072], F32, ss_)
            sq = sb("sq", [128, 256], F32, ss_)
            qn = sb("qn", [128, 256], F32, ss_)
            qb16 = sb("qb16", [128, 256], BF16, ss_)
            zs_box[0] = zs
            KVb = sb("KVb", [128, 17 * 512], F32, ss_)
            qrep = sb("qrep", [128, 512], F32, ss_)
            part = sb("part", [128, 8 * 65], F32, ss_)
            ssc2 = sb("ssc2", [128, 17 * 8], F32, ss_)
            pexp = sb("pexp", [128, 17 * 8], F32, ss_)
            sal = sb("sal", [128, 4, 17 * 8], F32, ss_)
            rept = sb("rept", [128, 128], F32, ss_)
            sel = sb("sel", [128, 16], F32, ss_)
            S.dma("sp", "rept", rept[0:16, :], c_rep, writes=["rept"])
            S.dma("sp", "sel", sel[:], c_sel, writes=["sel"])
            S.dma("sp", "sal", sal[:], c_sal, writes=["sal"])
            xmap[:] = [0, 1, 2, 3]
            S.dma("sp", "xi0", xt[0:NS, 0, :], x_s, writes=["xt0"])
            ffn(0, NS, 1, NS)
            norm_T(NS, 1, NS)
            project(NS, 1, NS, None, 0)
            S.dma("sp", "nrow", o_as[:, 127, 0, :, :].rearrange("b h d -> b (h d)"), zs[0:NS, 512:640], reads=["zs"])
            S.dma("sp", "nrow", o_as[:, 127, 1, :, :].rearrange("b h d -> b (h d)"), zs[0:NS, 640:768], reads=["zs"])
            for g in range(3):
                W = [128, 512, 2048][g]
                S.dma("sp", "nrow", o_bs[g][:, W - 1, 0, :, :].rearrange("b h d -> b (h d)"),
                      zs[0:NS, 1536 + g * 256:1536 + (g + 1) * 256], reads=["zs"])
                S.dma("sp", "nrow", o_bs[g][:, W - 1, 1, :, :].rearrange("b h d -> b (h d)"),
                      zs[0:NS, 2304 + g * 256:2304 + (g + 1) * 256], reads=["zs"])

            def sgroup(G, cache, dil, Hkv, Hq, qc0, kc0, vc0):
                RL = 2 * Hkv * HD
                KW = Hkv * HD
                kvv = KVb[:, 0:17 * RL].rearrange("p (j r) -> p j r", r=RL)
                S.op("pool", lambda: nc.gpsimd.memset(kvv[:, 16, :], 0.0), writes=["KVb"])
                src = cache.rearrange("b (jc jj r) k h d -> (b jc) jj r (k h d)", jc=8, jj=16, r=dil)[:, :, 0, :]
                S.dma("sp", "KVb", kvv[:, 0:16, :], src, writes=["KVb"], disjoint=True)
                k7 = KVb[7:128:8, :]
                S.dma("sp", "KVb", k7[:, 16 * RL:16 * RL + KW], zs[0:NS, kc0:kc0 + KW], reads=["zs"], writes=["KVb"], disjoint=True)
                S.dma("sp", "KVb", k7[:, 16 * RL + KW:17 * RL], zs[0:NS, vc0:vc0 + KW], reads=["zs"], writes=["KVb"], disjoint=True)
                W = Hq * HD
                S.op("pe", lambda: nc.tensor.matmul(psB[0][:, 0:W], lhsT=rept[0:NS, :], rhs=zs[0:NS, qc0:qc0 + W], start=True, stop=True),
                     reads=["rept", "zs"], writes=[PB[0]])
                S.op("act", lambda: nc.scalar.copy(out=qrep[:, 0:W], in_=psB[0][:, 0:W]), reads=[PB[0]], writes=[PB[0], "qrep"])
                sc3 = ssc2[:, 0:17 * Hq].rearrange("p (j h) -> p j h", h=Hq)
                pe3 = pexp[:, 0:17 * Hq].rearrange("p (j h) -> p j h", h=Hq)
                part3 = part[:, 0:Hq * 65].rearrange("p (h c) -> p h c", c=65)
                if G != 0:
                    K4 = kvv[:, :, 0:KW]
                    S.op("dve", lambda: nc.vector.tensor_tensor(out=K4, in0=K4, in1=qrep[:, 0:KW].unsqueeze(1).broadcast_to([128, 17, KW]),
                                                                op=ALU.mult), reads=["KVb", "qrep"], writes=["KVb"])
                    S.op("dve", lambda: nc.vector.tensor_reduce(out=sc3, in_=K4.rearrange("p j (h d) -> p j h d", h=Hkv), axis=AX.X, op=ALU.add),
                         reads=["KVb"], writes=["ssc2"])
                else:
                    prodA = KVb[:, 17 * RL:17 * RL + 17 * 256].rearrange("p (j h d) -> p j h d", h=4, d=HD)
                    for kvh in range(2):
                        S.op("dve", lambda kvh=kvh: nc.vector.tensor_tensor(
                            out=prodA, in0=kvv[:, :, kvh * HD:(kvh + 1) * HD].unsqueeze(2).broadcast_to([128, 17, 4, HD]),
                            in1=qrep[:, kvh * 256:(kvh + 1) * 256].rearrange("p (h d) -> p h d", h=4).unsqueeze(1).broadcast_to([128, 17, 4, HD]),
                            op=ALU.mult), reads=["KVb", "qrep"], writes=["KVb"])
                        S.op("dve", lambda kvh=kvh: nc.vector.tensor_reduce(out=sc3[:, :, kvh * 4:(kvh + 1) * 4], in_=prodA, axis=AX.X, op=ALU.add),
                             reads=["KVb"], writes=["ssc2"])
                S.op("dve", lambda: nc.vector.tensor_tensor(out=ssc2[:, 0:17 * Hq], in0=ssc2[:, 0:17 * Hq], in1=sal[:, G, 0:17 * Hq], op=ALU.add),
                     reads=["ssc2", "sal"], writes=["ssc2"])
                S.op("act", lambda: nc.scalar.activation(out=pexp[:, 0:17 * Hq], in_=ssc2[:, 0:17 * Hq], func=AF.Exp),
                     reads=["ssc2"], writes=["pexp"])
                S.op("dve", lambda: nc.vector.tensor_reduce(out=part3[:, :, 64], in_=pe3.rearrange("p j h -> p h j"), axis=AX.X, op=ALU.add),
                     reads=["pexp"], writes=["part"])
                if G != 0:
                    V4 = kvv[:, :, KW:2 * KW].rearrange("p j (h d) -> p j h d", h=Hkv)
                    S.op("dve", lambda: nc.vector.tensor_tensor(out=V4, in0=V4, in1=pe3.unsqueeze(3).broadcast_to([128, 17, Hq, HD]), op=ALU.mult),
                         reads=["KVb", "pexp"], writes=["KVb"])
                    S.op("dve", lambda: nc.vector.tensor_reduce(out=part3[:, :, 0:HD], in_=V4.rearrange("p j h d -> p h d j"), axis=AX.X, op=ALU.add),
                         reads=["KVb"], writes=["part"])
                else:
                    for kvh in range(2):
                        S.op("dve", lambda kvh=kvh: nc.vector.tensor_tensor(
                            out=prodA, in0=kvv[:, :, KW + kvh * HD:KW + (kvh + 1) * HD].unsqueeze(2).broadcast_to([128, 17, 4, HD]),
                            in1=pe3[:, :, kvh * 4:(kvh + 1) * 4].unsqueeze(3).broadcast_to([128, 17, 4, HD]), op=ALU.mult),
                            reads=["KVb", "pexp"], writes=["KVb"])
                        S.op("dve", lambda kvh=kvh: nc.vector.tensor_reduce(out=part3[:, kvh * 4:(kvh + 1) * 4, 0:HD],
                                                                            in_=prodA.rearrange("p j h d -> p h d j"), axis=AX.X, op=ALU.add),
                             reads=["KVb"], writes=["part"])
                if G == 0:
                    for hf in range(2):
                        S.op("pe", lambda hf=hf: nc.tensor.matmul(psB[4 + hf][0:NS, 0:260], lhsT=sel[:, 0:NS], rhs=part[:, hf * 260:(hf + 1) * 260],
                                                                  start=True, stop=True), reads=["sel", "part"], writes=[PB[4 + hf]])
                        S.op("dve", lambda hf=hf: nc.vector.tensor_copy(out=oacc[0:NS, hf * 4:(hf + 1) * 4, :],
                                                                        in_=psB[4 + hf][0:NS, 0:260].rearrange("p (h c) -> p h c", c=65)),
                             reads=[PB[4 + hf]], writes=[PB[4 + hf], "oacc"])
                else:
                    S.op("pe", lambda: nc.tensor.matmul(psB[3][0:NS, 0:260], lhsT=sel[:, 0:NS], rhs=part[:, 0:260],
                                                        start=(G == 1), stop=(G == 3)), reads=["sel", "part"], writes=[PB[3]])
                    if G == 3:
                        S.op("dve", lambda: nc.vector.tensor_copy(out=oacc[0:NS, 8:12, :],
                                                                  in_=psB[3][0:NS, 0:260].rearrange("p (h c) -> p h c", c=65)),
                             reads=[PB[3]], writes=[PB[3], "oacc"])

            sgroup(0, c_a, 1, 2, 8, 0, 512, 640)
            for g in range(3):
                sgroup(1 + g, c_b[g], DILS[g], 4, 4, 768 + g * 256, 1536 + g * 256, 2304 + g * 256)
            finish_o(NS, 0)
            merge_out(NS, 1, NS)
            ffn(1, NS, 1, NS)
            S.dma("pool", "yout0", y_s, xt[0:NS, 0, :], reads=["xt0"])

        S.finish()
        print("ops", S.nops, "waits", S.nwaits, "cnt", S.cnt, "dma sems", len(S.dsem))
    sal_np = np.zeros((128, 4, 17 * 8), np.float64)
    for p in range(128):
        jc = p % 8
        for G in range(4):
            Hq = 8 if G == 0 else 4
            dil = 1 if G == 0 else DILS[G - 1]
            for jj in range(17):
                for h in range(Hq):
                    sl = SLOPES[h] if G == 0 else SLOPES[8 + 4 * (G - 1) + h]
                    if jj < 16:
                        v = -sl * dil * (128 - (jc * 16 + jj))
                    else:
                        v = 0.0 if jc == 7 else -30000.0
                    sal_np[p, G, jj * Hq + h] = v
    rep_np = np.zeros((16, 128), np.float32)
    for p in range(128):
        rep_np[p // 8, p] = 1.0
    consts = {"c_ident": ident_np, "c_dt": dt_np, "c_ab": ab_np.reshape(128, -1), "c_sal": sal_np.astype(np.float32), "c_rep": rep_np,
              "c_sel": np.ascontiguousarray(rep_np.T)}
    return nc, consts


_CACHE = {}


def kernel(**inputs):
    import os
    WS = os.environ.get("KNOSAMPLE", "0") != "1"
    if "prog" not in _CACHE:
        _CACHE["prog"] = build_program(WS)
    nc, consts = _CACHE["prog"]
    f = lambda k: np.ascontiguousarray(np.asarray(inputs[k], dtype=np.float32))
    xp = f("x_prompt"); xs = f("x_sample").reshape(128, D)
    ca = f("cache_a_kv")[0]; cb1 = f("cache_b1_kv")[0]; cb2 = f("cache_b2_kv")[0]; cb3 = f("cache_b3_kv")[0]
    shared = {
        "norm_ffn1": f("norm_ffn1")[0], "norm_mix": f("norm_mix")[0], "norm_ffn2": f("norm_ffn2")[0],
        "w1_gate": f("w1_gate")[0], "w1_up": f("w1_up")[0], "w1_down": f("w1_down")[0],
        "w2_gate": f("w2_gate")[0], "w2_up": f("w2_up")[0], "w2_down": f("w2_down")[0],
        "w_in": f("w_in")[0], "q_norm_a": f("q_norm_a")[0], "k_norm_a": f("k_norm_a")[0],
        "q_norm_b": f("q_norm_b")[0], "k_norm_b": f("k_norm_b")[0], "sinks_a": f("sinks_a")[0].reshape(8),
        "w_up_a": f("w_up_a")[0], "w_up_b": f("w_up_b")[0], "w_o": f("w_o")[0],
    }
    shared.update(consts)
    in_maps = []
    for c in range(NCORES):
        m = dict(shared)
        m["x_prompt"] = xp[c * NSEQ:(c + 1) * NSEQ]
        m["x_sample"] = xs[c * NS:(c + 1) * NS]
        if WS:
            m["cache_a"] = ca[c * NS:(c + 1) * NS]
            m["cache_b1"] = cb1[c * NS:(c + 1) * NS]
            m["cache_b2"] = cb2[c * NS:(c + 1) * NS]
            m["cache_b3"] = cb3[c * NS:(c + 1) * NS]
        in_maps.append(m)
    res = run_bass_kernel_spmd(nc, in_maps, core_ids=list(range(NCORES)))
    R = res.results
    cat = lambda k: np.concatenate([np.asarray(r[k]) for r in R], axis=0)
    y_prompt = cat("y_prompt")
    y_sample = cat("y_sample").reshape(128, 1, D)
    outs = [y_prompt, y_sample]
    for k in ["a_p", "b1_p", "b2_p", "b3_p"] + (["a_s", "b1_s", "b2_s", "b3_s"] if WS else []):
        outs.append(cat(k)[None])
    return tuple(o.astype(np.float32) for o in outs)
```

```python
import numpy as np
from contextlib import ExitStack
import concourse.bass as bass
import concourse.mybir as mybir
from concourse.bass_utils import run_bass_kernel_spmd

F32 = mybir.dt.float32
BF16 = mybir.dt.bfloat16
AF = mybir.ActivationFunctionType
ALU = mybir.AluOpType
AX = mybir.AxisListType

NCORES = 8
D = 1024
DFF = 2816
NF = 22
SEQ = 2048
NSEQ = 2
HD = 64
EPS = 1e-6
NS = 16
NSTG = 4
N_AL = 20
SLOPES = [2.0 ** (-8.0 * i / N_AL) for i in range(1, N_AL + 1)]
DILS = [1, 4, 16]
WINB = [1, 4, 16]


class Sched:
    def __init__(self, nc, es):
        self.nc = nc
        self.es = es
        self.eng = {"pe": nc.tensor, "act": nc.scalar, "dve": nc.vector, "pool": nc.gpsimd, "sp": nc.sync}
        self.sem = {e: es.enter_context(nc.semaphore("s_" + e)) for e in ["pe", "act", "dve", "pool"]}
        self.cnt = {e: 0 for e in self.sem}
        self.pending = {e: [] for e in self.sem}
        self.lastw = {}
        self.readers = {}
        self.seen = {e: {} for e in self.eng}
        self.dsem = {}
        self.nwaits = 0
        self.nops = {e: 0 for e in self.eng}

    def _need(self, reads, writes):
        toks = []
        for r in reads:
            t = self.lastw.get(r)
            if t is not None:
                toks.append(t)
        for w in writes:
            t = self.lastw.get(w)
            if t is not None:
                toks.append(t)
            toks.extend(self.readers.get(w, []))
        return toks

    def _emit_waits(self, e, toks):
        best = {}
        for (own, sem, val) in toks:
            if own == e and (val is None or e == "pe"):
                continue
            if val is None:
                raise RuntimeError("dependency on unsignalled op")
            k = sem.name
            if self.seen[e].get(k, 0) >= val:
                continue
            if k not in best or best[k][1] < val:
                best[k] = (sem, val)
        for k, (sem, val) in best.items():
            self.eng[e].wait_ge(sem, val)
            self.seen[e][k] = val
            self.nwaits += 1

    def _record(self, tok, reads, writes):
        for r in reads:
            lst = self.readers.setdefault(r, [])
            lst[:] = [t for t in lst if t[1].name != tok[1].name]
            lst.append(tok)
        for w in writes:
            self.lastw[w] = tok
            self.readers[w] = []

    def op(self, e, fn, reads=(), writes=(), signal=True):
        toks = self._need(reads, writes)
        self._emit_waits(e, toks)
        inst = fn()
        self.nops[e] += 1
        if signal:
            self.cnt[e] += 1
            inst.then_inc(self.sem[e], 1)
            tok = (e, self.sem[e], self.cnt[e])
            for (rs, ws) in self.pending[e]:
                self._record(tok, rs, ws)
            self.pending[e] = []
            self._record(tok, reads, writes)
        else:
            self.pending[e].append((tuple(reads), tuple(writes)))
            bad = (e, self.sem[e], None)
            for w in writes:
                self.lastw[w] = bad
                self.readers[w] = []
            for r in reads:
                self.readers.setdefault(r, []).append(bad)
        return inst

    def dma(self, q, slot, out, in_, reads=(), writes=(), disjoint=False, **kw):
        if slot not in self.dsem:
            self.dsem[slot] = [self.es.enter_context(self.nc.semaphore("d_" + slot)), 0]
        ds = self.dsem[slot]
        toks = self._need(reads, writes)
        if disjoint:
            toks = [t for t in toks if t[1].name != ds[0].name]
        self._emit_waits(q, toks)
        ds[1] += 16
        self.eng[q].dma_start(out=out, in_=in_, **kw).then_inc(ds[0], 16)
        self.nops[q] += 1
        tok = ("dma", ds[0], ds[1])
        self._record(tok, reads, writes)
        return tok

    def barrier(self):
        for e in ["pe", "act", "dve", "pool", "sp"]:
            for f in ["pe", "act", "dve", "pool"]:
                if f != e and self.cnt[f] > self.seen[e].get(self.sem[f].name, 0):
                    self.eng[e].wait_ge(self.sem[f], self.cnt[f])
                    self.seen[e][self.sem[f].name] = self.cnt[f]
            for slot, (sem, val) in self.dsem.items():
                if val > self.seen[e].get(sem.name, 0):
                    self.eng[e].wait_ge(sem, val)
                    self.seen[e][sem.name] = val

    def finish(self):
        for e in ["sp"]:
            for slot, (sem, val) in self.dsem.items():
                if val > 0:
                    self.eng[e].wait_ge(sem, val)
            for f in ["pe", "act", "dve", "pool"]:
                if self.cnt[f] > 0:
                    self.eng[e].wait_ge(self.sem[f], self.cnt[f])


def host_tables():
    ident = np.eye(128, dtype=np.float32)
    k = np.arange(128)[:, None].astype(np.float64)
    q = np.arange(128)[None, :].astype(np.float64)
    tabs = []
    idx = {}
    for h in range(8):
        s = SLOPES[h]
        idx[("A", h, 1)] = len(tabs); tabs.append(np.where(q <= k, np.exp(-s * (128 + q - k)), 0.0))
        idx[("A", h, 0)] = len(tabs); tabs.append(np.where(q >= k, np.exp(-s * (q - k)), 0.0))
    for g in range(3):
        dil = DILS[g]
        mod = ((q - k) % dil) == 0
        for h in range(4):
            s = SLOPES[8 + g * 4 + h]
            if g == 0:
                idx[(g, h, 1)] = len(tabs); tabs.append(np.where(q <= k, np.exp(-s * (128 + q - k)), 0.0))
                idx[(g, h, 0)] = len(tabs); tabs.append(np.where(q >= k, np.exp(-s * (q - k)), 0.0))
            elif g == 1:
                idx[(g, h, 4)] = len(tabs); tabs.append(np.where((q <= k) & mod, np.exp(-s * (q - k)), 0.0))
                for dl in (3, 2, 1):
                    idx[(g, h, dl)] = len(tabs); tabs.append(np.where(mod, np.exp(-s * (q - k)), 0.0))
                idx[(g, h, 0)] = len(tabs); tabs.append(np.where((q >= k) & mod, np.exp(-s * (q - k)), 0.0))
            else:
                idx[(g, h, "m")] = len(tabs); tabs.append(np.where(mod, np.exp(-s * (q - k)), 0.0))
                idx[(g, h, "0")] = len(tabs); tabs.append(np.where((q >= k) & mod, np.exp(-s * (q - k)), 0.0))
    dt = np.stack(tabs, axis=1).astype(np.float32)
    ab = np.ones((128, 16, 3, 4, 2), np.float64)
    for g in (1, 2):
        for h in range(4):
            s = SLOPES[8 + g * 4 + h]
            for b in range(16):
                ab[:, b, g, h, 0] = np.exp(-s * 128.0 * b)
                ab[:, b, g, h, 1] = np.exp(s * 128.0 * b)
    return ident, dt, idx, ab.astype(np.float32)


def build_program(with_sample=True):
    ident_np, dt_np, TIDX, ab_np = host_tables()
    NTAB = dt_np.shape[1]
    nc = bass.Bass("TRN2", target_bir_lowering=False)

    def din(name, shape, dt=F32):
        return nc.dram_tensor(name, list(shape), dt, kind="ExternalInput").ap()

    def dout(name, shape):
        return nc.dram_tensor(name, list(shape), F32, kind="ExternalOutput").ap()

    def dscr(name, shape, dt=BF16):
        return nc.dram_tensor(name, list(shape), dt, kind="Internal").ap()

    x_p = din("x_prompt", [NSEQ, SEQ, D])
    x_s = din("x_sample", [NS, D])
    if with_sample:
        c_a = din("cache_a", [NS, 128, 2, 2, HD])
        c_b = [din("cache_b1", [NS, 128, 2, 4, HD]), din("cache_b2", [NS, 512, 2, 4, HD]),
               din("cache_b3", [NS, 2048, 2, 4, HD])]
    g_f1 = din("norm_ffn1", [D]); g_mx = din("norm_mix", [D]); g_f2 = din("norm_ffn2", [D])
    w_g = [din("w1_gate", [D, DFF]), din("w2_gate", [D, DFF])]
    w_u = [din("w1_up", [D, DFF]), din("w2_up", [D, DFF])]
    w_d = [din("w1_down", [DFF, D]), din("w2_down", [DFF, D])]
    w_in = din("w_in", [D, 5120])
    qn_a = din("q_norm_a", [HD]); kn_a = din("k_norm_a", [HD]); qn_b = din("q_norm_b", [HD]); kn_b = din("k_norm_b", [HD])
    sinks = din("sinks_a", [8])
    w_upa = din("w_up_a", [512, D]); w_upb = din("w_up_b", [256, D]); w_o = din("w_o", [D, D])
    c_ident = din("c_ident", [128, 128]); c_dt = din("c_dt", [128, NTAB, 128]); c_ab = din("c_ab", [128, 16 * 3 * 4 * 2])
    c_sal = din("c_sal", [128, 4, 17 * 8])
    c_rep = din("c_rep", [16, 128]); c_sel = din("c_sel", [128, 16])
    y_p = dout("y_prompt", [NSEQ, SEQ, D]); y_s = dout("y_sample", [NS, D])
    o_ap = dout("a_p", [NSEQ, 128, 2, 2, HD])
    o_bp = [dout("b1_p", [NSEQ, 128, 2, 4, HD]), dout("b2_p", [NSEQ, 512, 2, 4, HD]), dout("b3_p", [NSEQ, 2048, 2, 4, HD])]
    if with_sample:
        o_as = dout("a_s", [NS, 128, 2, 2, HD])
        o_bs = [dout("b1_s", [NS, 128, 2, 4, HD]), dout("b2_s", [NS, 512, 2, 4, HD]), dout("b3_s", [NS, 2048, 2, 4, HD])]
    s_gu = [dscr("s_gu1", [11, 128, 2, 2, 8, 128]), dscr("s_gu2", [11, 128, 2, 2, 8, 128])]
    s_d = [dscr("s_d1", [NF, 128, D]), dscr("s_d2", [NF, 128, D])]
    s_in = dscr("s_in", [6, 128, 8, 512])
    s_m = dscr("s_m", [8, 128, 22 * 128])
    s_o = dscr("s_o", [2, 128, 8, 512])

    es = ExitStack()
    with es:
        S = Sched(nc, es)

        def sb(name, shape, dt, stack=es):
            return stack.enter_context(nc.sbuf_tensor(name, list(shape), dt))

        psT = [es.enter_context(nc.psum_tensor("psT%d" % i, [128, 1024], BF16)) for i in range(2)]
        psB = [es.enter_context(nc.psum_tensor("psB%d" % i, [128, 512], F32)) for i in range(6)]
        PB = ["P%d" % i for i in range(6)]
        TB = ["T0", "T1"]

        idb = sb("idb", [128, 128], BF16)
        dtab = sb("dtab", [128, NTAB, 128], BF16)
        abt = sb("abt", [128, 16, 3, 4, 2], F32)
        gq_a = sb("gq_a", [128, HD], F32); gk_a = sb("gk_a", [128, HD], F32)
        gq_b = sb("gq_b", [128, HD], F32); gk_b = sb("gk_b", [128, HD], F32)
        esink = sb("esink", [128, 8], F32)
        mhalf = sb("mhalf", [128, 8], F32)
        small = sb("small", [128, 96], F32)

        S.op("pool", lambda: nc.gpsimd.memset(mhalf[:], -0.5), writes=["mhalf"])
        gcol = sb("gcol", [128, 4], F32)
        for ci, v in enumerate([qn_a, kn_a, qn_b, kn_b]):
            for hf in range(2):
                S.dma("sp", "gcol", gcol[hf * 64:(hf + 1) * 64, ci:ci + 1], v.rearrange("(d o) -> d o", o=1), writes=["gcol"],
                      allow_slow_non_contiguous=True)
        S.op("act", lambda: nc.scalar.mul(out=gcol[:, 0:1], in_=gcol[:, 0:1], mul=HD ** -0.5), reads=["gcol"], writes=["gcol"])
        S.op("act", lambda: nc.scalar.mul(out=gcol[:, 2:3], in_=gcol[:, 2:3], mul=HD ** -0.5), reads=["gcol"], writes=["gcol"])

        def bcast_load(dst, src, n, name):
            S.dma("sp", name, dst, src.rearrange("(o n) -> o n", o=1).broadcast_to([128, n]), writes=[name])

        bcast_load(gq_a[:], qn_a, HD, "gq_a"); bcast_load(gk_a[:], kn_a, HD, "gk_a")
        bcast_load(gq_b[:], qn_b, HD, "gq_b"); bcast_load(gk_b[:], kn_b, HD, "gk_b")
        bcast_load(esink[:], sinks, 8, "esink")
        S.dma("sp", "abt", abt[:].rearrange("p a b c d -> p (a b c d)"), c_ab, writes=["abt"])
        S.op("act", lambda: nc.scalar.mul(out=gq_a[:], in_=gq_a[:], mul=HD ** -0.5), reads=["gq_a"], writes=["gq_a"])
        S.op("act", lambda: nc.scalar.mul(out=gq_b[:], in_=gq_b[:], mul=HD ** -0.5), reads=["gq_b"], writes=["gq_b"])
        S.op("act", lambda: nc.scalar.activation(out=esink[:], in_=esink[:], func=AF.Exp), reads=["esink"], writes=["esink"])

        with ExitStack() as ps:
            stg_in = [sb("stg_in%d" % i, [128, 8, 1024], F32, ps) for i in range(2)]
            stg_out = [sb("stg_out%d" % i, [128, 8, 1024], BF16, ps) for i in range(2)]
            gains = sb("gains", [128, 3, 8], F32, ps)
            dt32 = sb("dt32", [128, NTAB, 128], F32, ps)
            id32 = sb("id32", [128, 128], F32, ps)
            for i, g in enumerate([g_f1, g_mx, g_f2]):
                S.dma("sp", "gains", gains[:, i, :], g.rearrange("(c p) -> p c", p=128), writes=["gains"],
                      allow_slow_non_contiguous=True)
            S.dma("sp", "id32", id32[:], c_ident, writes=["id32"])
            S.dma("sp", "dt32", dt32[:], c_dt, writes=["dt32"])
            S.op("dve", lambda: nc.vector.tensor_copy(out=idb[:], in_=id32[:]), reads=["id32"], writes=["idb"])
            S.op("dve", lambda: nc.vector.tensor_copy(out=dtab[:], in_=dt32[:]), reads=["dt32"], writes=["dtab"])

            pcount = [0]
            def block(src_ap, nk, n, gain_idx, perm_f, stores):
                i = pcount[0] % 2
                pcount[0] += 1
                tin = stg_in[i][:, 0:nk, 0:n]
                S.dma("sp", "stg_in%d" % i, tin, src_ap, writes=["stg_in%d" % i])
                flat = stg_out[i][:].rearrange("p k c -> p (k c)")[:, 0:nk * n]
                if perm_f:
                    f = n // 128
                    ov_all = flat.rearrange("p (f k c) -> p k f c", f=f, k=nk)
                    iv_all = tin.rearrange("p k (f c) -> p k f c", c=128)
                else:
                    ov_all = flat.rearrange("p (k c) -> p k c", k=nk)
                    iv_all = tin
                if gain_idx is not None:
                    for kc in range(nk):
                        S.op("act", lambda kc=kc: nc.scalar.activation(out=ov_all[:, kc], in_=iv_all[:, kc], func=AF.Copy,
                                                                       scale=gains[:, gain_idx, kc:kc + 1]),
                             reads=["stg_in%d" % i, "gains"], writes=["stg_out%d" % i], signal=(kc == nk - 1))
                else:
                    e = "dve" if pcount[0] % 2 else "pool"
                    if e == "dve":
                        S.op("dve", lambda: nc.vector.tensor_copy(out=ov_all, in_=iv_all), reads=["stg_in%d" % i], writes=["stg_out%d" % i])
                    else:
                        S.op("pool", lambda: nc.gpsimd.tensor_copy(out=ov_all, in_=iv_all), reads=["stg_in%d" % i], writes=["stg_out%d" % i])
                for (dst, src, res) in stores(flat):
                    S.dma("pool", "stg_out%d" % i, dst, src, reads=["stg_out%d" % i], writes=[res], disjoint=True)

            def wsrc(w, c0, ncol, nk):
                return w.rearrange("(c p) n -> p c n", p=128)[:, 0:nk, c0:c0 + ncol]

            for l in range(2):
                gi = 0 if l == 0 else 2
                for c0, n in ((0, 1024), (1024, 1024), (2048, 768)):
                    for gu, w in enumerate([w_g[l], w_u[l]]):
                        def st(flat, c0=c0, n=n, gu=gu, l=l):
                            out = []
                            for q_ in range(n // 256):
                                fp = c0 // 256 + q_
                                out.append((s_gu[l][fp, :, :, gu, :, :].rearrange("p f k c -> p f (k c)"),
                                            flat[:, q_ * 2048:(q_ + 1) * 2048].rearrange("p (f x) -> p f x", f=2),
                                            "s_gu%d_%d_%d" % (l, fp, gu)))
                            return out
                        block(wsrc(w, c0, n, 8), 8, n, gi, True, st)
                for f0, nf in ((0, 8), (8, 8), (16, 6)):
                    def st(flat, f0=f0, nf=nf, l=l):
                        return [(s_d[l][f0:f0 + nf].rearrange("f p c -> p f c"), flat.rearrange("p (f c) -> p f c", f=nf),
                                 "s_d%d_%d" % (l, f0))]
                    block(w_d[l].rearrange("(f p) c -> p f c", p=128)[:, f0:f0 + nf, :], nf, 1024, None, False, st)
            for bk in range(3):
                def st(flat, bk=bk):
                    v = flat.rearrange("p (k c) -> p k c", k=8)
                    return [(s_in[2 * bk + hf], v[:, :, hf * 512:(hf + 1) * 512], "s_in%d" % (2 * bk + hf)) for hf in range(2)]
                block(wsrc(w_in, bk * 1024, 1024, 8), 8, 1024, 1, False, st)
            for which, c0, off in [("ga", 3072, 6 * 128), ("gb", 4096, 14 * 128)]:
                def st(flat, off=off, which=which):
                    return [(s_m[:, :, off:off + 1024].rearrange("m p c -> p m c"), flat.rearrange("p (m x) -> p m x", m=8), "s_m_" + which)]
                block(wsrc(w_in, c0, 1024, 8), 8, 1024, 1, True, st)
            def st(flat):
                return [(s_m[:, :, 0:512].rearrange("m p c -> p m c"), flat.rearrange("p (m x) -> p m x", m=8), "s_m_upa")]
            block(wsrc(w_upa, 0, 1024, 4), 4, 1024, None, True, st)
            def st(flat):
                return [(s_m[:, :, 512:768].rearrange("m p c -> p m c"), flat.rearrange("p (m x) -> p m x", m=8), "s_m_upb")]
            block(wsrc(w_upb, 0, 1024, 2), 2, 1024, None, True, st)
            def st(flat):
                v = flat.rearrange("p (k c) -> p k c", k=8)
                return [(s_o[ch], v[:, :, ch * 512:(ch + 1) * 512], "s_o%d" % ch) for ch in range(2)]
            block(wsrc(w_o, 0, 1024, 8), 8, 1024, None, False, st)
            S.barrier()

        ms = ExitStack()
        es.enter_context(ms)
        xt = sb("xt", [128, 5, D], F32, ms)
        xmap = [0, 1, 2, 3]
        hb = sb("hb", [128, D], BF16, ms)
        hT = sb("hT", [128, 8, 512], BF16, ms)
        big = sb("big", [128, 11, 512], BF16, ms)
        wd = sb("wd", [128, 11, D], BF16, ms)
        ring = [sb("ring%d" % i, [128, 4096], BF16, ms) for i in range(4)]
        sg = [sb("sg%d" % i, [128, 512], F32, ms) for i in range(2)]
        hs = ExitStack()
        stg = [sb("stg%d" % i, [128, 256], F32, ms) for i in range(NSTG)]
        pT = [sb("pT%d" % i, [128, 512], BF16, ms) for i in range(2)]
        oacc = sb("oacc", [128, 12, 65], F32, ms)
        otmp = sb("otmp", [128, 4, 65], F32, ms)
        onrm = sb("onrm", [128, 768], BF16, ms)
        oT = sb("oT", [128, 6, 512], BF16, ms)
        rden = sb("rden", [128, 12], F32, ms)
        hb2 = [hb, sb("hb1", [128, D], BF16, ms)]
        sg.append(sb("sg2", [128, 512], F32, ms))
        pT.append(sb("pT2", [128, 512], BF16, ms))
        sg.append(sb("sg3", [128, 512], F32, ms))
        pT.append(sb("pT3", [128, 512], BF16, ms))
        sqb = [sb("sqb%d" % i, [128, 512], F32, hs) for i in range(3)]
        qbb = [sb("qbb%d" % i, [128, 512], BF16, hs) for i in range(3)]
        KTA = sb("KTA", [128, 1, SEQ], BF16, hs)
        VA = sb("VA", [128, 16, 2, 65], BF16, hs)
        KTB = sb("KTB", [128, 6, SEQ], BF16, hs)
        VB = sb("VB", [128, 16, 3, 4, 65], BF16, hs)
        zs_box = [None]

        S.op("dve", lambda: nc.vector.memset(VA[:].rearrange("p a b c -> p (a b c)"), 1.0), writes=["VA"])
        S.op("dve", lambda: nc.vector.memset(VB[:].rearrange("p a b c d -> p (a b c d)"), 1.0), writes=["VB"])
        for g_ in (1, 2):
            S.op("dve", lambda g_=g_: nc.vector.tensor_copy(out=VB[:, :, g_, :, HD:HD + 1], in_=abt[:, :, g_, :, 1:2]),
                 reads=["abt"], writes=["VB"])

        ring_i = [0]
        st_i = [0]
        bank_i = [0]
        tb_i = [0]
        sg_i = [0]
        pt_i = [0]

        def next_ring():
            i = ring_i[0] % 4
            ring_i[0] += 1
            return i

        def next_bank(lo=0, hi=4):
            i = lo + bank_i[0] % (hi - lo)
            bank_i[0] += 1
            return i

        def next_tb():
            i = tb_i[0] % 2
            tb_i[0] += 1
            return i

        ew_i = [0]

        def ew():
            ew_i[0] += 1
            return "dve" if ew_i[0] % 2 else "act"

        def copy_op(e, out, in_, reads, writes):
            if e == "act":
                S.op("act", lambda: nc.scalar.copy(out=out, in_=in_), reads=reads, writes=writes)
            elif e == "dve":
                S.op("dve", lambda: nc.vector.tensor_copy(out=out, in_=in_), reads=reads, writes=writes)
            else:
                S.op("pool", lambda: nc.gpsimd.tensor_copy(out=out, in_=in_), reads=reads, writes=writes)

        def norm_T(P, nsub, NT):
            for s in range(nsub):
                c = 32 + 3 * s
                S.op("act", lambda: nc.scalar.activation(out=hb2[s % 2][0:P, :], in_=xt[0:P, xmap[s], :], func=AF.Square,
                                                         accum_out=small[0:P, c:c + 1]),
                     reads=["xt%d" % xmap[s]], writes=["hb%d" % (s % 2), "nst%d" % s])
                S.op("pool", lambda: nc.gpsimd.tensor_scalar(out=small[0:P, c + 1:c + 2], in0=small[0:P, c:c + 1], scalar1=1.0 / D,
                                                             scalar2=EPS, op0=ALU.mult, op1=ALU.add),
                     reads=["nst%d" % s], writes=["nst%d" % s])
                S.op("pool", lambda: nc.gpsimd.tensor_tensor(out=small[0:P, c + 2:c + 3], in0=small[0:P, c + 1:c + 2],
                                                             in1=mhalf[0:P, 0:1], op=ALU.pow),
                     reads=["nst%d" % s, "mhalf"], writes=["nst%d" % s])
                if s >= 1:
                    norm_tail(P, s - 1)
            norm_tail(P, nsub - 1)

        def norm_tail(P, s):
            c = 32 + 3 * s
            hbuf = hb2[s % 2]
            S.op("act", lambda: nc.scalar.activation(out=hbuf[0:P, :], in_=xt[0:P, xmap[s], :], func=AF.Copy,
                                                     scale=small[0:P, c + 2:c + 3]),
                 reads=["xt%d" % xmap[s], "nst%d" % s], writes=["hb%d" % (s % 2)])
            t = next_tb()
            for kc in range(8):
                S.op("pe", lambda kc=kc: nc.tensor.transpose(out=psT[t][:, kc * 128:kc * 128 + P],
                                                             in_=hbuf[0:P, kc * 128:(kc + 1) * 128],
                                                             identity=idb[0:P, 0:P]),
                     reads=["hb%d" % (s % 2), "idb"], writes=[TB[t]], signal=(kc == 7))
            copy_op(ew(), hT[:, :, s * 128:s * 128 + P],
                    psT[t][:].rearrange("p (k c) -> p k c", k=8)[:, :, 0:P], [TB[t]], [TB[t], "hT"])

        def ffn(l, P, nsub, NT, on_sub_done=None):
            plan_l = []
            loads_idx = {}
            for half_ in range(2):
                for j_ in range(11):
                    f_ = half_ * 11 + j_
                    if f_ % 2 == 0 or j_ == 0:
                        loads_idx[(half_, f_ // 2)] = len(plan_l)
                        plan_l.append(f_ // 2)
            load_slot = {}
            emitted = [0]

            def ensure(k):
                while emitted[0] <= min(k, len(plan_l) - 1):
                    i_ = emitted[0]
                    r_ = next_ring()
                    S.dma("sp", "ring%d" % r_, ring[r_][:], s_gu[l][plan_l[i_]].rearrange("p f g k c -> p (f g k c)"),
                          reads=["s_gu%d_%d_%d" % (l, plan_l[i_], a_) for a_ in range(2)], writes=["ring%d" % r_])
                    load_slot[i_] = r_
                    emitted[0] += 1

            if LOOK > 0:
                ensure(1)
            norm_T(P, nsub, NT)
            for half in range(2):
                for j in range(11):
                    f = half * 11 + j
                    S.dma("sp", "wd%d" % j, wd[:, j, :], s_d[l][f], reads=["s_d%d_%d" % (l, (f // 8) * 8)], writes=["wd%d" % j])
                for j in range(11):
                    f = half * 11 + j
                    fp, fi = f // 2, f % 2
                    if fi == 0 or j == 0:
                        li = loads_idx[(half, fp)]
                        ensure(li + LOOK)
                        r = load_slot[li]
                        rv = ring[r][:].rearrange("p (f g k c) -> p f g k c", f=2, g=2, k=8)
                    bg, bu = next_bank(0, 2), 2 + next_bank(0, 2)
                    for gu, b in [(0, bg), (1, bu)]:
                        for kc in range(8):
                            S.op("pe", lambda gu=gu, b=b, kc=kc: nc.tensor.matmul(
                                psB[b][:, 0:NT], lhsT=rv[:, fi, gu, kc, :], rhs=hT[:, kc, 0:NT],
                                start=(kc == 0), stop=(kc == 7)),
                                reads=["ring%d" % r, "hT"], writes=[PB[b]], signal=(kc == 7))
                    si = sg_i[0] % 2
                    sg_i[0] += 1
                    S.op("act", lambda: nc.scalar.activation(out=sg[si][:, 0:NT], in_=psB[bg][:, 0:NT], func=AF.Silu),
                         reads=[PB[bg]], writes=[PB[bg], "sg%d" % si])
                    S.op("dve", lambda: nc.vector.tensor_tensor(out=big[:, j, 0:NT], in0=sg[si][:, 0:NT],
                                                                in1=psB[bu][:, 0:NT], op=ALU.mult),
                         reads=["sg%d" % si, PB[bu]], writes=[PB[bu], "big%d" % j])
                for s in range(nsub):
                    for ch in range(2):
                        b = 4 + ch
                        for j in range(11):
                            S.op("pe", lambda j=j, b=b, ch=ch: nc.tensor.matmul(
                                psB[b][0:P, :], lhsT=big[:, j, s * 128:s * 128 + P], rhs=wd[:, j, ch * 512:(ch + 1) * 512],
                                start=(j == 0), stop=(j == 10)),
                                reads=["big%d" % j, "wd%d" % j], writes=[PB[b]], signal=(j == 10))
                        S.op("dve", lambda b=b, ch=ch: nc.vector.scalar_tensor_tensor(
                            out=xt[0:P, xmap[s], ch * 512:(ch + 1) * 512], in0=psB[b][0:P, :], scalar=0.5,
                            in1=xt[0:P, xmap[s], ch * 512:(ch + 1) * 512], op0=ALU.mult, op1=ALU.add),
                            reads=[PB[b], "xt%d" % xmap[s]], writes=[PB[b], "xt%d" % xmap[s]])
                        if half == 1 and ch == 1 and on_sub_done is not None:
                            on_sub_done(s)

        def qk_norm(P, bank, c0, nh, gain_tab):
            W = nh * HD
            S.op("act", lambda: nc.scalar.activation(out=sq[0:P, 0:W], in_=psB[bank][0:P, c0:c0 + W], func=AF.Square),
                 reads=[PB[bank]], writes=[PB[bank], "sq"])
            S.op("dve", lambda: nc.vector.tensor_reduce(out=small[0:P, 8:8 + nh],
                                                        in_=sq[0:P, 0:W].rearrange("p (h d) -> p h d", h=nh),
                                                        axis=AX.X, op=ALU.add), reads=["sq"], writes=["small"])
            S.op("pool", lambda: nc.gpsimd.tensor_scalar(out=small[0:P, 16:16 + nh], in0=small[0:P, 8:8 + nh],
                                                         scalar1=1.0 / HD, scalar2=EPS, op0=ALU.mult, op1=ALU.add),
                 reads=["small"], writes=["small"])
            S.op("pool", lambda: nc.gpsimd.tensor_tensor(out=small[0:P, 24:24 + nh], in0=small[0:P, 16:16 + nh],
                                                         in1=mhalf[0:P, 0:nh], op=ALU.pow),
                 reads=["small", "mhalf"], writes=["small"])
            S.op("dve", lambda: nc.vector.tensor_tensor(
                out=qn[0:P, 0:W].rearrange("p (h d) -> p h d", h=nh),
                in0=psB[bank][0:P, c0:c0 + W].rearrange("p (h d) -> p h d", h=nh),
                in1=small[0:P, 24:24 + nh].unsqueeze(2).broadcast_to([P, nh, HD]), op=ALU.mult),
                reads=[PB[bank], "small"], writes=[PB[bank], "qn"])
            S.op("pool", lambda: nc.gpsimd.tensor_tensor(
                out=qn[0:P, 0:W].rearrange("p (h d) -> p h d", h=nh),
                in0=qn[0:P, 0:W].rearrange("p (h d) -> p h d", h=nh),
                in1=gain_tab[0:P, :].unsqueeze(1).broadcast_to([P, nh, HD]), op=ALU.mult),
                reads=["qn"], writes=["qn"])

        def transpose_to(P, src_tok, ncols, dsts):
            t = next_tb()
            n = ncols // 128
            for c in range(n):
                S.op("pe", lambda c=c: nc.tensor.transpose(out=psT[t][:, c * 128:c * 128 + P],
                                                           in_=src_tok[0:P, c * 128:(c + 1) * 128], identity=idb[0:P, 0:P]),
                     reads=["qb16"], writes=[TB[t]], signal=(c == n - 1))
            for c, (dst, res) in enumerate(dsts):
                copy_op(ew(), dst, psT[t][:, c * 128:c * 128 + P], [TB[t]], [TB[t], res])

        def kv_out(seq, blk, grp, kv, st_idx, nh):
            if seq is None:
                return
            if grp == "A":
                if blk == 15:
                    S.dma("sp", "stq%d" % st_idx, o_ap[seq, :, kv, :, :].rearrange("t h d -> t (h d)"),
                          stg[st_idx][:, 0:nh * HD], reads=["stg%d" % st_idx])
                return
            g = grp
            nb = WINB[g]
            if blk >= 16 - nb:
                t0 = (blk - (16 - nb)) * 128
                S.dma("sp", "stq%d" % st_idx, o_bp[g][seq, t0:t0 + 128, kv, :, :].rearrange("t h d -> t (h d)"),
                      stg[st_idx][:, 0:nh * HD], reads=["stg%d" % st_idx])

        def next_stg():
            i = st_i[0] % NSTG
            st_i[0] += 1
            return i

        def project(P, nsub, NT, seq, blk0):
            for nt in range(6):
                r = next_ring()
                S.dma("sp", "ring%d" % r, ring[r][:], s_in[nt].rearrange("p k c -> p (k c)"),
                      reads=["s_in%d" % nt], writes=["ring%d" % r])
                rv = ring[r][:].rearrange("p (k c) -> p k c", k=8)
                for s in range(nsub):
                    blk = blk0 + s
                    tok = slice(s * 128, s * 128 + P)
                    b = next_bank(0, 4)
                    for kc in range(8):
                        S.op("pe", lambda kc=kc: nc.tensor.matmul(psB[b][0:P, :], lhsT=hT[:, kc, tok], rhs=rv[:, kc, :],
                                                                  start=(kc == 0), stop=(kc == 7)),
                             reads=["hT", "ring%d" % r], writes=[PB[b]], signal=(kc == 7))
                    for uu in range(2):
                        u = nt * 2 + uu
                        c0 = uu * 256
                        if seq is None:
                            zs = zs_box[0]
                            if u in (0, 1, 3, 4, 5, 6, 7, 8):
                                qk_norm(P, b, c0, 4, gq_a if u < 2 else (gq_b if u < 6 else gk_b))
                                S.op("act", lambda: nc.scalar.copy(out=zs[0:P, u * 256:(u + 1) * 256], in_=qn[0:P, :]),
                                     reads=["qn"], writes=["zs"])
                            elif u == 2:
                                qk_norm(P, b, c0, 2, gk_a)
                                S.op("act", lambda: nc.scalar.copy(out=zs[0:P, 512:640], in_=qn[0:P, 0:128]),
                                     reads=["qn"], writes=["zs"])
                                S.op("dve", lambda: nc.vector.tensor_copy(out=zs[0:P, 640:768], in_=psB[b][0:P, c0 + 128:c0 + 256]),
                                     reads=[PB[b]], writes=[PB[b], "zs"])
                            else:
                                S.op("dve", lambda: nc.vector.tensor_copy(out=zs[0:P, u * 256:(u + 1) * 256], in_=psB[b][0:P, c0:c0 + 256]),
                                     reads=[PB[b]], writes=[PB[b], "zs"])
                            continue
                        if u in (0, 1):
                            qk_norm(P, b, c0, 4, gq_a)
                            S.op("act", lambda: nc.scalar.copy(out=qb16[0:P, :], in_=qn[0:P, :]), reads=["qn"], writes=["qb16"])
                            transpose_to(P, qb16, 256, [(big[:, 2 * u + c, tok], "big%d" % (2 * u + c)) for c in range(2)])
                        elif u == 2:
                            qk_norm(P, b, c0, 2, gk_a)
                            si = next_stg()
                            S.op("act", lambda: nc.scalar.copy(out=stg[si][0:P, 0:128], in_=qn[0:P, 0:128]),
                                 reads=["qn"], writes=["stg%d" % si])
                            kv_out(seq, blk, "A", 0, si, 2)
                            S.op("dve", lambda: nc.vector.tensor_copy(
                                out=qb16[0:P, :].rearrange("p (h r d) -> p h r d", h=2, r=2),
                                in_=qn[0:P, 0:128].rearrange("p (h d) -> p h d", h=2).unsqueeze(2).broadcast_to([P, 2, 2, HD])),
                                reads=["qn"], writes=["qb16"])
                            if seq is not None:
                                transpose_to(P, qb16, 256, [(KTA[:, c, blk * 128:blk * 128 + P], "KTA") for c in range(2)])
                            else:
                                transpose_to(P, qb16, 256, [(KTA[:, c, 0:P], "KTA") for c in range(2)])
                            si = next_stg()
                            S.op("act", lambda: nc.scalar.copy(out=stg[si][0:P, 0:128], in_=psB[b][0:P, c0 + 128:c0 + 256]),
                                 reads=[PB[b]], writes=[PB[b], "stg%d" % si])
                            kv_out(seq, blk, "A", 1, si, 2)
                            S.op("dve", lambda: nc.vector.tensor_copy(
                                out=VA[0:P, blk if seq is not None else 0, :, 0:HD],
                                in_=stg[si][0:P, 0:128].rearrange("p (h d) -> p h d", h=2)),
                                reads=["stg%d" % si], writes=["VA"])
                        elif u in (3, 4, 5):
                            g = u - 3
                            qk_norm(P, b, c0, 4, gq_b)
                            S.op("act", lambda: nc.scalar.copy(out=qb16[0:P, :], in_=qn[0:P, :]), reads=["qn"], writes=["qb16"])
                            transpose_to(P, qb16, 256, [(big[:, 4 + 2 * g + c, tok], "big%d" % (4 + 2 * g + c)) for c in range(2)])
                        elif u in (6, 7, 8):
                            g = u - 6
                            qk_norm(P, b, c0, 4, gk_b)
                            si = next_stg()
                            S.op("act", lambda: nc.scalar.copy(out=stg[si][0:P, :], in_=qn[0:P, :]), reads=["qn"], writes=["stg%d" % si])
                            kv_out(seq, blk, g, 0, si, 4)
                            S.op("dve", lambda: nc.vector.tensor_copy(out=qb16[0:P, :], in_=qn[0:P, :]), reads=["qn"], writes=["qb16"])
                            kcol = slice(blk * 128, blk * 128 + P) if seq is not None else slice(0, P)
                            transpose_to(P, qb16, 256, [(KTB[:, 2 * g + c, kcol], "KTB") for c in range(2)])
                        else:
                            g = u - 9
                            si = next_stg()
                            S.op("act", lambda: nc.scalar.copy(out=stg[si][0:P, :], in_=psB[b][0:P, c0:c0 + 256]),
                                 reads=[PB[b]], writes=[PB[b], "stg%d" % si])
                            kv_out(seq, blk, g, 1, si, 4)
                            vb = blk if seq is not None else 0
                            if g == 0 or seq is None:
                                S.op("dve", lambda: nc.vector.tensor_copy(
                                    out=VB[0:P, vb, g, :, 0:HD], in_=stg[si][0:P, :].rearrange("p (h d) -> p h d", h=4)),
                                    reads=["stg%d" % si], writes=["VB"])
                            else:
                                S.op("dve", lambda: nc.vector.tensor_tensor(
                                    out=VB[0:P, vb, g, :, 0:HD], in0=stg[si][0:P, :].rearrange("p (h d) -> p h d", h=4),
                                    in1=abt[0:P, vb, g, :, 1:2].broadcast_to([P, 4, HD]), op=ALU.mult),
                                    reads=["stg%d" % si, "abt"], writes=["VB"])
                                S.op("pool", lambda: nc.gpsimd.tensor_copy(out=VB[0:P, vb, g, :, HD:HD + 1],
                                                                           in_=abt[0:P, vb, g, :, 1:2]),
                                     reads=["abt"], writes=["VB"])

        def project_prompt(seq, blk0):
            P = 128
            tails = []
            gidx = [0]

            def emit_group(nt, s, r, rv):
                blk = blk0 + s
                tok = slice(s * 128, (s + 1) * 128)
                kcol = slice(blk * 128, (blk + 1) * 128)
                b = next_bank(0, 4)
                par = gidx[0] % 3
                gidx[0] += 1
                for kc in range(8):
                    S.op("pe", lambda kc=kc: nc.tensor.matmul(psB[b][:, :], lhsT=hT[:, kc, tok], rhs=rv[:, kc, :],
                                                              start=(kc == 0), stop=(kc == 7)),
                         reads=["hT", "ring%d" % r], writes=[PB[b]], signal=(kc == 7))
                ps3 = psB[b][:, :].rearrange("p (h d) -> p h d", h=8)
                st = "pst%d" % par
                c = 64 + par * 8
                if nt < 5:
                    S.op("act", lambda: nc.scalar.activation(out=sqb[par][:, :], in_=psB[b][:, :], func=AF.Square),
                         reads=[PB[b]], writes=[PB[b], "sqb%d" % par])
                    S.op("dve", lambda: nc.vector.tensor_reduce(out=small[:, c:c + 8],
                                                                in_=sqb[par][:, :].rearrange("p (h d) -> p h d", h=8),
                                                                axis=AX.X, op=ALU.add), reads=["sqb%d" % par], writes=[st])
                    S.op("pool", lambda: nc.gpsimd.tensor_scalar(out=small[:, c:c + 8], in0=small[:, c:c + 8],
                                                                 scalar1=1.0 / HD, scalar2=EPS, op0=ALU.mult, op1=ALU.add),
                         reads=[st], writes=[st])
                    S.op("pool", lambda: nc.gpsimd.tensor_tensor(out=small[:, c:c + 8], in0=small[:, c:c + 8],
                                                                 in1=mhalf[:, 0:8], op=ALU.pow), reads=[st, "mhalf"], writes=[st])
                plan = []

                def stageB():
                  if nt < 5:
                    rs = small[:, c:c + 8].unsqueeze(2).broadcast_to([128, 8, HD])
                    if nt == 0:
                        ov = qbb[par][:, :].rearrange("p (c hf d) -> p hf c d", c=4, hf=2)
                        iv = psB[b][:, :].rearrange("p (hf c d) -> p hf c d", hf=2, c=4)
                        rs4 = small[:, c:c + 8].rearrange("p (hf c) -> p hf c", hf=2).unsqueeze(3).broadcast_to([128, 2, 4, HD])
                        S.op("dve", lambda: nc.vector.tensor_tensor(out=ov, in0=iv, in1=rs4, op=ALU.mult),
                             reads=[PB[b], st], writes=[PB[b], "qbb%d" % par])
                    else:
                        S.op("dve", lambda: nc.vector.tensor_tensor(out=qbb[par][:, :].rearrange("p (h d) -> p h d", h=8),
                                                                    in0=ps3, in1=rs, op=ALU.mult),
                             reads=[PB[b], st], writes=[PB[b], "qbb%d" % par])

                  def k_out(c0, nh, gtab, grp):
                      need = (blk == 15) if grp == "A" else (blk >= 16 - WINB[grp])
                      if not need:
                          return
                      si = next_stg()
                      h0 = c0 // HD
                      S.op("dve", lambda: nc.vector.tensor_tensor(
                          out=stg[si][:, 0:nh * HD].rearrange("p (h d) -> p h d", h=nh), in0=ps3[:, h0:h0 + nh, :],
                          in1=small[:, c + h0:c + h0 + nh].unsqueeze(2).broadcast_to([128, nh, HD]), op=ALU.mult),
                          reads=[PB[b], st], writes=[PB[b], "stg%d" % si])
                      S.op("dve", lambda: nc.vector.tensor_tensor(
                          out=stg[si][:, 0:nh * HD].rearrange("p (h d) -> p h d", h=nh),
                          in0=stg[si][:, 0:nh * HD].rearrange("p (h d) -> p h d", h=nh),
                          in1=gtab[:, :].unsqueeze(1).broadcast_to([128, nh, HD]), op=ALU.mult),
                          reads=["stg%d" % si], writes=["stg%d" % si])
                      kv_out(seq, blk, grp, 0, si, nh)

                  def v_part(c0, nh, grp):
                      need = (blk == 15) if grp == "A" else (blk >= 16 - WINB[grp])
                      src = psB[b][:, c0:c0 + nh * HD]
                      if need:
                          si = next_stg()
                          S.op("act", lambda: nc.scalar.copy(out=stg[si][:, 0:nh * HD], in_=src),
                               reads=[PB[b]], writes=[PB[b], "stg%d" % si])
                          kv_out(seq, blk, grp, 1, si, nh)
                      s3 = src.rearrange("p (h d) -> p h d", h=nh)
                      if grp == "A":
                          copy_op(ew(), VA[:, blk, :, 0:HD], s3, [PB[b]], [PB[b], "VA"])
                      elif grp == 0:
                          copy_op(ew(), VB[:, blk, 0, :, 0:HD], s3, [PB[b]], [PB[b], "VB"])
                      else:
                          S.op("dve", lambda: nc.vector.tensor_tensor(
                              out=VB[:, blk, grp, :, 0:HD], in0=s3, in1=abt[:, blk, grp, :, 1:2].broadcast_to([128, 4, HD]),
                              op=ALU.mult), reads=[PB[b], "abt"], writes=[PB[b], "VB"])

                  if nt == 0:
                      plan.append(([0, 1, 2, 3], big[:, 0:4, tok], ["big0", "big1", "big2", "big3"], 0))
                  elif nt == 1:
                      k_out(0, 2, gk_a, "A")
                      v_part(128, 2, "A")
                      plan.append(([0], KTA[:, 0:1, kcol], ["KTA"], 1))
                      plan.append(([2, 3], big[:, 4:6, tok], ["big4", "big5"], 2))
                  elif nt == 2:
                      plan.append(([0, 1, 2, 3], big[:, 6:10, tok], ["big6", "big7", "big8", "big9"], 2))
                  elif nt == 3:
                      k_out(0, 4, gk_b, 0)
                      k_out(256, 4, gk_b, 1)
                      plan.append(([0, 1, 2, 3], KTB[:, 0:4, kcol], ["KTB"], 3))
                  elif nt == 4:
                      k_out(0, 4, gk_b, 2)
                      v_part(256, 4, 0)
                      plan.append(([0, 1], KTB[:, 4:6, kcol], ["KTB"], 3))
                  else:
                      v_part(0, 4, 1)
                      v_part(256, 4, 2)

                def tail():
                    if not plan:
                        return
                    t = next_tb()
                    allc = [ci for (cs, _, _, _) in plan for ci in cs]
                    for i, ci in enumerate(allc):
                        S.op("pe", lambda ci=ci: nc.tensor.transpose(out=psT[t][:, ci * 128:(ci + 1) * 128],
                                                                     in_=qbb[par][:, ci * 128:(ci + 1) * 128], identity=idb[:, :]),
                             reads=["qbb%d" % par, "idb"], writes=[TB[t]], signal=(i == len(allc) - 1))
                    for (cs, dst, dres, gi) in plan:
                        n = len(cs)
                        src = psT[t][:, cs[0] * 128:(cs[0] + n) * 128].rearrange("p (n c) -> p n c", n=n)
                        if ew() == "act":
                            S.op("act", lambda: nc.scalar.activation(out=dst, in_=src, func=AF.Copy, scale=gcol[:, gi:gi + 1]),
                                 reads=[TB[t], "gcol"], writes=[TB[t]] + dres)
                        else:
                            S.op("dve", lambda: nc.vector.tensor_scalar(out=dst, in0=src, scalar1=gcol[:, gi:gi + 1], scalar2=None,
                                                                        op0=ALU.mult), reads=[TB[t], "gcol"], writes=[TB[t]] + dres)
                return stageB, tail

            ptails = []
            pendB = [None]
            for nt in range(6):
                r = next_ring()
                S.dma("sp", "ring%d" % r, ring[r][:], s_in[nt].rearrange("p k c -> p (k c)"),
                      reads=["s_in%d" % nt], writes=["ring%d" % r])
                rv = ring[r][:].rearrange("p (k c) -> p k c", k=8)
                for s in range(4):
                    sB, tl_ = emit_group(nt, s, r, rv)
                    if pendB[0] is not None:
                        pendB[0]()
                    pendB[0] = sB
                    ptails.append(tl_)
                    if len(ptails) > PDEPTH:
                        ptails.pop(0)()
            pendB[0]()
            while ptails:
                ptails.pop(0)()

        def attention_block(s, blk):
            qcol = slice(s * 128, s * 128 + 128)
            jobs = []
            for h in range(8):
                kvh, par, ch = h // 4, h // 4, h % 4
                pairs = [(kb, TIDX[("A", h, blk - kb)]) for kb in (blk - 1, blk) if kb >= 0]
                jobs.append((h, "A", par, lambda kb: KTA[:, 0, kb * 128:(kb + 1) * 128], big[:, ch, qcol], pairs,
                             lambda kb, kvh=kvh: VA[:, kb, kvh, :], "big%d" % ch))
            for g in range(3):
                for hh in (0, 2, 1, 3):
                    par = hh % 2
                    ch = 2 * g + hh // 2
                    pairs = []
                    for kb in range(max(0, blk - WINB[g]), blk + 1):
                        dl = blk - kb
                        if g <= 1:
                            ti = TIDX[(g, hh, dl)]
                        else:
                            ti = TIDX[(g, hh, "0" if dl == 0 else "m")]
                        pairs.append((kb, ti))
                    jobs.append((8 + hh, g, par, lambda kb, ch=ch: KTB[:, ch, kb * 128:(kb + 1) * 128], big[:, 4 + ch, qcol], pairs,
                                 lambda kb, g=g, hh=hh: VB[:, kb, g, hh, :], "big%d" % (4 + ch)))
            items = []
            for job in jobs:
                npair = len(job[5])
                for pi_ in range(npair):
                    items.append((job, pi_))
            import os as _os
            if _os.environ.get("KATT", "1") == "1":
                batches = []
                cur = []
                for it in items:
                    if cur and (len(cur) == 4 or cur[-1][0][2] != it[0][2]):
                        batches.append(cur)
                        cur = []
                    cur.append(it)
                if cur:
                    batches.append(cur)
            else:
                batches = []
                for job in jobs:
                    its = [(job, pi_) for pi_ in range(len(job[5]))]
                    batches.extend([its[i:i + 4] for i in range(0, len(its), 4)])

            def front(batch):
                n = len(batch)
                b = next_bank(0, 4)
                reads = set()
                for i, (job, pi_) in enumerate(batch):
                    (slot, grp, par, ktf, qap, pairs, vf, qres) = job
                    kb, ti = pairs[pi_]
                    pr = slice(par * 64, par * 64 + 64)
                    S.op("pe", lambda i=i, kb=kb, ktf=ktf, qap=qap, pr=pr: nc.tensor.matmul(
                        psB[b][:, i * 128:(i + 1) * 128], lhsT=ktf(kb)[pr, :], rhs=qap[pr, :], start=True, stop=True),
                        reads=["KTA" if grp == "A" else "KTB", qres], writes=[PB[b]], signal=(i == n - 1))
                si = sg_i[0] % 4
                sg_i[0] += 1
                sgv = sg[si][:, 0:256].bitcast(BF16)
                S.op("act", lambda: nc.scalar.activation(out=sgv[:, 0:n * 128], in_=psB[b][:, 0:n * 128], func=AF.Exp),
                     reads=[PB[b]], writes=[PB[b], "sg%d" % si])
                pi = pt_i[0] % 4
                pt_i[0] += 1
                e = "dve" if (pt_i[0] % MASKMOD) else "pool"
                tis = [job[5][pi_][1] for (job, pi_) in batch]
                runs = []
                i = 0
                while i < n:
                    j = i
                    if j + 1 < n and tis[j + 1] == tis[i]:
                        while j + 1 < n and tis[j + 1] == tis[i]:
                            j += 1
                        in1 = dtab[:, tis[i]:tis[i] + 1, :].broadcast_to([128, j + 1 - i, 128])
                    else:
                        while j + 1 < n and tis[j + 1] == tis[j] + 1:
                            j += 1
                        in1 = dtab[:, tis[i]:tis[j] + 1, :]
                    runs.append((i, j + 1, in1))
                    i = j + 1
                for (a, z, in1) in runs:
                    ov = pT[pi][:, a * 128:z * 128].rearrange("p (n c) -> p n c", n=z - a)
                    iv = sgv[:, a * 128:z * 128].rearrange("p (n c) -> p n c", n=z - a)
                    if e == "dve":
                        S.op("dve", lambda ov=ov, iv=iv, in1=in1: nc.vector.tensor_tensor(out=ov, in0=iv, in1=in1, op=ALU.mult),
                             reads=["sg%d" % si, "dtab"], writes=["pT%d" % pi])
                    else:
                        S.op("pool", lambda ov=ov, iv=iv, in1=in1: nc.gpsimd.tensor_tensor(out=ov, in0=iv, in1=in1, op=ALU.mult),
                             reads=["sg%d" % si, "dtab"], writes=["pT%d" % pi])
                return pi

            def back(batch, pi):
                n = len(batch)
                for i, (job, pi_) in enumerate(batch):
                    (slot, grp, par, ktf, qap, pairs, vf, qres) = job
                    kb, ti = pairs[pi_]
                    npair = len(pairs)
                    vres = "VA" if grp == "A" else "VB"
                    if grp == "A":
                        accb, acol = 4 + slot // 4, (slot % 4) * 65
                    else:
                        accb, acol = 4 + (grp % 2), (slot - 8) * 65
                    last = (pi_ == npair - 1)
                    S.op("pe", lambda i=i, kb=kb, vf=vf, accb=accb, acol=acol, pi_=pi_, last=last: nc.tensor.matmul(
                        psB[accb][:, acol:acol + 65], lhsT=pT[pi][:, i * 128:(i + 1) * 128], rhs=vf(kb),
                        start=(pi_ == 0), stop=last),
                        reads=["pT%d" % pi, vres], writes=[PB[accb]], signal=(i == n - 1 or last))
                    if not last:
                        continue
                    if grp == "A" and slot % 4 == 3:
                        hs_ = slot - 3
                        S.op("dve", lambda hs_=hs_, accb=accb: nc.vector.tensor_copy(
                            out=oacc[:, hs_:hs_ + 4, :], in_=psB[accb][:, 0:260].rearrange("p (h c) -> p h c", h=4)),
                            reads=[PB[accb]], writes=[PB[accb], "oacc"])
                    elif grp != "A" and slot == 11:
                        g = grp
                        pv = psB[accb][:, 0:260].rearrange("p (h c) -> p h c", h=4)
                        if g == 0:
                            S.op("dve", lambda pv=pv: nc.vector.tensor_copy(out=oacc[:, 8:12, :], in_=pv),
                                 reads=[PB[accb]], writes=[PB[accb], "oacc"])
                        else:
                            S.op("dve", lambda pv=pv, g=g: nc.vector.tensor_tensor(
                                out=otmp[:], in0=pv, in1=abt[:, blk, g, :, 0:1].broadcast_to([128, 4, 65]), op=ALU.mult),
                                reads=[PB[accb], "abt"], writes=[PB[accb], "otmp"])
                            S.op("pool", lambda: nc.gpsimd.tensor_tensor(out=oacc[:, 8:12, :], in0=oacc[:, 8:12, :], in1=otmp[:],
                                                                         op=ALU.add), reads=["otmp", "oacc"], writes=["oacc"])

            pend = []
            for bi in range(0, len(batches), 2):
                grp_ = batches[bi:bi + 2]
                for batch in grp_:
                    pend.append((batch, front(batch)))
                while len(pend) > ADEPTH:
                    back(*pend.pop(0))
            while pend:
                back(*pend.pop(0))

        def finish_o(P, s):
            S.op("dve", lambda: nc.vector.tensor_tensor(out=oacc[0:P, 0:8, 64:65], in0=oacc[0:P, 0:8, 64:65],
                                                        in1=esink[0:P, :].unsqueeze(2), op=ALU.add),
                 reads=["oacc", "esink"], writes=["oacc"])
            S.op("dve", lambda: nc.vector.reciprocal(out=rden[0:P, :].unsqueeze(2), in_=oacc[0:P, :, 64:65]),
                 reads=["oacc"], writes=["rden"])
            S.op("dve", lambda: nc.vector.tensor_tensor(
                out=onrm[0:P, :].rearrange("p (h d) -> p h d", h=12), in0=oacc[0:P, :, 0:HD],
                in1=rden[0:P, :].unsqueeze(2).broadcast_to([P, 12, HD]), op=ALU.mult),
                reads=["oacc", "rden"], writes=["onrm"])
            t = next_tb()
            for c in range(6):
                S.op("pe", lambda c=c: nc.tensor.transpose(out=psT[t][:, c * 128:c * 128 + P],
                                                           in_=onrm[0:P, c * 128:(c + 1) * 128], identity=idb[0:P, 0:P]),
                     reads=["onrm", "idb"], writes=[TB[t]], signal=(c == 5))
            copy_op(ew(), oT[:, :, s * 128:s * 128 + P],
                    psT[t][:, 0:768].rearrange("p (k c) -> p k c", k=6)[:, :, 0:P], [TB[t]], [TB[t], "oT"])

        def merge_out(P, nsub, NT):
            for m in range(8):
                r = next_ring()
                S.dma("sp", "ring%d" % r, ring[r][:, 0:2816], s_m[m], reads=["s_m_ga", "s_m_gb", "s_m_upa", "s_m_upb"], writes=["ring%d" % r])
                rv = ring[r][:, 0:2816].rearrange("p (k c) -> p k c", k=22)
                ga_b, gb_b = (2, 3) if m % 2 == 0 else (4, 5)
                specs = [(ga_b, 6, 8, hT, 0), (gb_b, 14, 8, hT, 0), (0, 0, 4, oT, 0), (1, 4, 2, oT, 4)]
                for (b, k0, nk, src, s0) in specs:
                    for kc in range(nk):
                        S.op("pe", lambda b=b, k0=k0, kc=kc, src=src, s0=s0, nk=nk: nc.tensor.matmul(
                            psB[b][:, 0:NT], lhsT=rv[:, k0 + kc, :], rhs=src[:, s0 + kc, 0:NT],
                            start=(kc == 0), stop=(kc == nk - 1)),
                            reads=["ring%d" % r, "oT" if src is oT else "hT"], writes=[PB[b]], signal=(kc == nk - 1))
                ia, ib = (2 * m) % 3, (2 * m + 1) % 3
                sA, sB = sg[ia], sg[ib]
                S.op("act", lambda: nc.scalar.activation(out=sA[:, 0:NT], in_=psB[ga_b][:, 0:NT], func=AF.Sigmoid),
                     reads=[PB[ga_b]], writes=[PB[ga_b], "sg%d" % ia])
                S.op("act", lambda: nc.scalar.activation(out=sB[:, 0:NT], in_=psB[gb_b][:, 0:NT], func=AF.Sigmoid),
                     reads=[PB[gb_b]], writes=[PB[gb_b], "sg%d" % ib])
                S.op("dve", lambda: nc.vector.tensor_tensor(out=sA[:, 0:NT], in0=sA[:, 0:NT], in1=psB[0][:, 0:NT], op=ALU.mult),
                     reads=["sg%d" % ia, PB[0]], writes=["sg%d" % ia, PB[0]])
                S.op("dve", lambda: nc.vector.tensor_tensor(out=sB[:, 0:NT], in0=sB[:, 0:NT], in1=psB[1][:, 0:NT], op=ALU.mult),
                     reads=["sg%d" % ib, PB[1]], writes=["sg%d" % ib, PB[1]])
                S.op("pool", lambda m=m: nc.gpsimd.tensor_tensor(out=big[:, m, 0:NT], in0=sA[:, 0:NT], in1=sB[:, 0:NT], op=ALU.add),
                     reads=["sg%d" % ia, "sg%d" % ib], writes=["big%d" % m])
            rs = []
            for ch in range(2):
                r = next_ring()
                S.dma("sp", "ring%d" % r, ring[r][:], s_o[ch].rearrange("p k c -> p (k c)"), reads=["s_o%d" % ch], writes=["ring%d" % r])
                rs.append(r)
            for s in range(nsub):
                for ch in range(2):
                    r = rs[ch]
                    rv = ring[r][:].rearrange("p (k c) -> p k c", k=8)
                    b = 4 + ch
                    for kc in range(8):
                        S.op("pe", lambda kc=kc, b=b, rv=rv: nc.tensor.matmul(
                            psB[b][0:P, :], lhsT=big[:, kc, s * 128:s * 128 + P], rhs=rv[:, kc, :], start=(kc == 0), stop=(kc == 7)),
                            reads=["big%d" % kc, "ring%d" % r], writes=[PB[b]], signal=(kc == 7))
                    S.op("dve", lambda b=b, ch=ch: nc.vector.tensor_tensor(
                        out=xt[0:P, xmap[s], ch * 512:(ch + 1) * 512], in0=psB[b][0:P, :], in1=xt[0:P, xmap[s], ch * 512:(ch + 1) * 512], op=ALU.add),
                        reads=[PB[b], "xt%d" % xmap[s]], writes=[PB[b], "xt%d" % xmap[s]])

        import os as _os2
        LOOK = int(_os2.environ.get("KLOOK", "3"))
        ADEPTH = int(_os2.environ.get("KADEPTH", "2"))
        MASKMOD = int(_os2.environ.get("KMASKMOD", "1000000"))
        PDEPTH = int(_os2.environ.get("KPDEPTH", "2"))
        copies = []
        if with_sample:
            copies.append((o_as[:, 0:127].rearrange("b t k h d -> b (t k h d)"), c_a[:, 1:128].rearrange("b t k h d -> b (t k h d)")))
            copies.append((o_bs[0][:, 0:127].rearrange("b t k h d -> b (t k h d)"), c_b[0][:, 1:128].rearrange("b t k h d -> b (t k h d)")))
            for bb in range(0, NS, 4):
                copies.append((o_bs[1][bb:bb + 4, 0:511].rearrange("b t k h d -> b (t k h d)"),
                               c_b[1][bb:bb + 4, 1:512].rearrange("b t k h d -> b (t k h d)")))
            for bb in range(NS):
                copies.append((o_bs[2][bb:bb + 1, 0:2047].rearrange("b t k h d -> b (t k h d)"),
                               c_b[2][bb:bb + 1, 1:2048].rearrange("b t k h d -> b (t k h d)")))

        def issue_copies(n):
            for _ in range(n):
                if copies:
                    o, i = copies.pop(0)
                    S.dma("sp", "ccopy", o, i)

        import os
        STAGE = int(os.environ.get("KSTAGE", "9"))
        NTILES = int(os.environ.get("KTILES", "8"))
        tcount = 0
        for seq in range(NSEQ):
            for tl in range(4):
                if STAGE < 1 or tcount >= NTILES:
                    continue
                tcount += 1
                blk0 = tl * 4
                xmap[:] = [(4 * (tcount - 1) + s_) % 5 for s_ in range(4)]
                xfree = (4 * (tcount - 1) + 4) % 5
                if tcount == 1:
                    for s_ in range(4):
                        S.dma("sp", "xi%d" % s_, xt[:, s_, :], x_p[seq, tl * 512 + s_ * 128:tl * 512 + (s_ + 1) * 128, :], writes=["xt%d" % s_])
                ffn(0, 128, 4, 512)
                if STAGE >= 2:
                    norm_T(128, 4, 512)
                    project_prompt(seq, blk0)
                if STAGE >= 3:
                    for s in range(4):
                        issue_copies(1 if s < 3 else 0)
                        attention_block(s, blk0 + s)
                        finish_o(128, s)
                if STAGE >= 4:
                    merge_out(128, 4, 512)
                nxt = None
                if tcount < NTILES and not (seq == NSEQ - 1 and tl == 3):
                    nxt = (seq, tl + 1) if tl < 3 else (seq + 1, 0)

                def sub_done(s_, seq=seq, tl=tl, nxt=nxt):
                    sl_ = xmap[s_]
                    S.dma("pool", "yout%d" % sl_, y_p[seq, tl * 512 + s_ * 128:tl * 512 + (s_ + 1) * 128, :], xt[:, sl_, :],
                          reads=["xt%d" % sl_])
                    if nxt is not None and s_ < 3:
                        S.dma("pool", "xt%d" % sl_, xt[:, sl_, :],
                              x_p[nxt[0], nxt[1] * 512 + (s_ + 1) * 128:nxt[1] * 512 + (s_ + 2) * 128, :], writes=["xt%d" % sl_])

                if nxt is not None:
                    S.dma("pool", "xt%d" % xfree, xt[:, xfree, :], x_p[nxt[0], nxt[1] * 512:nxt[1] * 512 + 128, :], writes=["xt%d" % xfree])
                if STAGE >= 5:
                    ffn(1, 128, 4, 512, sub_done)
                else:
                    for s_ in range(4):
                        sub_done(s_)

        issue_copies(len(copies))
        if not with_sample:
            hs.close()
        if with_sample:
            S.barrier()
            hs.close()
            ss_ = ExitStack()
            es.enter_context(ss_)
            zs = sb("zs", [128, 3072], F32, ss_)
            sq = sb("sq", [128, 256], F32, ss_)
            qn = sb("qn", [128, 256], F32, ss_)
            qb16 = sb("qb16", [128, 256], BF16, ss_)
            zs_box[0] = zs
            KVb = sb("KVb", [128, 17 * 512], F32, ss_)
            qrep = sb("qrep", [128, 512], F32, ss_)
            part = sb("part", [128, 8 * 65], F32, ss_)
            ssc2 = sb("ssc2", [128, 17 * 8], F32, ss_)
            pexp = sb("pexp", [128, 17 * 8], F32, ss_)
            sal = sb("sal", [128, 4, 17 * 8], F32, ss_)
            rept = sb("rept", [128, 128], F32, ss_)
            sel = sb("sel", [128, 16], F32, ss_)
            S.dma("sp", "rept", rept[0:16, :], c_rep, writes=["rept"])
            S.dma("sp", "sel", sel[:], c_sel, writes=["sel"])
            S.dma("sp", "sal", sal[:], c_sal, writes=["sal"])
            xmap[:] = [0, 1, 2, 3]
            S.dma("sp", "xi0", xt[0:NS, 0, :], x_s, writes=["xt0"])
            ffn(0, NS, 1, NS)
            norm_T(NS, 1, NS)
            project(NS, 1, NS, None, 0)
            S.dma("sp", "nrow", o_as[:, 127, 0, :, :].rearrange("b h d -> b (h d)"), zs[0:NS, 512:640], reads=["zs"])
            S.dma("sp", "nrow", o_as[:, 127, 1, :, :].rearrange("b h d -> b (h d)"), zs[0:NS, 640:768], reads=["zs"])
            for g in range(3):
                W = [128, 512, 2048][g]
                S.dma("sp", "nrow", o_bs[g][:, W - 1, 0, :, :].rearrange("b h d -> b (h d)"),
                      zs[0:NS, 1536 + g * 256:1536 + (g + 1) * 256], reads=["zs"])
                S.dma("sp", "nrow", o_bs[g][:, W - 1, 1, :, :].rearrange("b h d -> b (h d)"),
                      zs[0:NS, 2304 + g * 256:2304 + (g + 1) * 256], reads=["zs"])

            def sgroup(G, cache, dil, Hkv, Hq, qc0, kc0, vc0):
                RL = 2 * Hkv * HD
                KW = Hkv * HD
                kvv = KVb[:, 0:17 * RL].rearrange("p (j r) -> p j r", r=RL)
                S.op("pool", lambda: nc.gpsimd.memset(kvv[:, 16, :], 0.0), writes=["KVb"])
                src = cache.rearrange("b (jc jj r) k h d -> (b jc) jj r (k h d)", jc=8, jj=16, r=dil)[:, :, 0, :]
                S.dma("sp", "KVb", kvv[:, 0:16, :], src, writes=["KVb"], disjoint=True)
                k7 = KVb[7:128:8, :]
                S.dma("sp", "KVb", k7[:, 16 * RL:16 * RL + KW], zs[0:NS, kc0:kc0 + KW], reads=["zs"], writes=["KVb"], disjoint=True)
                S.dma("sp", "KVb", k7[:, 16 * RL + KW:17 * RL], zs[0:NS, vc0:vc0 + KW], reads=["zs"], writes=["KVb"], disjoint=True)
                W = Hq * HD
                S.op("pe", lambda: nc.tensor.matmul(psB[0][:, 0:W], lhsT=rept[0:NS, :], rhs=zs[0:NS, qc0:qc0 + W], start=True, stop=True),
                     reads=["rept", "zs"], writes=[PB[0]])
                S.op("act", lambda: nc.scalar.copy(out=qrep[:, 0:W], in_=psB[0][:, 0:W]), reads=[PB[0]], writes=[PB[0], "qrep"])
                sc3 = ssc2[:, 0:17 * Hq].rearrange("p (j h) -> p j h", h=Hq)
                pe3 = pexp[:, 0:17 * Hq].rearrange("p (j h) -> p j h", h=Hq)
                part3 = part[:, 0:Hq * 65].rearrange("p (h c) -> p h c", c=65)
                if G != 0:
                    K4 = kvv[:, :, 0:KW]
                    S.op("dve", lambda: nc.vector.tensor_tensor(out=K4, in0=K4, in1=qrep[:, 0:KW].unsqueeze(1).broadcast_to([128, 17, KW]),
                                                                op=ALU.mult), reads=["KVb", "qrep"], writes=["KVb"])
                    S.op("dve", lambda: nc.vector.tensor_reduce(out=sc3, in_=K4.rearrange("p j (h d) -> p j h d", h=Hkv), axis=AX.X, op=ALU.add),
                         reads=["KVb"], writes=["ssc2"])
                else:
                    prodA = KVb[:, 17 * RL:17 * RL + 17 * 256].rearrange("p (j h d) -> p j h d", h=4, d=HD)
                    for kvh in range(2):
                        S.op("dve", lambda kvh=kvh: nc.vector.tensor_tensor(
                            out=prodA, in0=kvv[:, :, kvh * HD:(kvh + 1) * HD].unsqueeze(2).broadcast_to([128, 17, 4, HD]),
                            in1=qrep[:, kvh * 256:(kvh + 1) * 256].rearrange("p (h d) -> p h d", h=4).unsqueeze(1).broadcast_to([128, 17, 4, HD]),
                            op=ALU.mult), reads=["KVb", "qrep"], writes=["KVb"])
                        S.op("dve", lambda kvh=kvh: nc.vector.tensor_reduce(out=sc3[:, :, kvh * 4:(kvh + 1) * 4], in_=prodA, axis=AX.X, op=ALU.add),
                             reads=["KVb"], writes=["ssc2"])
                S.op("dve", lambda: nc.vector.tensor_tensor(out=ssc2[:, 0:17 * Hq], in0=ssc2[:, 0:17 * Hq], in1=sal[:, G, 0:17 * Hq], op=ALU.add),
                     reads=["ssc2", "sal"], writes=["ssc2"])
                S.op("act", lambda: nc.scalar.activation(out=pexp[:, 0:17 * Hq], in_=ssc2[:, 0:17 * Hq], func=AF.Exp),
                     reads=["ssc2"], writes=["pexp"])
                S.op("dve", lambda: nc.vector.tensor_reduce(out=part3[:, :, 64], in_=pe3.rearrange("p j h -> p h j"), axis=AX.X, op=ALU.add),
                     reads=["pexp"], writes=["part"])
                if G != 0:
                    V4 = kvv[:, :, KW:2 * KW].rearrange("p j (h d) -> p j h d", h=Hkv)
                    S.op("dve", lambda: nc.vector.tensor_tensor(out=V4, in0=V4, in1=pe3.unsqueeze(3).broadcast_to([128, 17, Hq, HD]), op=ALU.mult),
                         reads=["KVb", "pexp"], writes=["KVb"])
                    S.op("dve", lambda: nc.vector.tensor_reduce(out=part3[:, :, 0:HD], in_=V4.rearrange("p j h d -> p h d j"), axis=AX.X, op=ALU.add),
                         reads=["KVb"], writes=["part"])
                else:
                    for kvh in range(2):
                        S.op("dve", lambda kvh=kvh: nc.vector.tensor_tensor(
                            out=prodA, in0=kvv[:, :, KW + kvh * HD:KW + (kvh + 1) * HD].unsqueeze(2).broadcast_to([128, 17, 4, HD]),
                            in1=pe3[:, :, kvh * 4:(kvh + 1) * 4].unsqueeze(3).broadcast_to([128, 17, 4, HD]), op=ALU.mult),
                            reads=["KVb", "pexp"], writes=["KVb"])
                        S.op("dve", lambda kvh=kvh: nc.vector.tensor_reduce(out=part3[:, kvh * 4:(kvh + 1) * 4, 0:HD],
                                                                            in_=prodA.rearrange("p j h d -> p h d j"), axis=AX.X, op=ALU.add),
                             reads=["KVb"], writes=["part"])
                if G == 0:
                    for hf in range(2):
                        S.op("pe", lambda hf=hf: nc.tensor.matmul(psB[4 + hf][0:NS, 0:260], lhsT=sel[:, 0:NS], rhs=part[:, hf * 260:(hf + 1) * 260],
                                                                  start=True, stop=True), reads=["sel", "part"], writes=[PB[4 + hf]])
                        S.op("dve", lambda hf=hf: nc.vector.tensor_copy(out=oacc[0:NS, hf * 4:(hf + 1) * 4, :],
                                                                        in_=psB[4 + hf][0:NS, 0:260].rearrange("p (h c) -> p h c", c=65)),
                             reads=[PB[4 + hf]], writes=[PB[4 + hf], "oacc"])
                else:
                    S.op("pe", lambda: nc.tensor.matmul(psB[3][0:NS, 0:260], lhsT=sel[:, 0:NS], rhs=part[:, 0:260],
                                                        start=(G == 1), stop=(G == 3)), reads=["sel", "part"], writes=[PB[3]])
                    if G == 3:
                        S.op("dve", lambda: nc.vector.tensor_copy(out=oacc[0:NS, 8:12, :],
                                                                  in_=psB[3][0:NS, 0:260].rearrange("p (h c) -> p h c", c=65)),
                             reads=[PB[3]], writes=[PB[3], "oacc"])

            sgroup(0, c_a, 1, 2, 8, 0, 512, 640)
            for g in range(3):
                sgroup(1 + g, c_b[g], DILS[g], 4, 4, 768 + g * 256, 1536 + g * 256, 2304 + g * 256)
            finish_o(NS, 0)
            merge_out(NS, 1, NS)
            ffn(1, NS, 1, NS)
            S.dma("pool", "yout0", y_s, xt[0:NS, 0, :], reads=["xt0"])

        S.finish()
        print("ops", S.nops, "waits", S.nwaits, "cnt", S.cnt, "dma sems", len(S.dsem))
    sal_np = np.zeros((128, 4, 17 * 8), np.float64)
    for p in range(128):
        jc = p % 8
        for G in range(4):
            Hq = 8 if G == 0 else 4
            dil = 1 if G == 0 else DILS[G - 1]
            for jj in range(17):
                for h in range(Hq):
                    sl = SLOPES[h] if G == 0 else SLOPES[8 + 4 * (G - 1) + h]
                    if jj < 16:
                        v = -sl * dil * (128 - (jc * 16 + jj))
                    else:
                        v = 0.0 if jc == 7 else -30000.0
                    sal_np[p, G, jj * Hq + h] = v
    rep_np = np.zeros((16, 128), np.float32)
    for p in range(128):
        rep_np[p // 8, p] = 1.0
    consts = {"c_ident": ident_np, "c_dt": dt_np, "c_ab": ab_np.reshape(128, -1), "c_sal": sal_np.astype(np.float32), "c_rep": rep_np,
              "c_sel": np.ascontiguousarray(rep_np.T)}
    return nc, consts


_CACHE = {}


def kernel(**inputs):
    import os
    WS = os.environ.get("KNOSAMPLE", "0") != "1"
    if "prog" not in _CACHE:
        _CACHE["prog"] = build_program(WS)
    nc, consts = _CACHE["prog"]
    f = lambda k: np.ascontiguousarray(np.asarray(inputs[k], dtype=np.float32))
    xp = f("x_prompt"); xs = f("x_sample").reshape(128, D)
    ca = f("cache_a_kv")[0]; cb1 = f("cache_b1_kv")[0]; cb2 = f("cache_b2_kv")[0]; cb3 = f("cache_b3_kv")[0]
    shared = {
        "norm_ffn1": f("norm_ffn1")[0], "norm_mix": f("norm_mix")[0], "norm_ffn2": f("norm_ffn2")[0],
        "w1_gate": f("w1_gate")[0], "w1_up": f("w1_up")[0], "w1_down": f("w1_down")[0],
        "w2_gate": f("w2_gate")[0], "w2_up": f("w2_up")[0], "w2_down": f("w2_down")[0],
        "w_in": f("w_in")[0], "q_norm_a": f("q_norm_a")[0], "k_norm_a": f("k_norm_a")[0],
        "q_norm_b": f("q_norm_b")[0], "k_norm_b": f("k_norm_b")[0], "sinks_a": f("sinks_a")[0].reshape(8),
        "w_up_a": f("w_up_a")[0], "w_up_b": f("w_up_b")[0], "w_o": f("w_o")[0],
    }
    shared.update(consts)
    in_maps = []
    for c in range(NCORES):
        m = dict(shared)
        m["x_prompt"] = xp[c * NSEQ:(c + 1) * NSEQ]
        m["x_sample"] = xs[c * NS:(c + 1) * NS]
        if WS:
            m["cache_a"] = ca[c * NS:(c + 1) * NS]
            m["cache_b1"] = cb1[c * NS:(c + 1) * NS]
            m["cache_b2"] = cb2[c * NS:(c + 1) * NS]
            m["cache_b3"] = cb3[c * NS:(c + 1) * NS]
        in_maps.append(m)
    res = run_bass_kernel_spmd(nc, in_maps, core_ids=list(range(NCORES)))
    R = res.results
    cat = lambda k: np.concatenate([np.asarray(r[k]) for r in R], axis=0)
    y_prompt = cat("y_prompt")
    y_sample = cat("y_sample").reshape(128, 1, D)
    outs = [y_prompt, y_sample]
    for k in ["a_p", "b1_p", "b2_p", "b3_p"] + (["a_s", "b1_s", "b2_s", "b3_s"] if WS else []):
        outs.append(cat(k)[None])
    return tuple(o.astype(np.float32) for o in outs)
```
